# Optimizing a Trainium2 kernel written in Bass

```python
import math
import jax, jax.numpy as jnp
from jax import lax
import numpy as np

D_MODEL = 1024
BATCH = 8
SEQ = 2048
DEPTH = 1

N_ATTN_HEADS = 4
ATTN_HEAD_DIM = 64
ATTN_V_DIM = 2 * ATTN_HEAD_DIM
ATTN_WIDTH = N_ATTN_HEADS * ATTN_V_DIM
LRU_WIDTH = D_MODEL - ATTN_WIDTH
LRU_BLOCKS = 8
LRU_BLOCK_DIM = LRU_WIDTH // LRU_BLOCKS
LRU_CONV_WIDTH = 4
LRU_CONV_LEFT = 2
LRU_C = 8.0
N_DIRECTIONS = 2
IN_WIDTH = 3 * ATTN_WIDTH + 2 * LRU_WIDTH
D_FF = 2816
FFN_CONV_WIDTH = 3
FFN_CONV_LEFT = (FFN_CONV_WIDTH - 1) // 2
Q_BLOCK = 128
NORM_EPS = 1e-6

kernel_name = "hymba_diffattn_rglru_convglu_encoder"


def rms_norm(x, g):
    xf = x.astype(jnp.float32)
    y = xf * lax.rsqrt(jnp.mean(xf * xf, axis=-1, keepdims=True) + NORM_EPS)
    return (y * g.astype(jnp.float32)).astype(x.dtype)


def depthwise_conv(x, w, b, left):
    width = w.shape[0]
    s = x.shape[1]
    xp = jnp.pad(x, ((0, 0), (left, width - 1 - left), (0, 0)))
    out = b
    for tap in range(width):
        out = out + xp[:, tap:tap + s] * w[tap]
    return out


def diff_attention(q, k, v, lam, lambda_init, subln_g):
    b, s, _ = q.shape
    qh = q.reshape(b, s, N_ATTN_HEADS, 2, ATTN_HEAD_DIM)
    kh = k.reshape(b, s, N_ATTN_HEADS, 2, ATTN_HEAD_DIM)
    vh = v.reshape(b, s, N_ATTN_HEADS, ATTN_V_DIM).astype(jnp.float32)
    scale = ATTN_HEAD_DIM ** -0.5
    slopes = jnp.exp2(-8.0 * jnp.arange(1, N_ATTN_HEADS + 1, dtype=jnp.float32) / N_ATTN_HEADS)
    key_pos = jnp.arange(s)

    def block(start):
        qb = lax.dynamic_slice_in_dim(qh, start, Q_BLOCK, axis=1)
        sc = jnp.einsum('bqhcd,bkhcd->bhcqk', qb, kh).astype(jnp.float32) * scale
        dist = jnp.abs(start + jnp.arange(Q_BLOCK)[:, None] - key_pos[None, :]).astype(jnp.float32)
        sc = sc - slopes[:, None, None, None] * dist
        p = jax.nn.softmax(sc, axis=-1)
        w = p[:, :, 0] - lam * p[:, :, 1]
        return jnp.einsum('bhqk,bkhe->bqhe', w, vh)

    starts = jnp.arange(s // Q_BLOCK) * Q_BLOCK
    o = lax.map(block, starts)
    o = jnp.moveaxis(o, 0, 1).reshape(b, s, N_ATTN_HEADS, ATTN_V_DIM)
    o = rms_norm(o, subln_g) * (1.0 - lambda_init)
    return o.reshape(b, s, ATTN_WIDTH).astype(q.dtype)


def _linear_combine(c1, c2):
    a1, b1 = c1
    a2, b2 = c2
    return a1 * a2, a2 * b1 + b2


def rg_lru(xc, w_a, b_a, w_x, b_x, lru_lambda, reverse):
    b, s, c = xc.shape
    xb = xc.reshape(b, s, LRU_BLOCKS, LRU_BLOCK_DIM)
    r = jax.nn.sigmoid(jnp.einsum('bsni,nij->bsnj', xb, w_a.astype(jnp.float32)).reshape(b, s, c)
                       + b_a.astype(jnp.float32))
    i = jax.nn.sigmoid(jnp.einsum('bsni,nij->bsnj', xb, w_x.astype(jnp.float32)).reshape(b, s, c)
                       + b_x.astype(jnp.float32))
    log_a = -LRU_C * r * jax.nn.softplus(-lru_lambda.astype(jnp.float32))
    a = jnp.exp(log_a)
    u = jnp.sqrt(-jnp.expm1(2.0 * log_a)) * (i * xc)
    _, h = lax.associative_scan(_linear_combine, (a, u), axis=1, reverse=reverse)
    return h


def recurrent_group(xr, gr, conv_w, conv_b, w_a, b_a, w_x, b_x, lru_lambda):
    xc = depthwise_conv(xr, conv_w, conv_b, LRU_CONV_LEFT).astype(jnp.float32)
    y = (rg_lru(xc, w_a[0], b_a[0], w_x[0], b_x[0], lru_lambda[0], reverse=False)
         + rg_lru(xc, w_a[1], b_a[1], w_x[1], b_x[1], lru_lambda[1], reverse=True))
    return (jax.nn.gelu(gr.astype(jnp.float32)) * y).astype(xr.dtype)


def conv_glu_ffn(h, w_up, conv_w, conv_b, w_down):
    u = depthwise_conv(h @ w_up, conv_w, conv_b, FFN_CONV_LEFT)
    gate, val = jnp.split(u, 2, axis=-1)
    return (jax.nn.gelu(gate) * val) @ w_down


def setup_inputs(seed: int = 0) -> dict:
    key = jax.random.key(seed)
    ks = jax.random.split(key, 24)
    f32 = jnp.float32

    def nrm(k, shape, scale):
        return jax.random.normal(k, shape, f32) * scale

    a0 = jax.random.uniform(ks[12], (DEPTH, N_DIRECTIONS, LRU_WIDTH), f32, 0.9, 0.999)
    s0 = a0 ** (1.0 / LRU_C)
    lru_lambda = jnp.log(s0) - jnp.log1p(-s0)
    return {
        "x": jax.random.normal(ks[0], (BATCH, SEQ, D_MODEL), f32),
        "attn_norm_g": 1.0 + nrm(ks[1], (DEPTH, D_MODEL), 0.02),
        "w_in": nrm(ks[2], (DEPTH, D_MODEL, IN_WIDTH), D_MODEL ** -0.5),
        "lambda_q1": nrm(ks[3], (DEPTH, ATTN_HEAD_DIM), 0.1),
        "lambda_k1": nrm(ks[4], (DEPTH, ATTN_HEAD_DIM), 0.1),
        "lambda_q2": nrm(ks[5], (DEPTH, ATTN_HEAD_DIM), 0.1),
        "lambda_k2": nrm(ks[6], (DEPTH, ATTN_HEAD_DIM), 0.1),
        "subln_g": 1.0 + nrm(ks[7], (DEPTH, ATTN_V_DIM), 0.02),
        "lru_conv_w": nrm(ks[8], (DEPTH, LRU_CONV_WIDTH, LRU_WIDTH), LRU_CONV_WIDTH ** -0.5),
        "lru_conv_b": nrm(ks[9], (DEPTH, LRU_WIDTH), 0.02),
        "lru_w_a": nrm(ks[10], (DEPTH, N_DIRECTIONS, LRU_BLOCKS, LRU_BLOCK_DIM, LRU_BLOCK_DIM), LRU_BLOCK_DIM ** -0.5),
        "lru_b_a": nrm(ks[11], (DEPTH, N_DIRECTIONS, LRU_WIDTH), 0.1),
        "lru_w_x": nrm(ks[13], (DEPTH, N_DIRECTIONS, LRU_BLOCKS, LRU_BLOCK_DIM, LRU_BLOCK_DIM), LRU_BLOCK_DIM ** -0.5),
        "lru_b_x": nrm(ks[14], (DEPTH, N_DIRECTIONS, LRU_WIDTH), 0.1),
        "lru_lambda": lru_lambda,
        "w_out": nrm(ks[15], (DEPTH, D_MODEL, D_MODEL), D_MODEL ** -0.5),
        "ffn_norm_g": 1.0 + nrm(ks[16], (DEPTH, D_MODEL), 0.02),
        "w_up": nrm(ks[17], (DEPTH, D_MODEL, 2 * D_FF), D_MODEL ** -0.5),
        "ffn_conv_w": nrm(ks[18], (DEPTH, FFN_CONV_WIDTH, 2 * D_FF), FFN_CONV_WIDTH ** -0.5),
        "ffn_conv_b": nrm(ks[19], (DEPTH, 2 * D_FF), 0.02),
        "w_down": nrm(ks[20], (DEPTH, D_FF, D_MODEL), D_FF ** -0.5),
        "final_norm_g": 1.0 + nrm(ks[21], (D_MODEL,), 0.02),
    }


def reference(x, attn_norm_g, w_in, lambda_q1, lambda_k1, lambda_q2, lambda_k2, subln_g,
              lru_conv_w, lru_conv_b, lru_w_a, lru_b_a, lru_w_x, lru_b_x, lru_lambda,
              w_out, ffn_norm_g, w_up, ffn_conv_w, ffn_conv_b, w_down, final_norm_g):
    for l in range(DEPTH):
        h = rms_norm(x, attn_norm_g[l])
        proj = h @ w_in[l]
        q, k, v, xr, gr = jnp.split(
            proj, [ATTN_WIDTH, 2 * ATTN_WIDTH, 3 * ATTN_WIDTH, 3 * ATTN_WIDTH + LRU_WIDTH], axis=-1)
        lambda_init = 0.8 - 0.6 * math.exp(-0.3 * l)
        lam = (jnp.exp(jnp.sum(lambda_q1[l].astype(jnp.float32) * lambda_k1[l].astype(jnp.float32)))
               - jnp.exp(jnp.sum(lambda_q2[l].astype(jnp.float32) * lambda_k2[l].astype(jnp.float32)))
               + lambda_init)
        attn_out = diff_attention(q, k, v, lam, lambda_init, subln_g[l])
        lru_out = recurrent_group(xr, gr, lru_conv_w[l], lru_conv_b[l], lru_w_a[l], lru_b_a[l],
                                  lru_w_x[l], lru_b_x[l], lru_lambda[l])
        x = x + jnp.concatenate([attn_out, lru_out], axis=-1) @ w_out[l]
        x = x + conv_glu_ffn(rms_norm(x, ffn_norm_g[l]), w_up[l], ffn_conv_w[l], ffn_conv_b[l], w_down[l])
    return rms_norm(x, final_norm_g)
```

```python
import os
from contextlib import ExitStack
import numpy as np
import concourse.bass as bass
import concourse.mybir as mybir
from concourse.bass_utils import run_bass_kernel_spmd

F32 = mybir.dt.float32
BF16 = mybir.dt.bfloat16
AF = mybir.ActivationFunctionType
ALU = mybir.AluOpType

ENGS = ("pe", "act", "dve", "pool", "sp")
S_TOK = 2048
D = 1024
NT = 16
DFF = 2816
NJ = 22
EPS = 1e-6
LAMBDA_INIT = 0.8 - 0.6 * 1.0


class Op:
    __slots__ = ("eng", "fn", "deps", "signal", "count", "is_dma", "dma_sem", "dma_key", "dma_target", "name")

    def __init__(self, eng, fn, name=""):
        self.eng = eng
        self.fn = fn
        self.deps = []
        self.signal = False
        self.count = None
        self.is_dma = False
        self.dma_sem = None
        self.dma_key = None
        self.dma_target = 0
        self.name = name


def _buf(r):
    return r[0] if isinstance(r, tuple) else r


class Sched:
    def __init__(self, nc, es):
        self.nc = nc
        self.es = es
        self.per_eng = {e: [] for e in ENGS}
        self.last_writer = {}
        self.readers = {}
        self.eng_sem = {e: es.enter_context(nc.semaphore("prog_" + e)) for e in ENGS if e != "sp"}
        self.dma_sems = {}
        self.dma_counts = {}
        self.buf_deps = {}
        self.nops = 0

    def _add_dep(self, o, d):
        if d is o:
            return
        if d.is_dma:
            o.deps.append((d, 16 * self.dma_counts[d.dma_key]))
            return
        if d.eng == o.eng and o.eng == "pe":
            return
        d.signal = True
        o.deps.append((d, None))

    NON_ARENA = ("ps", "psu", "lam", "zr", "ecol", "pcol", "cA", "c2A", "wupb", "wdnb")

    def _check(self, rs):
        for r in rs:
            b = _buf(r)
            if isinstance(b, tuple) or b in self.NON_ARENA or b in self.buf_deps:
                continue
            raise RuntimeError(f"resource {r!r} is not an arena buffer")

    def op(self, eng, fn, reads=(), writes=(), name=""):
        self._check(reads)
        self._check(writes)
        o = Op(eng, fn, name)
        deps = {}
        for r in reads:
            w = self.last_writer.get(r)
            if w is not None:
                deps[id(w)] = w
        for r in writes:
            w = self.last_writer.get(r)
            if w is not None:
                deps[id(w)] = w
            for rd in self.readers.get(r, {}).values():
                deps[id(rd)] = rd
        for r in list(reads) + list(writes):
            for d in self.buf_deps.get(_buf(r), ()):
                deps[id(d)] = d
        for d in deps.values():
            self._add_dep(o, d)
        for r in reads:
            rd = self.readers.setdefault(r, {})
            key = ("dma", id(o)) if False else eng
            rd[key] = o
        for r in writes:
            self.last_writer[r] = o
            self.readers[r] = {}
        self.per_eng[eng].append(o)
        self.nops += 1
        return o

    def dma(self, queue, out, in_, semkey, reads=(), writes=(), name="", **kw):
        if semkey not in self.dma_sems:
            self.dma_sems[semkey] = self.es.enter_context(self.nc.semaphore("dma_" + semkey))
            self.dma_counts[semkey] = 0

        def fn(e):
            return e.dma_start(out=out, in_=in_, **kw)

        self._check(reads)
        self._check(writes)
        o = Op(queue, fn, name)
        o.is_dma = True
        o.dma_key = semkey
        o.dma_sem = self.dma_sems[semkey]
        deps = {}
        for r in reads:
            w = self.last_writer.get(r)
            if w is not None:
                deps[id(w)] = w
        for r in writes:
            w = self.last_writer.get(r)
            rds = self.readers.get(r, {})
            if w is not None and not (w.is_dma and rds):
                deps[id(w)] = w
            for rd in rds.values():
                deps[id(rd)] = rd
        for r in list(reads) + list(writes):
            for d in self.buf_deps.get(_buf(r), ()):
                deps[id(d)] = d
        for d in deps.values():
            self._add_dep(o, d)
        self.dma_counts[semkey] += 1
        o.dma_target = 16 * self.dma_counts[semkey]
        for r in reads:
            self.readers.setdefault(r, {})[("dma", semkey)] = o
        for r in writes:
            self.last_writer[r] = o
            self.readers[r] = {}
        self.per_eng[queue].append(o)
        return o

    def ops_touching(self, bufname):
        out = {}
        for r, w in self.last_writer.items():
            if _buf(r) == bufname:
                out[id(w)] = w
        for r, rd in self.readers.items():
            if _buf(r) == bufname:
                for o in rd.values():
                    out[id(o)] = o
        return list(out.values())

    def finalize(self):
        for e in ENGS:
            c = 0
            for o in self.per_eng[e]:
                if o.is_dma:
                    continue
                if o.signal:
                    c += 1
                    o.count = c

    def emit_engine(self, ename, e):
        known = {}
        for o in self.per_eng[ename]:
            waits = {}
            for d, ov in o.deps:
                if d.is_dma:
                    key = ("dma", d.dma_key)
                    sem, val = d.dma_sem, ov
                else:
                    key = ("eng", d.eng)
                    sem, val = self.eng_sem[d.eng], d.count
                if known.get(key, 0) >= val:
                    continue
                if key not in waits or waits[key][1] < val:
                    waits[key] = (sem, val)
            for key, (sem, val) in waits.items():
                e.wait_ge(sem, val)
                known[key] = val
            ins = o.fn(e)
            if o.is_dma:
                ins.then_inc(o.dma_sem, 16)
            elif o.signal:
                ins.then_inc(self.eng_sem[ename], 1)

    def emit(self, final_waits=()):
        self.finalize()
        nc = self.nc
        with nc.Block() as block:
            @block.tensor
            def _(e):
                self.emit_engine("pe", e)

            @block.scalar
            def _(e):
                self.emit_engine("act", e)

            @block.vector
            def _(e):
                self.emit_engine("dve", e)

            @block.gpsimd
            def _(e):
                self.emit_engine("pool", e)

            @block.sync
            def _(e):
                self.emit_engine("sp", e)
                for k in final_waits:
                    e.wait_ge(self.dma_sems[k], 16 * self.dma_counts[k])


class Arena:
    def __init__(self, S, tensor, size):
        self.S = S
        self.t = tensor
        self.size = size
        self.live = {}
        self.retired = []

    def alloc(self, name, n, dt=F32, top=False):
        n = (n + 7) // 8 * 8
        spans = sorted(self.live.values())
        gaps = []
        pos = 0
        for (o, m) in spans:
            if o - pos >= n:
                gaps.append((pos, o))
            pos = max(pos, o + m)
        if self.size - pos >= n:
            gaps.append((pos, self.size))
        if not gaps:
            raise RuntimeError(f"arena full allocating {name} ({n}); live={self.live}")
        if top:
            off = gaps[-1][1] - n
        else:
            off = gaps[0][0]
        self.live[name] = (off, n)
        deps = []
        for (o, m, ops) in self.retired:
            if o < off + n and off < o + m:
                deps.extend(ops)
        self.S.buf_deps[name] = deps
        v = self.t[:, off:off + n]
        return v if dt == F32 else v.bitcast(dt)

    def free(self, name):
        off, n = self.live.pop(name)
        self.retired.append((off, n, self.S.ops_touching(name)))


def build(debug_taps=()):
    nc = bass.Bass("TRN2", target_bir_lowering=False)
    dram_in = lambda n, s: nc.dram_tensor(n, list(s), F32, kind="ExternalInput").ap()
    x_d = dram_in("x", [S_TOK, D])
    w_in_d = dram_in("w_in", [D, 2560])
    w_out_d = dram_in("w_out", [D, D])
    w_up_d = dram_in("w_up", [D, 2 * DFF])
    w_dn_d = dram_in("w_down", [DFF, D])
    wg_d = dram_in("wg", [16, 128, 128])
    cols_d = dram_in("cols", [128, 220])
    rows_d = dram_in("rows", [1, 384])
    gvec_d = dram_in("gvec", [3, D])
    out_d = nc.dram_tensor("out", [S_TOK, D], F32, kind="ExternalOutput").ap()
    taps = {}

    es = ExitStack()
    with es:
        S = Sched(nc, es)
        ARENA_N = 53200
        arena_t = es.enter_context(nc.sbuf_tensor("arena", [128, ARENA_N], F32))
        A = Arena(S, arena_t, ARENA_N)
        P = es.enter_context(nc.psum_tensor("P", [128, 4096], F32))

        def bank(b):
            return P[:, b * 512:(b + 1) * 512]

        def bankbf(b):
            return P[:, b * 512:(b + 1) * 512].bitcast(BF16)

        def psr(b):
            return ("ps", b)

        rot = {"i": 0}

        def next_bank(lo=0, hi=8):
            n = hi - lo
            b = lo + rot["i"] % n
            rot["i"] += 1
            return b

        cols = A.alloc("cols", 224)
        rowsb = A.alloc("rowsb", 384)
        gb = A.alloc("gb", 1024)
        ident = A.alloc("ident", 64, BF16)
        zeros_bf = A.alloc("zeros_bf", 64, BF16)
        identf = A.alloc("identf", 128)
        st = A.alloc("stats", 256)
        ss1 = st[:, 0:16]
        rstd1 = st[:, 16:32]
        lam_s = st[:, 32:40]
        cA = st[:, 40:48]
        c2A = st[:, 48:56]
        ecol = st[:, 56:64]
        pcol = st[:, 64:72]
        zr = st[:, 72:88]
        ss3 = st[:, 88:104]
        rstd3 = st[:, 104:120]
        gsub = A.alloc("gsub", 128)
        junk = A.alloc("junk", 512, BF16)

        S.dma("sp", cols[:, 0:220], cols_d, "c_cols", writes=["cols"])
        S.dma("sp", rowsb, rows_d[0:1, :].to_broadcast([128, 384]), "c_rows", writes=["rowsb"])
        S.dma("sp", gb, gvec_d[0:1, :].to_broadcast([128, D]), "c_gb", writes=["gb"])

        S.op("pool", lambda e: e.iota(identf, pattern=[[1, 128]], base=0, channel_multiplier=-1,
                                      allow_small_or_imprecise_dtypes=True), writes=["identf"])
        S.op("dve", lambda e: e.tensor_single_scalar(out=ident, in_=identf, scalar=0.0, op=ALU.is_equal),
             reads=["identf"], writes=["ident"])

        S.op("pool", lambda e: e.memset(zeros_bf, 0.0), writes=["zeros_bf"])
        S.op("dve", lambda e: e.scalar_tensor_tensor(out=junk[:, 0:64], in0=rowsb[:, 0:64], scalar=1.0,
                                                     in1=rowsb[:, 64:128], op0=ALU.mult, op1=ALU.mult,
                                                     accum_out=lam_s[:, 0:1]),
             reads=["rowsb"], writes=["junk", ("lam", 0)])
        S.op("dve", lambda e: e.scalar_tensor_tensor(out=junk[:, 64:128], in0=rowsb[:, 128:192], scalar=1.0,
                                                     in1=rowsb[:, 192:256], op0=ALU.mult, op1=ALU.mult,
                                                     accum_out=lam_s[:, 1:2]),
             reads=["rowsb"], writes=["junk", ("lam", 1)])
        S.op("act", lambda e: e.activation(out=lam_s[:, 2:4], in_=lam_s[:, 0:2], func=AF.Exp),
             reads=[("lam", 0), ("lam", 1)], writes=[("lam", 2)])
        S.op("dve", lambda e: e.tensor_tensor(out=lam_s[:, 4:5], in0=lam_s[:, 3:4], in1=lam_s[:, 2:3],
                                              op=ALU.subtract), reads=[("lam", 2)], writes=[("lam", 4)])
        S.op("dve", lambda e: e.tensor_scalar(out=lam_s[:, 4:5], in0=lam_s[:, 4:5], scalar1=-LAMBDA_INIT,
                                              scalar2=None, op0=ALU.add), reads=[("lam", 4)], writes=[("lam", 4)])
        neglam = lam_s[:, 4:5]
        S.op("dve", lambda e: e.tensor_scalar(out=gsub, in0=rowsb[:, 256:384], scalar1=1.0 - LAMBDA_INIT,
                                              scalar2=None, op0=ALU.mult), reads=["rowsb"], writes=["gsub"])
        S.op("act", lambda e: e.activation(out=ecol, in_=cols[:, 36:44], func=AF.Exp, scale=-1.0),
             reads=["cols"], writes=["ecol"])
        S.op("dve", lambda e: e.tensor_scalar(out=pcol, in0=ecol, scalar1=1.0 / 7.0, scalar2=None, op0=ALU.mult),
             reads=["ecol"], writes=["pcol"])
        for cst in (-1.0 / 6, 1.0 / 5, -1.0 / 4, 1.0 / 3, -1.0 / 2, 1.0):
            S.op("dve", lambda e, cst=cst: e.scalar_tensor_tensor(out=pcol, in0=pcol, scalar=float(cst), in1=ecol,
                                                                  op0=ALU.add, op1=ALU.mult),
                 reads=["pcol", "ecol"], writes=["pcol"])
        S.op("dve", lambda e: e.tensor_scalar(out=cA, in0=pcol, scalar1=-8.0, scalar2=None, op0=ALU.mult),
             reads=["pcol"], writes=["cA"])
        S.op("dve", lambda e: e.tensor_scalar(out=c2A, in0=pcol, scalar1=-16.0, scalar2=None, op0=ALU.mult),
             reads=["pcol"], writes=["c2A"])

        hT = A.alloc("hT", 8192, BF16).rearrange("p (k t) -> p k t", k=8)
        mixA = A.alloc("mixA", 4096, BF16).rearrange("p (k t) -> p k t", k=4)
        win_qkv = A.alloc("win_qkv", 6144, BF16).rearrange("p (k c) -> p k c", k=8)
        win_lru = A.alloc("win_lru", 4096, BF16).rearrange("p (k c) -> p k c", k=8)
        wg = A.alloc("wg", 1024, BF16).rearrange("p (n c) -> p n c", n=16)

        w_in_v = w_in_d.rearrange("(k p) c -> p k c", p=128)
        for kt in range(8):
            S.dma("pool", win_qkv[:, kt, :], w_in_d[kt * 128:(kt + 1) * 128, 0:1536], "w_qkv",
                  writes=[("win_qkv", kt)])

        def rms_rstd(src_ap, src_res, ss_col, rstd_col, tag, inv_n):
            n = src_ap.shape[-1]
            S.op("act", lambda e: e.activation(out=junk[:, 0:n], in_=src_ap, func=AF.Square, accum_out=ss_col),
                 reads=list(src_res), writes=["junk", (tag, "ss")])
            S.op("act", lambda e: e.activation(out=ss_col, in_=ss_col, func=AF.Sqrt, scale=inv_n, bias=EPS),
                 reads=[(tag, "ss")], writes=[(tag, "ss")])
            S.op("dve", lambda e: e.reciprocal(out=rstd_col, in_=ss_col), reads=[(tag, "ss")], writes=[(tag, "rstd")])

        def rms_act(src_ap, src_res, ss_col, tag, inv_n):
            n = src_ap.shape[-1]
            S.op("act", lambda e: e.activation(out=junk[:, 0:n], in_=src_ap, func=AF.Square, accum_out=ss_col),
                 reads=list(src_res), writes=["junk", (tag, "ss")])
            S.op("act", lambda e: e.activation(out=ss_col, in_=ss_col, func=AF.Sqrt, scale=inv_n, bias=EPS),
                 reads=[(tag, "ss")], writes=[(tag, "ss")])

        def rms_dve(ss_col, rstd_col, tag):
            S.op("dve", lambda e: e.reciprocal(out=rstd_col, in_=ss_col), reads=[(tag, "ss")], writes=[(tag, "rstd")])

        def norm_dve(src_ap, src_res, hn_slot, hn_res, ss_col, rstd_col, tag):
            rms_dve(ss_col, rstd_col, tag)
            S.op("dve", lambda e: e.scalar_tensor_tensor(out=hn_slot, in0=src_ap, scalar=rstd_col, in1=gb,
                                                         op0=ALU.mult, op1=ALU.mult),
                 reads=list(src_res) + [(tag, "rstd"), "gb"], writes=[hn_res])

        def norm_part(src_ap, src_res, hn_slot, hn_res, ss_col, rstd_col, tag):
            rms_rstd(src_ap, src_res, ss_col, rstd_col, tag, 1.0 / D)
            S.op("dve", lambda e: e.scalar_tensor_tensor(out=hn_slot, in0=src_ap, scalar=rstd_col, in1=gb,
                                                         op0=ALU.mult, op1=ALU.mult),
                 reads=list(src_res) + [(tag, "rstd"), "gb"], writes=[hn_res])

        def transpose_part(tt, dstT, dst_name, hn_slot, hn_res):
            b = next_bank()
            pv = bankbf(b).rearrange("p (k t) -> p k t", k=8)
            for kt in range(8):
                S.op("pe", lambda e, kt=kt: e.transpose(out=pv[:, kt, :], in_=hn_slot[:, kt * 128:(kt + 1) * 128],
                                                        identity=ident),
                     reads=[hn_res, "ident"], writes=[psr(b)])
            S.op("act", lambda e: e.copy(out=dstT[:, :, tt * 128:(tt + 1) * 128], in_=pv),
                 reads=[psr(b)], writes=[(dst_name, k, tt) for k in range(8)])

        qz = [A.alloc(f"qz{c}", 4096, BF16).rearrange("p (h t) -> p h t", h=4) for c in range(2)]
        kT = A.alloc("kT", 4096, BF16).rearrange("p (h t) -> p h t", h=4)
        vaug = A.alloc("vaug", 4160, BF16).rearrange("p (t h e) -> p t h e", t=16, h=4)
        absd = A.alloc("absd", 3968)
        S.op("pool", lambda e: e.memset(vaug[:, :, :, 128:130], 1.0), writes=[("vaug", "init")])
        S.op("pool", lambda e: e.memset(qz[0][64:128, :, :].rearrange("p h t -> p (h t)"), 0.0), writes=[("qz0", "z")])
        S.op("pool", lambda e: e.memset(qz[1][0:64, :, :].rearrange("p h t -> p (h t)"), 0.0), writes=[("qz1", "z")])

        ev = {"i": 0}

        def unit_qk(h, which, tq):
            col0 = h * 128 if which == "q" else 512 + h * 128
            b = next_bank()
            for kt in range(8):
                S.op("pe", lambda e, kt=kt: e.matmul(
                    bank(b), lhsT=win_qkv[:, kt, col0:col0 + 128], rhs=hT[:, kt, tq * 512:(tq + 1) * 512],
                    start=(kt == 0), stop=(kt == 7)),
                     reads=[("win_qkv", kt)] + [("hT", kt, t) for t in range(tq * 4, tq * 4 + 4)],
                     writes=[psr(b)])
            if which == "q":
                for c in range(2):
                    o_ap = qz[c][c * 64:(c + 1) * 64, h, tq * 512:(tq + 1) * 512]
                    i_ap = bank(b)[c * 64:(c + 1) * 64, :]
                    if (ev["i"] + c) % 2 == 0:
                        S.op("act", lambda e, o_ap=o_ap, i_ap=i_ap: e.activation(
                            out=o_ap, in_=i_ap, func=AF.Identity, scale=0.125),
                             reads=[psr(b), (f"qz{c}", "z")], writes=[(f"qz{c}", h, tq)])
                    else:
                        S.op("dve", lambda e, o_ap=o_ap, i_ap=i_ap: e.tensor_scalar(
                            out=o_ap, in0=i_ap, scalar1=0.125, scalar2=None, op0=ALU.mult),
                             reads=[psr(b), (f"qz{c}", "z")], writes=[(f"qz{c}", h, tq)])
            else:
                o_ap = kT[:, h, tq * 512:(tq + 1) * 512]
                if ev["i"] % 2 == 0:
                    S.op("act", lambda e: e.copy(out=o_ap, in_=bank(b)), reads=[psr(b)], writes=[("kT", h, tq)])
                else:
                    S.op("dve", lambda e: e.tensor_copy(out=o_ap, in_=bank(b)), reads=[psr(b)], writes=[("kT", h, tq)])
            ev["i"] += 1

        def unit_v(tt):
            b = next_bank()
            for kt in range(8):
                S.op("pe", lambda e, kt=kt: e.matmul(
                    bank(b), lhsT=hT[:, kt, tt * 128:(tt + 1) * 128], rhs=win_qkv[:, kt, 1024:1536],
                    start=(kt == 0), stop=(kt == 7)),
                     reads=[("win_qkv", kt), ("hT", kt, tt)], writes=[psr(b)])
            src = bank(b).rearrange("p (h e) -> p h e", h=4)
            if tt % 2 == 0:
                S.op("act", lambda e: e.copy(out=vaug[:, tt, :, 0:128], in_=src),
                     reads=[psr(b), ("vaug", "init")], writes=[("vaug", tt)])
            else:
                S.op("dve", lambda e: e.tensor_copy(out=vaug[:, tt, :, 0:128], in_=src),
                     reads=[psr(b), ("vaug", "init")], writes=[("vaug", tt)])

        NS = 3
        xs = [A.alloc(f"xs{i}", 1024) for i in range(NS)]
        hnA = [A.alloc(f"hnA{i}", 512, BF16) for i in range(NS)]
        ready = []

        def chunk_units(tq):
            u = []
            for h in range(4):
                u.append(lambda h=h: unit_qk(h, "q", tq))
                u.append(lambda h=h: unit_qk(h, "k", tq))
            for t in range(tq * 4, tq * 4 + 4):
                u.append(lambda t=t: unit_v(t))
            return u

        for tt in range(NT + 1):
            if tt < NT:
                sl = tt % NS
                S.dma("sp", xs[sl], x_d[tt * 128:(tt + 1) * 128, :], f"xs{sl}", writes=[f"xs{sl}"])
                norm_part(xs[sl], [f"xs{sl}"], hnA[sl], f"hnA{sl}", ss1[:, tt:tt + 1], rstd1[:, tt:tt + 1], ("n1", tt))
            if tt >= 1:
                pt = tt - 1
                transpose_part(pt, hT, "hT", hnA[pt % NS], f"hnA{pt % NS}")
                if pt % 4 == 3:
                    ready.extend(chunk_units(pt // 4))
            for _ in range(3):
                if ready:
                    ready.pop(0)()
        while ready:
            ready.pop(0)()
        for i in range(NS):
            A.free(f"xs{i}")
            A.free(f"hnA{i}")
        A.free("win_qkv")

        for kt in range(8):
            S.dma("pool", win_lru[:, kt, :], w_in_d[kt * 128:(kt + 1) * 128, 1536:2560], "w_lru",
                  writes=[("win_lru", kt)])
        S.dma("pool", wg, wg_d.rearrange("n p c -> p n c"), "w_g", writes=["wg"])

        S.op("pool", lambda e: e.iota(absd, pattern=[[1, 3968]], base=-1920, channel_multiplier=-1,
                                      allow_small_or_imprecise_dtypes=True), writes=["absd"])
        S.op("act", lambda e: e.activation(out=absd, in_=absd, func=AF.Abs), reads=["absd"], writes=["absd"])
        NSL = 8
        sc = [A.alloc(f"sc{i}", 512) for i in range(NSL)]
        ET = [A.alloc(f"ET{i}", 256, BF16) for i in range(NSL)]
        o1 = A.alloc("o1", 512)
        ob = A.alloc("ob", 512)
        on = A.alloc("on", 256, BF16)

        btab = A.alloc("btab", 128).rearrange("p (v m) -> p v m", v=8)
        fgt = A.alloc("fgt", 32).rearrange("p (v q) -> p v q", v=8)
        numb = [A.alloc(f"numb{c}", 520)[:, 0:516].rearrange("p (q e) -> p q e", q=4) for c in range(2)]
        sqj = A.alloc("sqj", 128)
        klf = st[:, 120:121]
        cmf = st[:, 128:144]
        ptab = st[:, 144:148]
        S.op("pool", lambda e: e.iota(klf, pattern=[[0, 1]], base=0, channel_multiplier=1,
                                      allow_small_or_imprecise_dtypes=True), writes=[("zr", "klf")])
        S.op("pool", lambda e: e.iota(cmf, pattern=[[1, 16]], base=0, channel_multiplier=0,
                                      allow_small_or_imprecise_dtypes=True), writes=[("zr", "cmf")])
        S.op("pool", lambda e: e.iota(ptab, pattern=[[128, 4]], base=0, channel_multiplier=1,
                                      allow_small_or_imprecise_dtypes=True), writes=[("zr", "ptab")])
        for h_ in range(4):
            sl_ = 2.0 ** (-2.0 * (h_ + 1))
            for sg in range(2):
                v_ = h_ * 2 + sg
                ksign = sl_ if sg == 0 else -sl_
                cadd = 0.0 if sg == 0 else sl_ * 511.0
                S.op("dve", lambda e, v_=v_, sl_=sl_, cadd=cadd: e.tensor_scalar(
                    out=btab[:, v_, :], in0=cmf, scalar1=-sl_ * 128.0, scalar2=cadd, op0=ALU.mult, op1=ALU.add),
                     reads=[("zr", "cmf")], writes=[("btab", v_)])
                S.op("dve", lambda e, v_=v_, ksign=ksign: e.scalar_tensor_tensor(
                    out=btab[:, v_, :], in0=klf.to_broadcast([128, 16]), scalar=ksign, in1=btab[:, v_, :],
                    op0=ALU.mult, op1=ALU.add), reads=[("zr", "klf"), ("btab", v_)], writes=[("btab", v_)])
            S.op("act", lambda e, h_=h_, sl_=sl_: e.activation(out=fgt[:, 2 * h_, :], in_=ptab, func=AF.Exp, scale=-sl_),
                 reads=[("zr", "ptab")], writes=[("fgt", 2 * h_)])
            S.op("act", lambda e, h_=h_, sl_=sl_: e.activation(out=fgt[:, 2 * h_ + 1, :], in_=ptab, func=AF.Exp,
                                                               scale=sl_, bias=-sl_ * 511.0),
                 reads=[("zr", "ptab")], writes=[("fgt", 2 * h_ + 1)])

        iters = []
        groups_at = []
        for h in range(4):
            slope_h = 2.0 ** (-2.0 * (h + 1))
            dmax = 40.0 / slope_h
            for qc in range(4):
                cls_kbs = {"B": [], "D": [], "A": []}
                for kb in range(16):
                    q0, q1, k0, k1 = qc * 512, qc * 512 + 511, kb * 128, kb * 128 + 127
                    mind = max(0, k0 - q1, q0 - k1)
                    if mind <= dmax:
                        cl_ = "D" if h < 2 else ("B" if k1 < q0 else ("A" if k0 > q1 else "D"))
                        cls_kbs[cl_].append(kb)
                order = [cl for cl in ("B", "D", "A") if cls_kbs[cl]]
                for c in range(2):
                    for ci_, cl in enumerate(order):
                        g = len(groups_at)
                        groups_at.append((h, qc, c, cl, ci_ == 0, ci_ == len(order) - 1))
                        kbs = cls_kbs[cl]
                        for n, kb in enumerate(kbs):
                            iters.append((h, qc, c, kb, n == 0, n == len(kbs) - 1, g))
        NI = len(iters)
        SCB = (0, 1, 2, 7)
        ACC = ((3, 4), (5, 6))
        bank_of = {}
        free_sb = list(SCB)

        def take_score_bank():
            return free_sb.pop(0)

        def release_score_bank(b):
            free_sb.append(b)

        def acc_ap(grp, qi):
            bk = ACC[grp % 2][qi // 2]
            return bank(bk)[:, (qi % 2) * 129:(qi % 2) * 129 + 129], bk

        def emit_qk(i):
            h, qc, c, kb, first, last, grp = iters[i]
            sb_ = take_score_bank()
            bank_of[i] = sb_
            S.op("pe", lambda e: e.matmul(
                bank(sb_), lhsT=kT[:, h, kb * 128:(kb + 1) * 128],
                rhs=qz[c][:, h, qc * 512:(qc + 1) * 512], start=True, stop=True),
                 reads=[("kT", h, kb // 4), (f"qz{c}", h, qc), (f"qz{c}", "z")], writes=[psr(sb_)])

        def emit_bias(i):
            h, qc, c, kb, first, last, grp = iters[i]
            if groups_at[grp][3] != "D":
                return
            sb_ = bank_of[i]
            s = i % NSL
            slope = 2.0 ** (-2.0 * (h + 1))
            Dv = qc * 512 - kb * 128 + 1920
            S.op("dve", lambda e: e.scalar_tensor_tensor(out=sc[s], in0=absd[:, Dv:Dv + 512], scalar=-slope,
                                                         in1=bank(sb_), op0=ALU.mult, op1=ALU.add),
                 reads=["absd", psr(sb_)], writes=[f"sc{s}"])
            release_score_bank(sb_)

        def emit_softmax(i):
            h, qc, c, kb, first, last, grp = iters[i]
            cl = groups_at[grp][3]
            sb_ = bank_of[i]
            s = i % NSL
            if cl != "D":
                sg = 0 if cl == "B" else 1
                m_ = (4 * qc - kb) if cl == "B" else (kb - 4 * qc)
                v_ = h * 2 + sg
                S.op("act", lambda e: e.activation(out=ET[s], in_=bank(sb_), func=AF.Exp,
                                                   bias=btab[:, v_, m_:m_ + 1]),
                     reads=[psr(sb_), ("btab", v_)], writes=[f"ET{s}"])
                release_score_bank(sb_)
                return
            S.op("act", lambda e: e.activation(out=ET[s], in_=sc[s], func=AF.Exp),
                 reads=[f"sc{s}"], writes=[f"ET{s}"])

        def emit_pv(i):
            h, qc, c, kb, first, last, grp = iters[i]
            s = i % NSL
            for qi in range(4):
                dst, bk = acc_ap(grp, qi)
                S.op("pe", lambda e, dst=dst, qi=qi: e.matmul(
                    dst, lhsT=ET[s][:, qi * 128:(qi + 1) * 128], rhs=vaug[:, kb, h, 0:129],
                    start=(first and qi % 2 == 0), stop=(last and qi % 2 == 1)),
                     reads=[f"ET{s}", ("vaug", kb)], writes=[psr(bk)])
                if qi == 0:
                    S.op("pe", lambda e, bk=bk: e.matmul(
                        bank(bk)[:, 258:512], lhsT=zeros_bf, rhs=kT[:, 0, 0:254], start=False, stop=False),
                         reads=["zeros_bf", ("kT", 0, 0)], writes=[psr(bk)])

        def finalize_stages(grp):
            h, qc, c, cl, first_cls, last_cls = groups_at[grp]
            nb = numb[c]
            nn = f"numb{c}"

            def s_combine():
                if cl == "D":
                    for j in range(2):
                        bk = ACC[grp % 2][j]
                        src = bank(bk)[:, 0:258]
                        dst = nb[:, 2 * j:2 * j + 2, :].rearrange("p q e -> p (q e)")
                        if first_cls:
                            S.op("dve", lambda e, src=src, dst=dst: e.tensor_copy(out=dst, in_=src),
                                 reads=[psr(bk)], writes=[(nn, 2 * j), (nn, 2 * j + 1)])
                        else:
                            S.op("dve", lambda e, src=src, dst=dst: e.tensor_tensor(out=dst, in0=src, in1=dst, op=ALU.add),
                                 reads=[psr(bk), (nn, 2 * j), (nn, 2 * j + 1)], writes=[(nn, 2 * j), (nn, 2 * j + 1)])
                    return
                fcol = fgt[:, 2 * h + (0 if cl == "B" else 1), :]
                for qi in range(4):
                    a, bk = acc_ap(grp, qi)
                    if first_cls:
                        S.op("dve", lambda e, a=a, qi=qi: e.tensor_scalar(
                            out=nb[:, qi, :], in0=a, scalar1=fcol[:, qi:qi + 1], scalar2=None, op0=ALU.mult),
                             reads=[psr(bk), ("fgt", 2 * h), ("fgt", 2 * h + 1)], writes=[(nn, qi)])
                    else:
                        S.op("dve", lambda e, a=a, qi=qi: e.scalar_tensor_tensor(
                            out=nb[:, qi, :], in0=a, scalar=fcol[:, qi:qi + 1], in1=nb[:, qi, :],
                            op0=ALU.mult, op1=ALU.add),
                             reads=[psr(bk), ("fgt", 2 * h), ("fgt", 2 * h + 1), (nn, qi)], writes=[(nn, qi)])

            if not last_cls:
                return [(0, s_combine)]

            single = first_cls and last_cls
            on_act = h < 2

            def srcv(qi):
                if single:
                    a, bk = acc_ap(grp, qi)
                    return a, psr(bk)
                return nb[:, qi, :], (nn, qi)

            def s_recip():
                for qi in range(4):
                    a, r_ = srcv(qi)
                    S.op("dve", lambda e, qi=qi, a=a: e.reciprocal(out=zr[:, qi:qi + 1], in_=a[:, 128:129]),
                         reads=[r_], writes=[("zr", qi)])

            if c == 0:
                def s_o1():
                    for qi in range(4):
                        a, r_ = srcv(qi)
                        if on_act:
                            S.op("act", lambda e, qi=qi, a=a: e.activation(
                                out=o1[:, qi * 128:(qi + 1) * 128], in_=a[:, 0:128], func=AF.Identity,
                                scale=zr[:, qi:qi + 1]), reads=[r_, ("zr", qi)], writes=[("o1", qi)])
                        else:
                            S.op("dve", lambda e, qi=qi, a=a: e.tensor_scalar(
                                out=o1[:, qi * 128:(qi + 1) * 128], in0=a[:, 0:128], scalar1=zr[:, qi:qi + 1],
                                scalar2=None, op0=ALU.mult),
                                 reads=[r_, ("zr", qi)], writes=[("o1", qi)])
                st_ = [] if single else [(0, s_combine)]
                return st_ + [(1, s_recip), (3 if on_act else 1, s_o1)]

            def s_ob():
                s_recip()
                S.op("dve", lambda e: e.tensor_scalar(out=zr[:, 4:8], in0=zr[:, 0:4], scalar1=neglam, scalar2=None,
                                                      op0=ALU.mult),
                     reads=[("zr", q) for q in range(4)] + [("lam", 4)], writes=[("zr", 4)])
                for qi in range(4):
                    a, r_ = srcv(qi)
                    S.op("dve", lambda e, qi=qi, a=a: e.scalar_tensor_tensor(
                        out=ob[:, qi * 128:(qi + 1) * 128], in0=a[:, 0:128], scalar=zr[:, 4 + qi:5 + qi],
                        in1=o1[:, qi * 128:(qi + 1) * 128], op0=ALU.mult, op1=ALU.add),
                         reads=[r_, ("zr", 4), ("o1", qi)], writes=[("ob", qi)])

            def s_sq():
                for qi in range(4):
                    if on_act:
                        S.op("act", lambda e, qi=qi: e.activation(
                            out=junk[:, 0:128], in_=ob[:, qi * 128:(qi + 1) * 128], func=AF.Square,
                            accum_out=zr[:, 8 + qi:9 + qi]),
                             reads=[("ob", qi)], writes=["junk", ("zr", 8 + qi)])
                        continue
                    S.op("dve", lambda e, qi=qi: e.scalar_tensor_tensor(
                        out=sqj[:, 0:128], in0=ob[:, qi * 128:(qi + 1) * 128], scalar=1.0,
                        in1=ob[:, qi * 128:(qi + 1) * 128], op0=ALU.mult, op1=ALU.mult,
                        accum_out=zr[:, 8 + qi:9 + qi]),
                         reads=[("ob", qi)], writes=["sqj", ("zr", 8 + qi)])

            def s_rstd():
                S.op("act", lambda e: e.activation(out=zr[:, 8:12], in_=zr[:, 8:12], func=AF.Ln,
                                                   scale=1.0 / 128, bias=EPS),
                     reads=[("zr", 8 + q) for q in range(4)], writes=[("zr", 8 + q) for q in range(4)])
                S.op("act", lambda e: e.activation(out=zr[:, 12:16], in_=zr[:, 8:12], func=AF.Exp, scale=-0.5),
                     reads=[("zr", 8 + q) for q in range(4)], writes=[("zr", 13)])

            def s_on():
                for qi in range(4):
                    S.op("dve", lambda e, qi=qi: e.scalar_tensor_tensor(
                        out=on[:, qi * 128:(qi + 1) * 128], in0=ob[:, qi * 128:(qi + 1) * 128],
                        scalar=zr[:, 12 + qi:13 + qi], in1=gsub, op0=ALU.mult, op1=ALU.mult),
                         reads=[("ob", qi), ("zr", 13), "gsub"], writes=[("on", qi)])

            trb = {}

            def s_tr():
                trb["b"] = take_score_bank()
                pv = bankbf(trb["b"])
                for qi in range(4):
                    S.op("pe", lambda e, qi=qi: e.transpose(out=pv[:, qi * 128:(qi + 1) * 128],
                                                            in_=on[:, qi * 128:(qi + 1) * 128], identity=ident),
                         reads=[("on", qi), "ident"], writes=[psr(trb["b"])])

            def s_copy():
                pv = bankbf(trb["b"])
                if on_act:
                    S.op("act", lambda e: e.copy(out=mixA[:, h, qc * 512:(qc + 1) * 512], in_=pv[:, 0:512]),
                         reads=[psr(trb["b"])], writes=[("mixA", h, qc)])
                else:
                    S.op("dve", lambda e: e.tensor_copy(out=mixA[:, h, qc * 512:(qc + 1) * 512], in_=pv[:, 0:512]),
                         reads=[psr(trb["b"])], writes=[("mixA", h, qc)])
                release_score_bank(trb["b"])

            st_ = [] if single else [(0, s_combine)]
            return st_ + [(1, s_ob), (4 if on_act else 2, s_sq), (6, s_rstd), (9, s_on), (11, s_tr), (13, s_copy)]

        LAG = 2
        pending = {}
        LOOK = 6
        nq = {"n": 0}

        def fill_qk(cur):
            while free_sb and nq["n"] < NI and nq["n"] <= cur + LOOK:
                emit_qk(nq["n"])
                emit_bias(nq["n"])
                nq["n"] += 1

        fill_qk(0)
        for i in range(NI):
            emit_softmax(i)
            for fn in pending.pop(i, []):
                fn()
            fill_qk(i)
            emit_pv(i)
            if iters[i][5]:
                for off, fn in finalize_stages(iters[i][6]):
                    pending.setdefault(i + 1 + LAG + off, []).append(fn)
        for k in sorted(pending):
            for fn in pending[k]:
                fn()

        for nm in ("btab", "fgt", "numb0", "numb1", "sqj", "qz0", "qz1", "kT", "vaug", "absd", "sc0", "sc1", "sc2", "sc3", "sc4", "sc5", "sc6", "sc7", "ET0", "ET1", "ET2", "ET3", "ET4", "ET5", "ET6", "ET7", "o1", "ob", "on"):
            A.free(nm)

        wout = A.alloc("wout", 4096, BF16).rearrange("p (k c) -> p k c", k=8)
        mixL = A.alloc("mixL", 4096, BF16).rearrange("p (k t) -> p k t", k=4)
        for kt in range(8):
            S.dma("pool", wout[:, kt, :], w_out_d[kt * 128:(kt + 1) * 128, :], "w_out", writes=[("wout", kt)])

        xrp = A.alloc("xrp", 2056)
        gg = A.alloc("gg", 2048)
        xcs = [A.alloc("xc0", 2048, top=True), A.alloc("xc1", 2048)]
        xcb = A.alloc("xcb", 1024, BF16)
        TA = [A.alloc(f"TA{i}", 2048) for i in range(2)]
        TB = [A.alloc(f"TB{i}", 2048) for i in range(2)]
        TC0 = A.alloc("TC0", 2048)
        TD = [A.alloc(f"TD{i}", 2048) for i in range(2)]
        S.op("pool", lambda e: e.memset(xrp[:, 0:2], 0.0), writes=[("xrp", "pad0")])
        S.op("pool", lambda e: e.memset(xrp[:, 2050:2056], 0.0), writes=[("xrp", "pad1")])

        def lru_proj(col0, evac):
            for tq in range(4):
                b = next_bank()
                for kt in range(8):
                    S.op("pe", lambda e, b=b, kt=kt, tq=tq: e.matmul(
                        bank(b), lhsT=win_lru[:, kt, col0:col0 + 128], rhs=hT[:, kt, tq * 512:(tq + 1) * 512],
                        start=(kt == 0), stop=(kt == 7)),
                         reads=[("win_lru", kt)] + [("hT", kt, t) for t in range(tq * 4, tq * 4 + 4)],
                         writes=[psr(b)])
                evac(tq, b)

        def lru_A(ct):
            xc = xcs[ct % 2]
            xn = f"xc{ct % 2}"
            lru_proj(ct * 128, lambda tq, b: S.op(
                "act", lambda e: e.copy(out=xrp[:, 2 + tq * 512:2 + (tq + 1) * 512], in_=bank(b)),
                reads=[psr(b)], writes=[("xrp", tq)]))
            xr_all = [("xrp", t) for t in range(4)] + [("xrp", "pad0"), ("xrp", "pad1")]
            S.op("dve", lambda e: e.tensor_scalar(out=xc, in0=xrp[:, 0:2048], scalar1=cols[:, ct * 4:ct * 4 + 1],
                                                  scalar2=cols[:, 16 + ct:17 + ct], op0=ALU.mult, op1=ALU.add),
                 reads=xr_all + ["cols"], writes=[xn])
            for k in range(1, 4):
                S.op("dve", lambda e, k=k: e.scalar_tensor_tensor(
                    out=xc, in0=xrp[:, k:k + 2048], scalar=cols[:, ct * 4 + k:ct * 4 + k + 1], in1=xc,
                    op0=ALU.mult, op1=ALU.add),
                     reads=xr_all + ["cols", xn], writes=[xn])
            S.op("dve", lambda e: e.tensor_copy(out=xcb, in_=xc), reads=[xn], writes=["xcb"])

        def lru_G(ct):
            for d in range(2):
                ci = d * 4 + ct
                for gate, (dstt, dn, bcol) in enumerate(((TA[d], f"TA{d}", 20 + ci), (TB[d], f"TB{d}", 28 + ci))):
                    widx = (gate * 2 + d) * 4 + ct
                    for tq in range(4):
                        b = next_bank()
                        S.op("pe", lambda e, b=b, widx=widx, tq=tq: e.matmul(
                            bank(b), lhsT=wg[:, widx, :], rhs=xcb[:, tq * 512:(tq + 1) * 512], start=True, stop=True),
                             reads=["wg", "xcb"], writes=[psr(b)])
                        S.op("act", lambda e, b=b, dstt=dstt, tq=tq, bcol=bcol: e.activation(
                            out=dstt[:, tq * 512:(tq + 1) * 512], in_=bank(b), func=AF.Sigmoid,
                            bias=cols[:, bcol:bcol + 1]),
                             reads=[psr(b), "cols"], writes=[(dn, tq)])

        def lru_E(ct):
            xc = xcs[ct % 2]
            xn = f"xc{ct % 2}"
            xr_all = [("xrp", t) for t in range(4)]
            abuf = [TC0, xrp[:, 2:2050]]
            ares = [["TC0"], xr_all]
            R = [[(f"TA{d}", t) for t in range(4)] for d in range(2)]
            I = [[(f"TB{d}", t) for t in range(4)] for d in range(2)]
            for d in range(2):
                ci = d * 4 + ct
                S.op("act", lambda e, d=d, ci=ci: e.activation(out=abuf[d], in_=TA[d], func=AF.Exp,
                                                               scale=cA[:, ci:ci + 1]),
                     reads=R[d] + ["cA"], writes=ares[d])
            for d in range(2):
                ci = d * 4 + ct
                if d == 0:
                    S.op("act", lambda e, d=d, ci=ci: e.activation(out=TA[d], in_=TA[d], func=AF.Exp,
                                                                   scale=c2A[:, ci:ci + 1]),
                         reads=R[d] + ["c2A"], writes=R[d])
                else:
                    S.op("dve", lambda e, d=d: e.tensor_tensor(out=TA[d], in0=abuf[d], in1=abuf[d], op=ALU.mult),
                         reads=ares[d], writes=R[d])
                S.op("dve", lambda e, d=d: e.tensor_tensor(out=TB[d], in0=TB[d], in1=xc, op=ALU.mult),
                     reads=I[d] + [xn], writes=I[d])
            for d in range(2):
                S.op("act", lambda e, d=d: e.activation(out=TA[d], in_=TA[d], func=AF.Sqrt, scale=-1.0, bias=1.0),
                     reads=R[d], writes=R[d])
            for d in range(2):
                S.op("dve", lambda e, d=d: e.tensor_tensor(out=TB[d], in0=TB[d], in1=TA[d], op=ALU.mult),
                     reads=I[d] + R[d], writes=I[d])
                if d == 0:
                    S.op("dve", lambda e: e.tensor_tensor_scan(
                        out=TD[0], data0=abuf[0], data1=TB[0], initial=0.0, op0=ALU.mult, op1=ALU.add),
                         reads=ares[0] + I[0], writes=["TD0"])
                else:
                    S.op("dve", lambda e: e.tensor_tensor_scan(
                        out=TD[1][:, ::-1], data0=abuf[1][:, ::-1], data1=TB[1][:, ::-1], initial=0.0,
                        op0=ALU.mult, op1=ALU.add),
                         reads=ares[1] + I[1], writes=["TD1"])

        def lru_C(ct):
            lru_proj(512 + ct * 128, lambda tq, b: S.op(
                "act", lambda e: e.activation(out=gg[:, tq * 512:(tq + 1) * 512], in_=bank(b), func=AF.Gelu_apprx_tanh),
                reads=[psr(b)], writes=[("gg", tq)]))

        def lru_D(ct):
            S.op("dve", lambda e: e.tensor_tensor(out=TD[0], in0=TD[0], in1=TD[1], op=ALU.add),
                 reads=["TD0", "TD1"], writes=["TD0"])
            S.op("dve", lambda e: e.tensor_tensor(out=mixL[:, ct, :], in0=TD[0], in1=gg, op=ALU.mult),
                 reads=["TD0"] + [("gg", t) for t in range(4)], writes=[("mixL", ct, q) for q in range(4)])

        lru_A(0)
        for ct in range(4):
            lru_G(ct)
            if ct + 1 < 4:
                lru_A(ct + 1)
            lru_E(ct)
            lru_C(ct)
            lru_D(ct)
        for nm in ("xrp", "gg", "xc0", "xc1", "xcb", "TA0", "TA1", "TB0", "TB1", "TC0", "TD0", "TD1", "win_lru", "wg"):
            A.free(nm)

        x1t = [None] * NT
        for tt_ in range(NT - 1, -1, -1):
            x1t[tt_] = A.alloc(f"x1t{tt_}", 1024, top=True)
        groups = [(0, 6), (6, 12), (12, 16), (16, 22)]
        wup = [A.alloc(f"wup{i}", 2048, BF16, top=True) for i in range(2)]
        wup3 = [w.rearrange("p (k c) -> p k c", k=8) for w in wup]
        wdn = [A.alloc(f"wdn{i}", 3072, BF16, top=True).rearrange("p (j c) -> p j c", j=6) for i in range(2)]
        w_up_v = w_up_d.rearrange("(k p) c -> p k c", p=128)
        w_dn_v = w_dn_d.rearrange("(j p) c -> p j c", p=128)

        def load_wup(jp):
            sl = jp % 2
            c0 = jp * 256
            for kt in range(8):
                S.dma("pool", wup3[sl][:, kt, 0:256], w_up_d[kt * 128:(kt + 1) * 128, c0:c0 + 256], f"wup{sl}",
                      writes=[(f"wup{sl}", kt, 0)])
                S.dma("pool", wup3[sl][:, kt, 256:512], w_up_d[kt * 128:(kt + 1) * 128, DFF + c0:DFF + c0 + 256],
                      f"wup{sl}", writes=[(f"wup{sl}", kt, 1)])

        def load_wdn(g):
            j0, j1 = groups[g]
            sl = g % 2
            for jj in range(j1 - j0):
                S.dma("pool", wdn[sl][:, jj, :], w_dn_d[(j0 + jj) * 128:(j0 + jj + 1) * 128, :], f"wdn{sl}",
                      writes=[(f"wdn{sl}", jj)])

        load_wup(0)
        load_wup(1)
        load_wdn(0)
        c_order = list(range(NT - 1, -1, -1))
        S.dma("sp", gb, gvec_d[1:2, :].to_broadcast([128, D]), "c_gb", writes=["gb"])
        for tt in c_order:
            S.dma("sp", x1t[tt], x_d[tt * 128:(tt + 1) * 128, :], f"x1_{tt}", writes=[f"x1t{tt}"])
        hnC = [A.alloc(f"hnC{i}", 512, BF16) for i in range(3)]
        def c_mm(tt, cc, b, kts):
            for kt in kts:
                mx, mname = (mixA, "mixA") if kt < 4 else (mixL, "mixL")
                S.op("pe", lambda e, kt=kt, mx=mx: e.matmul(
                    bank(b), lhsT=mx[:, kt % 4, tt * 128:(tt + 1) * 128], rhs=wout[:, kt, cc * 512:(cc + 1) * 512],
                    start=(kt == 0), stop=(kt == 7)),
                     reads=[(mname, kt % 4, tt // 4), ("wout", kt)], writes=[psr(b)])

        def c_add(tt, cc, b):
            S.op("dve", lambda e: e.tensor_tensor(
                out=x1t[tt][:, cc * 512:(cc + 1) * 512], in0=bank(b), in1=x1t[tt][:, cc * 512:(cc + 1) * 512],
                op=ALU.add), reads=[psr(b), f"x1t{tt}"], writes=[f"x1t{tt}"])

        head = c_order[:4]
        hb_ = {}
        for tt in head:
            for cc in range(2):
                hb_[(tt, cc)] = next_bank()
                c_mm(tt, cc, hb_[(tt, cc)], range(7))
        def c_norm_dve(n_):
            tt = c_order[n_]
            sl = n_ % 3
            norm_dve(x1t[tt], [f"x1t{tt}"], hnC[sl], f"hnC{sl}", ss1[:, tt:tt + 1], rstd1[:, tt:tt + 1], ("n2", tt))

        def c_tr(n_):
            tt = c_order[n_]
            sl = n_ % 3
            transpose_part(tt, hT, "hT", hnC[sl], f"hnC{sl}")

        for n_, tt in enumerate(c_order):
            for cc in range(2):
                if tt in head:
                    b = hb_[(tt, cc)]
                    c_mm(tt, cc, b, [7])
                else:
                    b = next_bank()
                    c_mm(tt, cc, b, range(8))
                c_add(tt, cc, b)
            rms_act(x1t[tt], [f"x1t{tt}"], ss1[:, tt:tt + 1], ("n2", tt), 1.0 / D)
            if n_ >= 1:
                c_norm_dve(n_ - 1)
            if n_ >= 2:
                c_tr(n_ - 2)
        c_norm_dve(NT - 1)
        c_tr(NT - 2)
        c_tr(NT - 1)
        if "mixT" in debug_taps:
            taps["mixT"] = nc.dram_tensor("tap_mixT", [128, 8 * S_TOK], F32, kind="ExternalOutput").ap()
            for kt in range(8):
                mx, mname = (mixA, "mixA") if kt < 4 else (mixL, "mixL")
                S.dma("pool", taps["mixT"][:, kt * S_TOK:(kt + 1) * S_TOK], mx[:, kt % 4, :], "out",
                      reads=[(mname, kt % 4, q) for q in range(4)])
        A.free("mixA")
        A.free("mixL")
        for i in range(3):
            A.free(f"hnC{i}")
        A.free("wout")
        if "x1" in debug_taps:
            taps["x1"] = nc.dram_tensor("tap_x1", [128, NT * D], F32, kind="ExternalOutput").ap()
            for t in range(NT):
                S.dma("sp", taps["x1"][:, t * D:(t + 1) * D], x1t[t], "out", reads=[f"x1t{t}"])

        S.dma("sp", gb, gvec_d[2:3, :].to_broadcast([128, D]), "c_gb", writes=["gb"])
        actT = A.alloc("actT", 7168, BF16).rearrange("p (j t) -> p j t", j=7)
        cgb = [A.alloc(f"cg{i}", 2048) for i in range(2)]
        cvb = [A.alloc(f"cv{i}", 2048) for i in range(2)]
        outb = None
        U_g = P[:, 0:2048]
        U_v = P[:, 2048:4096]
        NSLOT = 7

        def up_tile(j):
            jp = j // 2
            sl = jp % 2
            cj = (j % 2) * 128
            bs = j % 2
            cg, cv = cgb[bs], cvb[bs]
            slot = j % NSLOT
            for half, (U, boff, wc0, cbuf, cname, jt) in enumerate((
                    (U_g, 0, cj, cg, f"cg{bs}", j), (U_v, 4, 256 + cj, cv, f"cv{bs}", NJ + j))):
                for tq in range(4):
                    b = boff + tq
                    for kt in range(8):
                        S.op("pe", lambda e, b=b, kt=kt, tq=tq, wc0=wc0: e.matmul(
                            bank(b), lhsT=wup3[sl][:, kt, wc0:wc0 + 128], rhs=hT[:, kt, tq * 512:(tq + 1) * 512],
                            start=(kt == 0), stop=(kt == 7)),
                             reads=[(f"wup{sl}", kt, half)] + [("hT", kt, t) for t in range(tq * 4, tq * 4 + 4)],
                             writes=[psr(b)])
                rb = [psr(boff + t) for t in range(4)]
                w0 = cols[:, 44 + jt * 3:45 + jt * 3]
                w1 = cols[:, 45 + jt * 3:46 + jt * 3]
                w2 = cols[:, 46 + jt * 3:47 + jt * 3]
                bb = cols[:, 176 + jt:177 + jt]
                S.op("act", lambda e, U=U, cbuf=cbuf, w1=w1, bb=bb: e.activation(
                    out=cbuf, in_=U, func=AF.Identity, scale=w1, bias=bb),
                     reads=rb + ["cols"], writes=[cname])
                if half == 1:
                    S.op("act", lambda e: e.activation(out=cg, in_=cg, func=AF.Gelu_apprx_tanh),
                         reads=[f"cg{bs}"], writes=[f"cg{bs}"])
                S.op("dve", lambda e, U=U, cbuf=cbuf, w0=w0: e.scalar_tensor_tensor(
                    out=cbuf[:, 1:2048], in0=U[:, 0:2047], scalar=w0, in1=cbuf[:, 1:2048],
                    op0=ALU.mult, op1=ALU.add), reads=rb + ["cols", cname], writes=[cname])
                S.op("dve", lambda e, U=U, cbuf=cbuf, w2=w2: e.scalar_tensor_tensor(
                    out=cbuf[:, 0:2047], in0=U[:, 1:2048], scalar=w2, in1=cbuf[:, 0:2047],
                    op0=ALU.mult, op1=ALU.add), reads=rb + ["cols", cname], writes=[cname])
            if j % 2 == 1 and jp + 2 < 11:
                load_wup(jp + 2)
            S.op("dve", lambda e: e.tensor_tensor(out=actT[:, slot, :], in0=cg, in1=cv, op=ALU.mult),
                 reads=[f"cg{bs}", f"cv{bs}"], writes=[("actT", slot)])

        def fin_out(tt):
            osl = tt % 4
            ob_ = outb_box[0]
            rms_dve(ss3[:, tt:tt + 1], rstd3[:, tt:tt + 1], ("n3", tt))
            S.op("dve", lambda e: e.scalar_tensor_tensor(
                out=ob_[osl], in0=x1t[tt], scalar=rstd3[:, tt:tt + 1], in1=gb, op0=ALU.mult, op1=ALU.mult),
                 reads=[f"x1t{tt}", (("n3", tt), "rstd"), "gb"], writes=[f"outb{osl}"])
            S.dma("sp", out_d[tt * 128:(tt + 1) * 128, :], ob_[osl], f"out{osl}", reads=[f"outb{osl}"])

        def down_partial(g):
            nonlocal_outb = outb_box
            j0, j1 = groups[g]
            last = (g == len(groups) - 1)
            nj = j1 - j0
            sl = g % 2
            for tt in range(NT):
                for cc in range(2):
                    b = next_bank()
                    for jj in range(nj):
                        slot = (j0 + jj) % NSLOT
                        S.op("pe", lambda e, b=b, jj=jj, tt=tt, cc=cc, slot=slot: e.matmul(
                            bank(b), lhsT=actT[:, slot, tt * 128:(tt + 1) * 128],
                            rhs=wdn[sl][:, jj, cc * 512:(cc + 1) * 512],
                            start=(jj == 0), stop=(jj == nj - 1)),
                             reads=[("actT", slot), (f"wdn{sl}", jj)], writes=[psr(b)])
                    S.op("dve", lambda e, b=b, tt=tt, cc=cc: e.tensor_tensor(
                        out=x1t[tt][:, cc * 512:(cc + 1) * 512], in0=bank(b), in1=x1t[tt][:, cc * 512:(cc + 1) * 512],
                        op=ALU.add), reads=[psr(b), f"x1t{tt}"], writes=[f"x1t{tt}"])
                if last:
                    rms_act(x1t[tt], [f"x1t{tt}"], ss3[:, tt:tt + 1], ("n3", tt), 1.0 / D)
                    if tt >= 1:
                        fin_out(tt - 1)
            if last:
                fin_out(NT - 1)

        outb_box = [None]
        NG = len(groups)
        for j in range(groups[0][0], groups[0][1]):
            up_tile(j)
        for g in range(NG):
            if g + 1 < NG:
                load_wdn(g + 1)
                up_tile(groups[g + 1][0])
            else:
                for nm in ("cg0", "cg1", "cv0", "cv1"):
                    A.free(nm)
                outb_box[0] = [A.alloc(f"outb{i}", 1024) for i in range(4)]
            down_partial(g)
            if g + 1 < NG:
                for j in range(groups[g + 1][0] + 1, groups[g + 1][1]):
                    up_tile(j)
        S.emit(final_waits=[k for k in ("out", "out0", "out1", "out2", "out3") if k in S.dma_sems])
    return nc, S


_CACHE = {}


def _pack_inputs(inp):
    f = lambda a: np.ascontiguousarray(np.asarray(a, dtype=np.float32))
    cols = np.zeros((128, 220), np.float32)
    cw = f(inp["lru_conv_w"])[0]
    cb = f(inp["lru_conv_b"])[0]
    for ct in range(4):
        for k in range(4):
            cols[:, ct * 4 + k] = cw[k, ct * 128:(ct + 1) * 128]
        cols[:, 16 + ct] = cb[ct * 128:(ct + 1) * 128]
    ba = f(inp["lru_b_a"])[0]
    bx = f(inp["lru_b_x"])[0]
    lm = f(inp["lru_lambda"])[0]
    for d in range(2):
        for ct in range(4):
            cols[:, 20 + d * 4 + ct] = ba[d, ct * 128:(ct + 1) * 128]
            cols[:, 28 + d * 4 + ct] = bx[d, ct * 128:(ct + 1) * 128]
            cols[:, 36 + d * 4 + ct] = lm[d, ct * 128:(ct + 1) * 128]
    fw_ = f(inp["ffn_conv_w"])[0]
    fb_ = f(inp["ffn_conv_b"])[0]
    for jt in range(44):
        for k in range(3):
            cols[:, 44 + jt * 3 + k] = fw_[k, jt * 128:(jt + 1) * 128]
        cols[:, 176 + jt] = fb_[jt * 128:(jt + 1) * 128]
    rows = np.concatenate([f(inp["lambda_q1"])[0], f(inp["lambda_k1"])[0], f(inp["lambda_q2"])[0],
                           f(inp["lambda_k2"])[0], f(inp["subln_g"])[0]])[None, :]
    gvec = np.stack([f(inp["attn_norm_g"])[0], f(inp["ffn_norm_g"])[0], f(inp["final_norm_g"])], 0)
    wa = f(inp["lru_w_a"])[0]
    wx = f(inp["lru_w_x"])[0]
    wg = np.zeros((16, 128, 128), np.float32)
    for gate, w in enumerate((wa, wx)):
        for d in range(2):
            for ct in range(4):
                idx = (gate * 2 + d) * 4 + ct
                wg[idx, 0:64, 0:64] = w[d, 2 * ct]
                wg[idx, 64:128, 64:128] = w[d, 2 * ct + 1]
    shared = {
        "w_in": f(inp["w_in"])[0], "w_out": f(inp["w_out"])[0], "w_up": f(inp["w_up"])[0],
        "w_down": f(inp["w_down"])[0], "wg": wg, "cols": cols, "rows": np.ascontiguousarray(rows),
        "gvec": np.ascontiguousarray(gvec),
    }
    return shared


def kernel(**inputs):
    x = np.asarray(inputs["x"], dtype=np.float32)
    B = x.shape[0]
    shared = _pack_inputs(inputs)
    taps = tuple(t for t in os.environ.get("KTAPS", "").split(",") if t)
    key = ("nc", taps)
    if key not in _CACHE:
        _CACHE[key] = build(debug_taps=taps)
    nc, _ = _CACHE[key]
    in_maps = []
    for b in range(B):
        m = dict(shared)
        m["x"] = np.ascontiguousarray(x[b])
        in_maps.append(m)
    res = run_bass_kernel_spmd(nc, in_maps, core_ids=list(range(B)))
    out = np.stack([np.asarray(r["out"], dtype=np.float32) for r in res.results], 0)
    if taps:
        kernel.last_taps = [{k: v for k, v in r.items() if k.startswith("tap_")} for r in res.results]
    return out
```

```python
import os
from contextlib import ExitStack
import numpy as np
import concourse.bass as bass
import concourse.mybir as mybir
from concourse.bass_utils import run_bass_kernel_spmd

F32 = mybir.dt.float32
BF16 = mybir.dt.bfloat16
AF = mybir.ActivationFunctionType
ALU = mybir.AluOpType

ENGS = ("pe", "act", "dve", "pool", "sp")
S_TOK = 2048
D = 1024
NT = 16
DFF = 2816
NJ = 22
EPS = 1e-6
LAMBDA_INIT = 0.8 - 0.6 * 1.0


class Op:
    __slots__ = ("eng", "fn", "deps", "signal", "count", "is_dma", "dma_sem", "dma_key", "dma_target", "name")

    def __init__(self, eng, fn, name=""):
        self.eng = eng
        self.fn = fn
        self.deps = []
        self.signal = False
        self.count = None
        self.is_dma = False
        self.dma_sem = None
        self.dma_key = None
        self.dma_target = 0
        self.name = name


def _buf(r):
    return r[0] if isinstance(r, tuple) else r


class Sched:
    def __init__(self, nc, es):
        self.nc = nc
        self.es = es
        self.per_eng = {e: [] for e in ENGS}
        self.last_writer = {}
        self.readers = {}
        self.eng_sem = {e: es.enter_context(nc.semaphore("prog_" + e)) for e in ENGS if e != "sp"}
        self.dma_sems = {}
        self.dma_counts = {}
        self.buf_deps = {}
        self.nops = 0

    def _add_dep(self, o, d):
        if d is o:
            return
        if d.is_dma:
            o.deps.append((d, 16 * self.dma_counts[d.dma_key]))
            return
        if d.eng == o.eng and o.eng == "pe":
            return
        d.signal = True
        o.deps.append((d, None))

    NON_ARENA = ("ps", "psu", "lam", "zr", "ecol", "pcol", "cA", "c2A", "wupb", "wdnb")

    def _check(self, rs):
        for r in rs:
            b = _buf(r)
            if isinstance(b, tuple) or b in self.NON_ARENA or b in self.buf_deps:
                continue
            raise RuntimeError(f"resource {r!r} is not an arena buffer")

    def op(self, eng, fn, reads=(), writes=(), name=""):
        self._check(reads)
        self._check(writes)
        o = Op(eng, fn, name)
        deps = {}
        for r in reads:
            w = self.last_writer.get(r)
            if w is not None:
                deps[id(w)] = w
        for r in writes:
            w = self.last_writer.get(r)
            if w is not None:
                deps[id(w)] = w
            for rd in self.readers.get(r, {}).values():
                deps[id(rd)] = rd
        for r in list(reads) + list(writes):
            for d in self.buf_deps.get(_buf(r), ()):
                deps[id(d)] = d
        for d in deps.values():
            self._add_dep(o, d)
        for r in reads:
            rd = self.readers.setdefault(r, {})
            key = ("dma", id(o)) if False else eng
            rd[key] = o
        for r in writes:
            self.last_writer[r] = o
            self.readers[r] = {}
        self.per_eng[eng].append(o)
        self.nops += 1
        return o

    def dma(self, queue, out, in_, semkey, reads=(), writes=(), name="", **kw):
        if semkey not in self.dma_sems:
            self.dma_sems[semkey] = self.es.enter_context(self.nc.semaphore("dma_" + semkey))
            self.dma_counts[semkey] = 0

        def fn(e):
            return e.dma_start(out=out, in_=in_, **kw)

        self._check(reads)
        self._check(writes)
        o = Op(queue, fn, name)
        o.is_dma = True
        o.dma_key = semkey
        o.dma_sem = self.dma_sems[semkey]
        deps = {}
        for r in reads:
            w = self.last_writer.get(r)
            if w is not None:
                deps[id(w)] = w
        for r in writes:
            w = self.last_writer.get(r)
            rds = self.readers.get(r, {})
            if w is not None and not (w.is_dma and rds):
                deps[id(w)] = w
            for rd in rds.values():
                deps[id(rd)] = rd
        for r in list(reads) + list(writes):
            for d in self.buf_deps.get(_buf(r), ()):
                deps[id(d)] = d
        for d in deps.values():
            self._add_dep(o, d)
        self.dma_counts[semkey] += 1
        o.dma_target = 16 * self.dma_counts[semkey]
        for r in reads:
            self.readers.setdefault(r, {})[("dma", semkey)] = o
        for r in writes:
            self.last_writer[r] = o
            self.readers[r] = {}
        self.per_eng[queue].append(o)
        return o

    def ops_touching(self, bufname):
        out = {}
        for r, w in self.last_writer.items():
            if _buf(r) == bufname:
                out[id(w)] = w
        for r, rd in self.readers.items():
            if _buf(r) == bufname:
                for o in rd.values():
                    out[id(o)] = o
        return list(out.values())

    def finalize(self):
        for e in ENGS:
            c = 0
            for o in self.per_eng[e]:
                if o.is_dma:
                    continue
                if o.signal:
                    c += 1
                    o.count = c

    def emit_engine(self, ename, e):
        known = {}
        for o in self.per_eng[ename]:
            waits = {}
            for d, ov in o.deps:
                if d.is_dma:
                    key = ("dma", d.dma_key)
                    sem, val = d.dma_sem, ov
                else:
                    key = ("eng", d.eng)
                    sem, val = self.eng_sem[d.eng], d.count
                if known.get(key, 0) >= val:
                    continue
                if key not in waits or waits[key][1] < val:
                    waits[key] = (sem, val)
            for key, (sem, val) in waits.items():
                e.wait_ge(sem, val)
                known[key] = val
            ins = o.fn(e)
            if o.is_dma:
                ins.then_inc(o.dma_sem, 16)
            elif o.signal:
                ins.then_inc(self.eng_sem[ename], 1)

    def emit(self, final_waits=()):
        self.finalize()
        nc = self.nc
        with nc.Block() as block:
            @block.tensor
            def _(e):
                self.emit_engine("pe", e)

            @block.scalar
            def _(e):
                self.emit_engine("act", e)

            @block.vector
            def _(e):
                self.emit_engine("dve", e)

            @block.gpsimd
            def _(e):
                self.emit_engine("pool", e)

            @block.sync
            def _(e):
                self.emit_engine("sp", e)
                for k in final_waits:
                    e.wait_ge(self.dma_sems[k], 16 * self.dma_counts[k])


class Arena:
    def __init__(self, S, tensor, size):
        self.S = S
        self.t = tensor
        self.size = size
        self.live = {}
        self.retired = []

    def alloc(self, name, n, dt=F32, top=False):
        n = (n + 7) // 8 * 8
        spans = sorted(self.live.values())
        gaps = []
        pos = 0
        for (o, m) in spans:
            if o - pos >= n:
                gaps.append((pos, o))
            pos = max(pos, o + m)
        if self.size - pos >= n:
            gaps.append((pos, self.size))
        if not gaps:
            raise RuntimeError(f"arena full allocating {name} ({n}); live={self.live}")
        if top:
            off = gaps[-1][1] - n
        else:
            off = gaps[0][0]
        self.live[name] = (off, n)
        deps = []
        for (o, m, ops) in self.retired:
            if o < off + n and off < o + m:
                deps.extend(ops)
        self.S.buf_deps[name] = deps
        v = self.t[:, off:off + n]
        return v if dt == F32 else v.bitcast(dt)

    def free(self, name):
        off, n = self.live.pop(name)
        self.retired.append((off, n, self.S.ops_touching(name)))


def build(debug_taps=()):
    nc = bass.Bass("TRN2", target_bir_lowering=False)
    dram_in = lambda n, s: nc.dram_tensor(n, list(s), F32, kind="ExternalInput").ap()
    x_d = dram_in("x", [S_TOK, D])
    w_in_d = dram_in("w_in", [D, 2560])
    w_out_d = dram_in("w_out", [D, D])
    w_up_d = dram_in("w_up", [D, 2 * DFF])
    w_dn_d = dram_in("w_down", [DFF, D])
    wg_d = dram_in("wg", [16, 128, 128])
    cols_d = dram_in("cols", [128, 220])
    rows_d = dram_in("rows", [1, 384])
    gvec_d = dram_in("gvec", [3, D])
    out_d = nc.dram_tensor("out", [S_TOK, D], F32, kind="ExternalOutput").ap()
    taps = {}

    es = ExitStack()
    with es:
        S = Sched(nc, es)
        ARENA_N = 53200
        arena_t = es.enter_context(nc.sbuf_tensor("arena", [128, ARENA_N], F32))
        A = Arena(S, arena_t, ARENA_N)
        P = es.enter_context(nc.psum_tensor("P", [128, 4096], F32))

        def bank(b):
            return P[:, b * 512:(b + 1) * 512]

        def bankbf(b):
            return P[:, b * 512:(b + 1) * 512].bitcast(BF16)

        def psr(b):
            return ("ps", b)

        rot = {"i": 0}

        def next_bank(lo=0, hi=8):
            n = hi - lo
            b = lo + rot["i"] % n
            rot["i"] += 1
            return b

        cols = A.alloc("cols", 224)
        rowsb = A.alloc("rowsb", 384)
        gb = A.alloc("gb", 1024)
        ident = A.alloc("ident", 64, BF16)
        zeros_bf = A.alloc("zeros_bf", 64, BF16)
        identf = A.alloc("identf", 128)
        st = A.alloc("stats", 256)
        ss1 = st[:, 0:16]
        rstd1 = st[:, 16:32]
        lam_s = st[:, 32:40]
        cA = st[:, 40:48]
        c2A = st[:, 48:56]
        ecol = st[:, 56:64]
        pcol = st[:, 64:72]
        zr = st[:, 72:88]
        ss3 = st[:, 88:104]
        rstd3 = st[:, 104:120]
        gsub = A.alloc("gsub", 128)
        junk = A.alloc("junk", 512, BF16)

        S.dma("sp", cols[:, 0:220], cols_d, "c_cols", writes=["cols"])
        S.dma("sp", rowsb, rows_d[0:1, :].to_broadcast([128, 384]), "c_rows", writes=["rowsb"])
        S.dma("sp", gb, gvec_d[0:1, :].to_broadcast([128, D]), "c_gb", writes=["gb"])

        S.op("pool", lambda e: e.iota(identf, pattern=[[1, 128]], base=0, channel_multiplier=-1,
                                      allow_small_or_imprecise_dtypes=True), writes=["identf"])
        S.op("dve", lambda e: e.tensor_single_scalar(out=ident, in_=identf, scalar=0.0, op=ALU.is_equal),
             reads=["identf"], writes=["ident"])

        S.op("pool", lambda e: e.memset(zeros_bf, 0.0), writes=["zeros_bf"])
        S.op("dve", lambda e: e.scalar_tensor_tensor(out=junk[:, 0:64], in0=rowsb[:, 0:64], scalar=1.0,
                                                     in1=rowsb[:, 64:128], op0=ALU.mult, op1=ALU.mult,
                                                     accum_out=lam_s[:, 0:1]),
             reads=["rowsb"], writes=["junk", ("lam", 0)])
        S.op("dve", lambda e: e.scalar_tensor_tensor(out=junk[:, 64:128], in0=rowsb[:, 128:192], scalar=1.0,
                                                     in1=rowsb[:, 192:256], op0=ALU.mult, op1=ALU.mult,
                                                     accum_out=lam_s[:, 1:2]),
             reads=["rowsb"], writes=["junk", ("lam", 1)])
        S.op("act", lambda e: e.activation(out=lam_s[:, 2:4], in_=lam_s[:, 0:2], func=AF.Exp),
             reads=[("lam", 0), ("lam", 1)], writes=[("lam", 2)])
        S.op("dve", lambda e: e.tensor_tensor(out=lam_s[:, 4:5], in0=lam_s[:, 3:4], in1=lam_s[:, 2:3],
                                              op=ALU.subtract), reads=[("lam", 2)], writes=[("lam", 4)])
        S.op("dve", lambda e: e.tensor_scalar(out=lam_s[:, 4:5], in0=lam_s[:, 4:5], scalar1=-LAMBDA_INIT,
                                              scalar2=None, op0=ALU.add), reads=[("lam", 4)], writes=[("lam", 4)])
        neglam = lam_s[:, 4:5]
        S.op("dve", lambda e: e.tensor_scalar(out=gsub, in0=rowsb[:, 256:384], scalar1=1.0 - LAMBDA_INIT,
                                              scalar2=None, op0=ALU.mult), reads=["rowsb"], writes=["gsub"])
        S.op("act", lambda e: e.activation(out=ecol, in_=cols[:, 36:44], func=AF.Exp, scale=-1.0),
             reads=["cols"], writes=["ecol"])
        S.op("dve", lambda e: e.tensor_scalar(out=pcol, in0=ecol, scalar1=1.0 / 7.0, scalar2=None, op0=ALU.mult),
             reads=["ecol"], writes=["pcol"])
        for cst in (-1.0 / 6, 1.0 / 5, -1.0 / 4, 1.0 / 3, -1.0 / 2, 1.0):
            S.op("dve", lambda e, cst=cst: e.scalar_tensor_tensor(out=pcol, in0=pcol, scalar=float(cst), in1=ecol,
                                                                  op0=ALU.add, op1=ALU.mult),
                 reads=["pcol", "ecol"], writes=["pcol"])
        S.op("dve", lambda e: e.tensor_scalar(out=cA, in0=pcol, scalar1=-8.0, scalar2=None, op0=ALU.mult),
             reads=["pcol"], writes=["cA"])
        S.op("dve", lambda e: e.tensor_scalar(out=c2A, in0=pcol, scalar1=-16.0, scalar2=None, op0=ALU.mult),
             reads=["pcol"], writes=["c2A"])

        hT = A.alloc("hT", 8192, BF16).rearrange("p (k t) -> p k t", k=8)
        mixA = A.alloc("mixA", 4096, BF16).rearrange("p (k t) -> p k t", k=4)
        win_qkv = A.alloc("win_qkv", 6144, BF16).rearrange("p (k c) -> p k c", k=8)
        win_lru = A.alloc("win_lru", 4096, BF16).rearrange("p (k c) -> p k c", k=8)
        wg = A.alloc("wg", 1024, BF16).rearrange("p (n c) -> p n c", n=16)

        w_in_v = w_in_d.rearrange("(k p) c -> p k c", p=128)
        for kt in range(8):
            S.dma("pool", win_qkv[:, kt, :], w_in_d[kt * 128:(kt + 1) * 128, 0:1536], "w_qkv",
                  writes=[("win_qkv", kt)])

        def rms_rstd(src_ap, src_res, ss_col, rstd_col, tag, inv_n):
            n = src_ap.shape[-1]
            S.op("act", lambda e: e.activation(out=junk[:, 0:n], in_=src_ap, func=AF.Square, accum_out=ss_col),
                 reads=list(src_res), writes=["junk", (tag, "ss")])
            S.op("act", lambda e: e.activation(out=ss_col, in_=ss_col, func=AF.Sqrt, scale=inv_n, bias=EPS),
                 reads=[(tag, "ss")], writes=[(tag, "ss")])
            S.op("dve", lambda e: e.reciprocal(out=rstd_col, in_=ss_col), reads=[(tag, "ss")], writes=[(tag, "rstd")])

        def rms_act(src_ap, src_res, ss_col, tag, inv_n):
            n = src_ap.shape[-1]
            S.op("act", lambda e: e.activation(out=junk[:, 0:n], in_=src_ap, func=AF.Square, accum_out=ss_col),
                 reads=list(src_res), writes=["junk", (tag, "ss")])
            S.op("act", lambda e: e.activation(out=ss_col, in_=ss_col, func=AF.Sqrt, scale=inv_n, bias=EPS),
                 reads=[(tag, "ss")], writes=[(tag, "ss")])

        def rms_dve(ss_col, rstd_col, tag):
            S.op("dve", lambda e: e.reciprocal(out=rstd_col, in_=ss_col), reads=[(tag, "ss")], writes=[(tag, "rstd")])

        def norm_dve(src_ap, src_res, hn_slot, hn_res, ss_col, rstd_col, tag):
            rms_dve(ss_col, rstd_col, tag)
            S.op("dve", lambda e: e.scalar_tensor_tensor(out=hn_slot, in0=src_ap, scalar=rstd_col, in1=gb,
                                                         op0=ALU.mult, op1=ALU.mult),
                 reads=list(src_res) + [(tag, "rstd"), "gb"], writes=[hn_res])

        def norm_part(src_ap, src_res, hn_slot, hn_res, ss_col, rstd_col, tag):
            rms_rstd(src_ap, src_res, ss_col, rstd_col, tag, 1.0 / D)
            S.op("dve", lambda e: e.scalar_tensor_tensor(out=hn_slot, in0=src_ap, scalar=rstd_col, in1=gb,
                                                         op0=ALU.mult, op1=ALU.mult),
                 reads=list(src_res) + [(tag, "rstd"), "gb"], writes=[hn_res])

        def transpose_part(tt, dstT, dst_name, hn_slot, hn_res):
            b = next_bank()
            pv = bankbf(b).rearrange("p (k t) -> p k t", k=8)
            for kt in range(8):
                S.op("pe", lambda e, kt=kt: e.transpose(out=pv[:, kt, :], in_=hn_slot[:, kt * 128:(kt + 1) * 128],
                                                        identity=ident),
                     reads=[hn_res, "ident"], writes=[psr(b)])
            S.op("act", lambda e: e.copy(out=dstT[:, :, tt * 128:(tt + 1) * 128], in_=pv),
                 reads=[psr(b)], writes=[(dst_name, k, tt) for k in range(8)])

        qz = [A.alloc(f"qz{c}", 4096, BF16).rearrange("p (h t) -> p h t", h=4) for c in range(2)]
        kT = A.alloc("kT", 4096, BF16).rearrange("p (h t) -> p h t", h=4)
        vaug = A.alloc("vaug", 4160, BF16).rearrange("p (t h e) -> p t h e", t=16, h=4)
        absd = A.alloc("absd", 3968)
        S.op("pool", lambda e: e.memset(vaug[:, :, :, 128:130], 1.0), writes=[("vaug", "init")])
        S.op("pool", lambda e: e.memset(qz[0][64:128, :, :].rearrange("p h t -> p (h t)"), 0.0), writes=[("qz0", "z")])
        S.op("pool", lambda e: e.memset(qz[1][0:64, :, :].rearrange("p h t -> p (h t)"), 0.0), writes=[("qz1", "z")])

        ev = {"i": 0}

        def unit_qk(h, which, tq):
            col0 = h * 128 if which == "q" else 512 + h * 128
            b = next_bank()
            for kt in range(8):
                S.op("pe", lambda e, kt=kt: e.matmul(
                    bank(b), lhsT=win_qkv[:, kt, col0:col0 + 128], rhs=hT[:, kt, tq * 512:(tq + 1) * 512],
                    start=(kt == 0), stop=(kt == 7)),
                     reads=[("win_qkv", kt)] + [("hT", kt, t) for t in range(tq * 4, tq * 4 + 4)],
                     writes=[psr(b)])
            if which == "q":
                for c in range(2):
                    o_ap = qz[c][c * 64:(c + 1) * 64, h, tq * 512:(tq + 1) * 512]
                    i_ap = bank(b)[c * 64:(c + 1) * 64, :]
                    if (ev["i"] + c) % 2 == 0:
                        S.op("act", lambda e, o_ap=o_ap, i_ap=i_ap: e.activation(
                            out=o_ap, in_=i_ap, func=AF.Identity, scale=0.125),
                             reads=[psr(b), (f"qz{c}", "z")], writes=[(f"qz{c}", h, tq)])
                    else:
                        S.op("dve", lambda e, o_ap=o_ap, i_ap=i_ap: e.tensor_scalar(
                            out=o_ap, in0=i_ap, scalar1=0.125, scalar2=None, op0=ALU.mult),
                             reads=[psr(b), (f"qz{c}", "z")], writes=[(f"qz{c}", h, tq)])
            else:
                o_ap = kT[:, h, tq * 512:(tq + 1) * 512]
                if ev["i"] % 2 == 0:
                    S.op("act", lambda e: e.copy(out=o_ap, in_=bank(b)), reads=[psr(b)], writes=[("kT", h, tq)])
                else:
                    S.op("dve", lambda e: e.tensor_copy(out=o_ap, in_=bank(b)), reads=[psr(b)], writes=[("kT", h, tq)])
            ev["i"] += 1

        def unit_v(tt):
            b = next_bank()
            for kt in range(8):
                S.op("pe", lambda e, kt=kt: e.matmul(
                    bank(b), lhsT=hT[:, kt, tt * 128:(tt + 1) * 128], rhs=win_qkv[:, kt, 1024:1536],
                    start=(kt == 0), stop=(kt == 7)),
                     reads=[("win_qkv", kt), ("hT", kt, tt)], writes=[psr(b)])
            src = bank(b).rearrange("p (h e) -> p h e", h=4)
            if tt % 2 == 0:
                S.op("act", lambda e: e.copy(out=vaug[:, tt, :, 0:128], in_=src),
                     reads=[psr(b), ("vaug", "init")], writes=[("vaug", tt)])
            else:
                S.op("dve", lambda e: e.tensor_copy(out=vaug[:, tt, :, 0:128], in_=src),
                     reads=[psr(b), ("vaug", "init")], writes=[("vaug", tt)])

        NS = 3
        xs = [A.alloc(f"xs{i}", 1024) for i in range(NS)]
        hnA = [A.alloc(f"hnA{i}", 512, BF16) for i in range(NS)]
        ready = []

        def chunk_units(tq):
            u = []
            for h in range(4):
                u.append(lambda h=h: unit_qk(h, "q", tq))
                u.append(lambda h=h: unit_qk(h, "k", tq))
            return u

        for tt in range(NT + 1):
            if tt < NT:
                sl = tt % NS
                S.dma("sp", xs[sl], x_d[tt * 128:(tt + 1) * 128, :], f"xs{sl}", writes=[f"xs{sl}"])
                norm_part(xs[sl], [f"xs{sl}"], hnA[sl], f"hnA{sl}", ss1[:, tt:tt + 1], rstd1[:, tt:tt + 1], ("n1", tt))
            if tt >= 1:
                pt = tt - 1
                transpose_part(pt, hT, "hT", hnA[pt % NS], f"hnA{pt % NS}")
                ready.append(lambda t=pt: unit_v(t))
                if pt % 4 == 3:
                    ready.extend(chunk_units(pt // 4))
            for _ in range(3):
                if ready:
                    ready.pop(0)()
        while ready:
            ready.pop(0)()
        for i in range(NS):
            A.free(f"xs{i}")
            A.free(f"hnA{i}")
        A.free("win_qkv")

        for kt in range(8):
            S.dma("pool", win_lru[:, kt, :], w_in_d[kt * 128:(kt + 1) * 128, 1536:2560], "w_lru",
                  writes=[("win_lru", kt)])
        S.dma("pool", wg, wg_d.rearrange("n p c -> p n c"), "w_g", writes=["wg"])

        S.op("pool", lambda e: e.iota(absd, pattern=[[1, 3968]], base=-1920, channel_multiplier=-1,
                                      allow_small_or_imprecise_dtypes=True), writes=["absd"])
        S.op("act", lambda e: e.activation(out=absd, in_=absd, func=AF.Abs), reads=["absd"], writes=["absd"])
        NSL = 6
        sc = [A.alloc(f"sc{i}", 512) for i in range(NSL)]
        ET = [A.alloc(f"ET{i}", 256, BF16) for i in range(NSL)]
        o1 = A.alloc("o1", 512)
        ob = A.alloc("ob", 512)
        on = A.alloc("on", 256, BF16)

        btab = A.alloc("btab", 128).rearrange("p (v m) -> p v m", v=8)
        fgt = A.alloc("fgt", 32).rearrange("p (v q) -> p v q", v=8)
        numb = [A.alloc(f"numb{c}", 520)[:, 0:516].rearrange("p (q e) -> p q e", q=4) for c in range(2)]
        sqj = A.alloc("sqj", 128)
        klf = st[:, 120:121]
        cmf = st[:, 128:144]
        ptab = st[:, 144:148]
        S.op("pool", lambda e: e.iota(klf, pattern=[[0, 1]], base=0, channel_multiplier=1,
                                      allow_small_or_imprecise_dtypes=True), writes=[("zr", "klf")])
        S.op("pool", lambda e: e.iota(cmf, pattern=[[1, 16]], base=0, channel_multiplier=0,
                                      allow_small_or_imprecise_dtypes=True), writes=[("zr", "cmf")])
        S.op("pool", lambda e: e.iota(ptab, pattern=[[128, 4]], base=0, channel_multiplier=1,
                                      allow_small_or_imprecise_dtypes=True), writes=[("zr", "ptab")])
        for h_ in range(4):
            sl_ = 2.0 ** (-2.0 * (h_ + 1))
            for sg in range(2):
                v_ = h_ * 2 + sg
                ksign = sl_ if sg == 0 else -sl_
                cadd = 0.0 if sg == 0 else sl_ * 511.0
                S.op("dve", lambda e, v_=v_, sl_=sl_, cadd=cadd: e.tensor_scalar(
                    out=btab[:, v_, :], in0=cmf, scalar1=-sl_ * 128.0, scalar2=cadd, op0=ALU.mult, op1=ALU.add),
                     reads=[("zr", "cmf")], writes=[("btab", v_)])
                S.op("dve", lambda e, v_=v_, ksign=ksign: e.scalar_tensor_tensor(
                    out=btab[:, v_, :], in0=klf.to_broadcast([128, 16]), scalar=ksign, in1=btab[:, v_, :],
                    op0=ALU.mult, op1=ALU.add), reads=[("zr", "klf"), ("btab", v_)], writes=[("btab", v_)])
            S.op("act", lambda e, h_=h_, sl_=sl_: e.activation(out=fgt[:, 2 * h_, :], in_=ptab, func=AF.Exp, scale=-sl_),
                 reads=[("zr", "ptab")], writes=[("fgt", 2 * h_)])
            S.op("act", lambda e, h_=h_, sl_=sl_: e.activation(out=fgt[:, 2 * h_ + 1, :], in_=ptab, func=AF.Exp,
                                                               scale=sl_, bias=-sl_ * 511.0),
                 reads=[("zr", "ptab")], writes=[("fgt", 2 * h_ + 1)])

        iters = []
        groups_at = []
        for h in range(4):
            slope_h = 2.0 ** (-2.0 * (h + 1))
            dmax = 40.0 / slope_h
            for qc in range(4):
                cls_kbs = {"B": [], "D": [], "A": []}
                for kb in range(16):
                    q0, q1, k0, k1 = qc * 512, qc * 512 + 511, kb * 128, kb * 128 + 127
                    mind = max(0, k0 - q1, q0 - k1)
                    if mind <= dmax:
                        cl_ = "D" if h < 2 else ("B" if k1 < q0 else ("A" if k0 > q1 else "D"))
                        cls_kbs[cl_].append(kb)
                order = [cl for cl in ("B", "D", "A") if cls_kbs[cl]]
                for c in range(2):
                    for ci_, cl in enumerate(order):
                        g = len(groups_at)
                        groups_at.append((h, qc, c, cl, ci_ == 0, ci_ == len(order) - 1))
                        kbs = cls_kbs[cl]
                        for n, kb in enumerate(kbs):
                            iters.append((h, qc, c, kb, n == 0, n == len(kbs) - 1, g))
        NI = len(iters)
        SCB = (0, 1, 2, 7)
        ACC = ((3, 4), (5, 6))
        bank_of = {}
        free_sb = list(SCB)

        def take_score_bank():
            return free_sb.pop(0)

        def release_score_bank(b):
            free_sb.append(b)

        def acc_ap(grp, qi):
            bk = ACC[grp % 2][qi // 2]
            return bank(bk)[:, (qi % 2) * 129:(qi % 2) * 129 + 129], bk

        def emit_qk(i):
            h, qc, c, kb, first, last, grp = iters[i]
            sb_ = take_score_bank()
            bank_of[i] = sb_
            S.op("pe", lambda e: e.matmul(
                bank(sb_), lhsT=kT[:, h, kb * 128:(kb + 1) * 128],
                rhs=qz[c][:, h, qc * 512:(qc + 1) * 512], start=True, stop=True),
                 reads=[("kT", h, kb // 4), (f"qz{c}", h, qc), (f"qz{c}", "z")], writes=[psr(sb_)])

        def emit_bias(i):
            h, qc, c, kb, first, last, grp = iters[i]
            if groups_at[grp][3] != "D":
                return
            sb_ = bank_of[i]
            s = i % NSL
            slope = 2.0 ** (-2.0 * (h + 1))
            Dv = qc * 512 - kb * 128 + 1920
            S.op("dve", lambda e: e.scalar_tensor_tensor(out=sc[s], in0=absd[:, Dv:Dv + 512], scalar=-slope,
                                                         in1=bank(sb_), op0=ALU.mult, op1=ALU.add),
                 reads=["absd", psr(sb_)], writes=[f"sc{s}"])
            release_score_bank(sb_)

        def emit_softmax(i):
            h, qc, c, kb, first, last, grp = iters[i]
            cl = groups_at[grp][3]
            sb_ = bank_of[i]
            s = i % NSL
            if cl != "D":
                sg = 0 if cl == "B" else 1
                m_ = (4 * qc - kb) if cl == "B" else (kb - 4 * qc)
                v_ = h * 2 + sg
                S.op("act", lambda e: e.activation(out=ET[s], in_=bank(sb_), func=AF.Exp,
                                                   bias=btab[:, v_, m_:m_ + 1]),
                     reads=[psr(sb_), ("btab", v_)], writes=[f"ET{s}"])
                release_score_bank(sb_)
                return
            S.op("act", lambda e: e.activation(out=ET[s], in_=sc[s], func=AF.Exp),
                 reads=[f"sc{s}"], writes=[f"ET{s}"])

        def emit_pv(i):
            h, qc, c, kb, first, last, grp = iters[i]
            s = i % NSL
            for qi in range(4):
                dst, bk = acc_ap(grp, qi)
                S.op("pe", lambda e, dst=dst, qi=qi: e.matmul(
                    dst, lhsT=ET[s][:, qi * 128:(qi + 1) * 128], rhs=vaug[:, kb, h, 0:129],
                    start=(first and qi % 2 == 0), stop=(last and qi % 2 == 1)),
                     reads=[f"ET{s}", ("vaug", kb)], writes=[psr(bk)])
                if qi == 0:
                    S.op("pe", lambda e, bk=bk: e.matmul(
                        bank(bk)[:, 258:512], lhsT=zeros_bf, rhs=kT[:, 0, 0:254], start=False, stop=False),
                         reads=["zeros_bf", ("kT", 0, 0)], writes=[psr(bk)])

        def finalize_stages(grp):
            h, qc, c, cl, first_cls, last_cls = groups_at[grp]
            nb = numb[c]
            nn = f"numb{c}"

            def s_combine():
                if cl == "D":
                    for j in range(2):
                        bk = ACC[grp % 2][j]
                        src = bank(bk)[:, 0:258]
                        dst = nb[:, 2 * j:2 * j + 2, :].rearrange("p q e -> p (q e)")
                        if first_cls:
                            S.op("dve", lambda e, src=src, dst=dst: e.tensor_copy(out=dst, in_=src),
                                 reads=[psr(bk)], writes=[(nn, 2 * j), (nn, 2 * j + 1)])
                        else:
                            S.op("dve", lambda e, src=src, dst=dst: e.tensor_tensor(out=dst, in0=src, in1=dst, op=ALU.add),
                                 reads=[psr(bk), (nn, 2 * j), (nn, 2 * j + 1)], writes=[(nn, 2 * j), (nn, 2 * j + 1)])
                    return
                fcol = fgt[:, 2 * h + (0 if cl == "B" else 1), :]
                for qi in range(4):
                    a, bk = acc_ap(grp, qi)
                    if first_cls:
                        S.op("dve", lambda e, a=a, qi=qi: e.tensor_scalar(
                            out=nb[:, qi, :], in0=a, scalar1=fcol[:, qi:qi + 1], scalar2=None, op0=ALU.mult),
                             reads=[psr(bk), ("fgt", 2 * h), ("fgt", 2 * h + 1)], writes=[(nn, qi)])
                    else:
                        S.op("dve", lambda e, a=a, qi=qi: e.scalar_tensor_tensor(
                            out=nb[:, qi, :], in0=a, scalar=fcol[:, qi:qi + 1], in1=nb[:, qi, :],
                            op0=ALU.mult, op1=ALU.add),
                             reads=[psr(bk), ("fgt", 2 * h), ("fgt", 2 * h + 1), (nn, qi)], writes=[(nn, qi)])

            if not last_cls:
                return [(0, s_combine)]

            single = first_cls and last_cls
            on_act = h < 2

            def srcv(qi):
                if single:
                    a, bk = acc_ap(grp, qi)
                    return a, psr(bk)
                return nb[:, qi, :], (nn, qi)

            def s_recip():
                for qi in range(4):
                    a, r_ = srcv(qi)
                    S.op("dve", lambda e, qi=qi, a=a: e.reciprocal(out=zr[:, qi:qi + 1], in_=a[:, 128:129]),
                         reads=[r_], writes=[("zr", qi)])

            if c == 0:
                def s_o1():
                    for qi in range(4):
                        a, r_ = srcv(qi)
                        if on_act:
                            S.op("act", lambda e, qi=qi, a=a: e.activation(
                                out=o1[:, qi * 128:(qi + 1) * 128], in_=a[:, 0:128], func=AF.Identity,
                                scale=zr[:, qi:qi + 1]), reads=[r_, ("zr", qi)], writes=[("o1", qi)])
                        else:
                            S.op("dve", lambda e, qi=qi, a=a: e.tensor_scalar(
                                out=o1[:, qi * 128:(qi + 1) * 128], in0=a[:, 0:128], scalar1=zr[:, qi:qi + 1],
                                scalar2=None, op0=ALU.mult),
                                 reads=[r_, ("zr", qi)], writes=[("o1", qi)])
                st_ = [] if single else [(0, s_combine)]
                return st_ + [(1, s_recip), (3 if on_act else 1, s_o1)]

            def s_ob():
                s_recip()
                S.op("dve", lambda e: e.tensor_scalar(out=zr[:, 4:8], in0=zr[:, 0:4], scalar1=neglam, scalar2=None,
                                                      op0=ALU.mult),
                     reads=[("zr", q) for q in range(4)] + [("lam", 4)], writes=[("zr", 4)])
                for qi in range(4):
                    a, r_ = srcv(qi)
                    S.op("dve", lambda e, qi=qi, a=a: e.scalar_tensor_tensor(
                        out=ob[:, qi * 128:(qi + 1) * 128], in0=a[:, 0:128], scalar=zr[:, 4 + qi:5 + qi],
                        in1=o1[:, qi * 128:(qi + 1) * 128], op0=ALU.mult, op1=ALU.add),
                         reads=[r_, ("zr", 4), ("o1", qi)], writes=[("ob", qi)])

            def s_sq():
                for qi in range(4):
                    if on_act:
                        S.op("act", lambda e, qi=qi: e.activation(
                            out=junk[:, 0:128], in_=ob[:, qi * 128:(qi + 1) * 128], func=AF.Square,
                            accum_out=zr[:, 8 + qi:9 + qi]),
                             reads=[("ob", qi)], writes=["junk", ("zr", 8 + qi)])
                        continue
                    S.op("dve", lambda e, qi=qi: e.scalar_tensor_tensor(
                        out=sqj[:, 0:128], in0=ob[:, qi * 128:(qi + 1) * 128], scalar=1.0,
                        in1=ob[:, qi * 128:(qi + 1) * 128], op0=ALU.mult, op1=ALU.mult,
                        accum_out=zr[:, 8 + qi:9 + qi]),
                         reads=[("ob", qi)], writes=["sqj", ("zr", 8 + qi)])

            def s_rstd():
                S.op("act", lambda e: e.activation(out=zr[:, 8:12], in_=zr[:, 8:12], func=AF.Ln,
                                                   scale=1.0 / 128, bias=EPS),
                     reads=[("zr", 8 + q) for q in range(4)], writes=[("zr", 8 + q) for q in range(4)])
                S.op("act", lambda e: e.activation(out=zr[:, 12:16], in_=zr[:, 8:12], func=AF.Exp, scale=-0.5),
                     reads=[("zr", 8 + q) for q in range(4)], writes=[("zr", 13)])

            def s_on():
                for qi in range(4):
                    S.op("dve", lambda e, qi=qi: e.scalar_tensor_tensor(
                        out=on[:, qi * 128:(qi + 1) * 128], in0=ob[:, qi * 128:(qi + 1) * 128],
                        scalar=zr[:, 12 + qi:13 + qi], in1=gsub, op0=ALU.mult, op1=ALU.mult),
                         reads=[("ob", qi), ("zr", 13), "gsub"], writes=[("on", qi)])

            trb = {}

            def s_tr():
                trb["b"] = take_score_bank()
                pv = bankbf(trb["b"])
                for qi in range(4):
                    S.op("pe", lambda e, qi=qi: e.transpose(out=pv[:, qi * 128:(qi + 1) * 128],
                                                            in_=on[:, qi * 128:(qi + 1) * 128], identity=ident),
                         reads=[("on", qi), "ident"], writes=[psr(trb["b"])])

            def s_copy():
                pv = bankbf(trb["b"])
                if on_act:
                    S.op("act", lambda e: e.copy(out=mixA[:, h, qc * 512:(qc + 1) * 512], in_=pv[:, 0:512]),
                         reads=[psr(trb["b"])], writes=[("mixA", h, qc)])
                else:
                    S.op("dve", lambda e: e.tensor_copy(out=mixA[:, h, qc * 512:(qc + 1) * 512], in_=pv[:, 0:512]),
                         reads=[psr(trb["b"])], writes=[("mixA", h, qc)])
                release_score_bank(trb["b"])

            st_ = [] if single else [(0, s_combine)]
            return st_ + [(1, s_ob), (4 if on_act else 2, s_sq), (6, s_rstd), (9, s_on), (11, s_tr), (13, s_copy)]

        LAG = 2
        pending = {}
        LOOK = 4
        nq = {"n": 0}

        def fill_qk(cur):
            while free_sb and nq["n"] < NI and nq["n"] <= cur + LOOK:
                emit_qk(nq["n"])
                emit_bias(nq["n"])
                nq["n"] += 1

        fill_qk(0)
        for i in range(NI):
            emit_softmax(i)
            for fn in pending.pop(i, []):
                fn()
            fill_qk(i)
            emit_pv(i)
            if iters[i][5]:
                for off, fn in finalize_stages(iters[i][6]):
                    pending.setdefault(i + 1 + LAG + off, []).append(fn)
        for k in sorted(pending):
            for fn in pending[k]:
                fn()

        for nm in ("btab", "fgt", "numb0", "numb1", "sqj", "qz0", "qz1", "kT", "vaug", "absd", "sc0", "sc1", "sc2", "sc3", "sc4", "sc5", "ET0", "ET1", "ET2", "ET3", "ET4", "ET5", "o1", "ob", "on"):
            A.free(nm)

        wout = A.alloc("wout", 4096, BF16).rearrange("p (k c) -> p k c", k=8)
        mixL = A.alloc("mixL", 4096, BF16).rearrange("p (k t) -> p k t", k=4)
        for kt in range(8):
            S.dma("pool", wout[:, kt, :], w_out_d[kt * 128:(kt + 1) * 128, :], "w_out", writes=[("wout", kt)])

        xrp = A.alloc("xrp", 2056)
        gg = A.alloc("gg", 2048)
        xcs = [A.alloc("xc0", 2048, top=True), A.alloc("xc1", 2048)]
        xcb = A.alloc("xcb", 1024, BF16)
        TA = [A.alloc(f"TA{i}", 2048) for i in range(2)]
        TB = [A.alloc(f"TB{i}", 2048) for i in range(2)]
        TC0 = A.alloc("TC0", 2048)
        TD = [A.alloc(f"TD{i}", 2048) for i in range(2)]
        S.op("pool", lambda e: e.memset(xrp[:, 0:2], 0.0), writes=[("xrp", "pad0")])
        S.op("pool", lambda e: e.memset(xrp[:, 2050:2056], 0.0), writes=[("xrp", "pad1")])

        def lru_proj(col0, evac):
            for tq in range(4):
                b = next_bank()
                for kt in range(8):
                    S.op("pe", lambda e, b=b, kt=kt, tq=tq: e.matmul(
                        bank(b), lhsT=win_lru[:, kt, col0:col0 + 128], rhs=hT[:, kt, tq * 512:(tq + 1) * 512],
                        start=(kt == 0), stop=(kt == 7)),
                         reads=[("win_lru", kt)] + [("hT", kt, t) for t in range(tq * 4, tq * 4 + 4)],
                         writes=[psr(b)])
                evac(tq, b)

        def lru_A(ct):
            xc = xcs[ct % 2]
            xn = f"xc{ct % 2}"
            lru_proj(ct * 128, lambda tq, b: S.op(
                "act", lambda e: e.copy(out=xrp[:, 2 + tq * 512:2 + (tq + 1) * 512], in_=bank(b)),
                reads=[psr(b)], writes=[("xrp", tq)]))
            xr_all = [("xrp", t) for t in range(4)] + [("xrp", "pad0"), ("xrp", "pad1")]
            S.op("dve", lambda e: e.tensor_scalar(out=xc, in0=xrp[:, 0:2048], scalar1=cols[:, ct * 4:ct * 4 + 1],
                                                  scalar2=cols[:, 16 + ct:17 + ct], op0=ALU.mult, op1=ALU.add),
                 reads=xr_all + ["cols"], writes=[xn])
            for k in range(1, 4):
                S.op("dve", lambda e, k=k: e.scalar_tensor_tensor(
                    out=xc, in0=xrp[:, k:k + 2048], scalar=cols[:, ct * 4 + k:ct * 4 + k + 1], in1=xc,
                    op0=ALU.mult, op1=ALU.add),
                     reads=xr_all + ["cols", xn], writes=[xn])
            S.op("dve", lambda e: e.tensor_copy(out=xcb, in_=xc), reads=[xn], writes=["xcb"])

        def lru_G(ct):
            for d in range(2):
                ci = d * 4 + ct
                for gate, (dstt, dn, bcol) in enumerate(((TA[d], f"TA{d}", 20 + ci), (TB[d], f"TB{d}", 28 + ci))):
                    widx = (gate * 2 + d) * 4 + ct
                    for tq in range(4):
                        b = next_bank()
                        S.op("pe", lambda e, b=b, widx=widx, tq=tq: e.matmul(
                            bank(b), lhsT=wg[:, widx, :], rhs=xcb[:, tq * 512:(tq + 1) * 512], start=True, stop=True),
                             reads=["wg", "xcb"], writes=[psr(b)])
                        S.op("act", lambda e, b=b, dstt=dstt, tq=tq, bcol=bcol: e.activation(
                            out=dstt[:, tq * 512:(tq + 1) * 512], in_=bank(b), func=AF.Sigmoid,
                            bias=cols[:, bcol:bcol + 1]),
                             reads=[psr(b), "cols"], writes=[(dn, tq)])

        def lru_E(ct):
            xc = xcs[ct % 2]
            xn = f"xc{ct % 2}"
            xr_all = [("xrp", t) for t in range(4)]
            abuf = [TC0, xrp[:, 2:2050]]
            ares = [["TC0"], xr_all]
            R = [[(f"TA{d}", t) for t in range(4)] for d in range(2)]
            I = [[(f"TB{d}", t) for t in range(4)] for d in range(2)]
            for d in range(2):
                ci = d * 4 + ct
                S.op("act", lambda e, d=d, ci=ci: e.activation(out=abuf[d], in_=TA[d], func=AF.Exp,
                                                               scale=cA[:, ci:ci + 1]),
                     reads=R[d] + ["cA"], writes=ares[d])
            for d in range(2):
                ci = d * 4 + ct
                if d == 0:
                    S.op("act", lambda e, d=d, ci=ci: e.activation(out=TA[d], in_=TA[d], func=AF.Exp,
                                                                   scale=c2A[:, ci:ci + 1]),
                         reads=R[d] + ["c2A"], writes=R[d])
                else:
                    S.op("dve", lambda e, d=d: e.tensor_tensor(out=TA[d], in0=abuf[d], in1=abuf[d], op=ALU.mult),
                         reads=ares[d], writes=R[d])
                S.op("dve", lambda e, d=d: e.tensor_tensor(out=TB[d], in0=TB[d], in1=xc, op=ALU.mult),
                     reads=I[d] + [xn], writes=I[d])
            for d in range(2):
                S.op("act", lambda e, d=d: e.activation(out=TA[d], in_=TA[d], func=AF.Sqrt, scale=-1.0, bias=1.0),
                     reads=R[d], writes=R[d])
            for d in range(2):
                S.op("dve", lambda e, d=d: e.tensor_tensor(out=TB[d], in0=TB[d], in1=TA[d], op=ALU.mult),
                     reads=I[d] + R[d], writes=I[d])
                if d == 0:
                    S.op("dve", lambda e: e.tensor_tensor_scan(
                        out=TD[0], data0=abuf[0], data1=TB[0], initial=0.0, op0=ALU.mult, op1=ALU.add),
                         reads=ares[0] + I[0], writes=["TD0"])
                else:
                    S.op("dve", lambda e: e.tensor_tensor_scan(
                        out=TD[1][:, ::-1], data0=abuf[1][:, ::-1], data1=TB[1][:, ::-1], initial=0.0,
                        op0=ALU.mult, op1=ALU.add),
                         reads=ares[1] + I[1], writes=["TD1"])

        def lru_C(ct):
            lru_proj(512 + ct * 128, lambda tq, b: S.op(
                "act", lambda e: e.activation(out=gg[:, tq * 512:(tq + 1) * 512], in_=bank(b), func=AF.Gelu_apprx_tanh),
                reads=[psr(b)], writes=[("gg", tq)]))

        def lru_D(ct):
            S.op("dve", lambda e: e.tensor_tensor(out=TD[0], in0=TD[0], in1=TD[1], op=ALU.add),
                 reads=["TD0", "TD1"], writes=["TD0"])
            S.op("dve", lambda e: e.tensor_tensor(out=mixL[:, ct, :], in0=TD[0], in1=gg, op=ALU.mult),
                 reads=["TD0"] + [("gg", t) for t in range(4)], writes=[("mixL", ct, q) for q in range(4)])

        lru_A(0)
        for ct in range(4):
            lru_G(ct)
            if ct + 1 < 4:
                lru_A(ct + 1)
            lru_E(ct)
            lru_C(ct)
            lru_D(ct)
        for nm in ("xrp", "gg", "xc0", "xc1", "xcb", "TA0", "TA1", "TB0", "TB1", "TC0", "TD0", "TD1", "win_lru", "wg"):
            A.free(nm)

        x1t = [None] * NT
        for tt_ in range(NT - 1, -1, -1):
            x1t[tt_] = A.alloc(f"x1t{tt_}", 1024, top=True)
        groups = [(0, 6), (6, 12), (12, 16), (16, 22)]
        wup = [A.alloc(f"wup{i}", 2048, BF16, top=True) for i in range(2)]
        wup3 = [w.rearrange("p (k c) -> p k c", k=8) for w in wup]
        wdn = [A.alloc(f"wdn{i}", 3072, BF16, top=True).rearrange("p (j c) -> p j c", j=6) for i in range(2)]
        w_up_v = w_up_d.rearrange("(k p) c -> p k c", p=128)
        w_dn_v = w_dn_d.rearrange("(j p) c -> p j c", p=128)

        def load_wup(jp):
            sl = jp % 2
            c0 = jp * 256
            for kt in range(8):
                S.dma("pool", wup3[sl][:, kt, 0:256], w_up_d[kt * 128:(kt + 1) * 128, c0:c0 + 256], f"wup{sl}",
                      writes=[(f"wup{sl}", kt, 0)])
                S.dma("pool", wup3[sl][:, kt, 256:512], w_up_d[kt * 128:(kt + 1) * 128, DFF + c0:DFF + c0 + 256],
                      f"wup{sl}", writes=[(f"wup{sl}", kt, 1)])

        def load_wdn(g):
            j0, j1 = groups[g]
            sl = g % 2
            for jj in range(j1 - j0):
                S.dma("pool", wdn[sl][:, jj, :], w_dn_d[(j0 + jj) * 128:(j0 + jj + 1) * 128, :], f"wdn{sl}",
                      writes=[(f"wdn{sl}", jj)])

        load_wup(0)
        load_wup(1)
        load_wdn(0)
        c_order = list(range(NT - 1, -1, -1))
        S.dma("sp", gb, gvec_d[1:2, :].to_broadcast([128, D]), "c_gb", writes=["gb"])
        for tt in c_order:
            S.dma("sp", x1t[tt], x_d[tt * 128:(tt + 1) * 128, :], f"x1_{tt}", writes=[f"x1t{tt}"])
        hnC = [A.alloc(f"hnC{i}", 512, BF16) for i in range(3)]
        def c_mm(tt, cc, b, kts):
            for kt in kts:
                mx, mname = (mixA, "mixA") if kt < 4 else (mixL, "mixL")
                S.op("pe", lambda e, kt=kt, mx=mx: e.matmul(
                    bank(b), lhsT=mx[:, kt % 4, tt * 128:(tt + 1) * 128], rhs=wout[:, kt, cc * 512:(cc + 1) * 512],
                    start=(kt == 0), stop=(kt == 7)),
                     reads=[(mname, kt % 4, tt // 4), ("wout", kt)], writes=[psr(b)])

        def c_add(tt, cc, b):
            S.op("dve", lambda e: e.tensor_tensor(
                out=x1t[tt][:, cc * 512:(cc + 1) * 512], in0=bank(b), in1=x1t[tt][:, cc * 512:(cc + 1) * 512],
                op=ALU.add), reads=[psr(b), f"x1t{tt}"], writes=[f"x1t{tt}"])

        head = c_order[:4]
        hb_ = {}
        for tt in head:
            for cc in range(2):
                hb_[(tt, cc)] = next_bank()
                c_mm(tt, cc, hb_[(tt, cc)], range(7))
        def c_norm_dve(n_):
            tt = c_order[n_]
            sl = n_ % 3
            norm_dve(x1t[tt], [f"x1t{tt}"], hnC[sl], f"hnC{sl}", ss1[:, tt:tt + 1], rstd1[:, tt:tt + 1], ("n2", tt))

        def c_tr(n_):
            tt = c_order[n_]
            sl = n_ % 3
            transpose_part(tt, hT, "hT", hnC[sl], f"hnC{sl}")

        for n_, tt in enumerate(c_order):
            for cc in range(2):
                if tt in head:
                    b = hb_[(tt, cc)]
                    c_mm(tt, cc, b, [7])
                else:
                    b = next_bank()
                    c_mm(tt, cc, b, range(8))
                c_add(tt, cc, b)
            rms_act(x1t[tt], [f"x1t{tt}"], ss1[:, tt:tt + 1], ("n2", tt), 1.0 / D)
            if n_ >= 1:
                c_norm_dve(n_ - 1)
            if n_ >= 2:
                c_tr(n_ - 2)
        c_norm_dve(NT - 1)
        c_tr(NT - 2)
        c_tr(NT - 1)
        if "mixT" in debug_taps:
            taps["mixT"] = nc.dram_tensor("tap_mixT", [128, 8 * S_TOK], F32, kind="ExternalOutput").ap()
            for kt in range(8):
                mx, mname = (mixA, "mixA") if kt < 4 else (mixL, "mixL")
                S.dma("pool", taps["mixT"][:, kt * S_TOK:(kt + 1) * S_TOK], mx[:, kt % 4, :], "out",
                      reads=[(mname, kt % 4, q) for q in range(4)])
        A.free("mixA")
        A.free("mixL")
        for i in range(3):
            A.free(f"hnC{i}")
        A.free("wout")
        if "x1" in debug_taps:
            taps["x1"] = nc.dram_tensor("tap_x1", [128, NT * D], F32, kind="ExternalOutput").ap()
            for t in range(NT):
                S.dma("sp", taps["x1"][:, t * D:(t + 1) * D], x1t[t], "out", reads=[f"x1t{t}"])

        S.dma("sp", gb, gvec_d[2:3, :].to_broadcast([128, D]), "c_gb", writes=["gb"])
        actT = A.alloc("actT", 7168, BF16).rearrange("p (j t) -> p j t", j=7)
        cgb = [A.alloc(f"cg{i}", 2048) for i in range(2)]
        cvb = [A.alloc(f"cv{i}", 2048) for i in range(2)]
        outb = None
        U_g = P[:, 0:2048]
        U_v = P[:, 2048:4096]
        NSLOT = 7

        def up_tile(j):
            jp = j // 2
            sl = jp % 2
            cj = (j % 2) * 128
            bs = j % 2
            cg, cv = cgb[bs], cvb[bs]
            slot = j % NSLOT
            for half, (U, boff, wc0, cbuf, cname, jt) in enumerate((
                    (U_g, 0, cj, cg, f"cg{bs}", j), (U_v, 4, 256 + cj, cv, f"cv{bs}", NJ + j))):
                for tq in range(4):
                    b = boff + tq
                    for kt in range(8):
                        S.op("pe", lambda e, b=b, kt=kt, tq=tq, wc0=wc0: e.matmul(
                            bank(b), lhsT=wup3[sl][:, kt, wc0:wc0 + 128], rhs=hT[:, kt, tq * 512:(tq + 1) * 512],
                            start=(kt == 0), stop=(kt == 7)),
                             reads=[(f"wup{sl}", kt, half)] + [("hT", kt, t) for t in range(tq * 4, tq * 4 + 4)],
                             writes=[psr(b)])
                rb = [psr(boff + t) for t in range(4)]
                w0 = cols[:, 44 + jt * 3:45 + jt * 3]
                w1 = cols[:, 45 + jt * 3:46 + jt * 3]
                w2 = cols[:, 46 + jt * 3:47 + jt * 3]
                bb = cols[:, 176 + jt:177 + jt]
                S.op("act", lambda e, U=U, cbuf=cbuf, w1=w1, bb=bb: e.activation(
                    out=cbuf, in_=U, func=AF.Identity, scale=w1, bias=bb),
                     reads=rb + ["cols"], writes=[cname])
                if half == 1:
                    S.op("act", lambda e: e.activation(out=cg, in_=cg, func=AF.Gelu_apprx_tanh),
                         reads=[f"cg{bs}"], writes=[f"cg{bs}"])
                S.op("dve", lambda e, U=U, cbuf=cbuf, w0=w0: e.scalar_tensor_tensor(
                    out=cbuf[:, 1:2048], in0=U[:, 0:2047], scalar=w0, in1=cbuf[:, 1:2048],
                    op0=ALU.mult, op1=ALU.add), reads=rb + ["cols", cname], writes=[cname])
                S.op("dve", lambda e, U=U, cbuf=cbuf, w2=w2: e.scalar_tensor_tensor(
                    out=cbuf[:, 0:2047], in0=U[:, 1:2048], scalar=w2, in1=cbuf[:, 0:2047],
                    op0=ALU.mult, op1=ALU.add), reads=rb + ["cols", cname], writes=[cname])
            if j % 2 == 1 and jp + 2 < 11:
                load_wup(jp + 2)
            S.op("dve", lambda e: e.tensor_tensor(out=actT[:, slot, :], in0=cg, in1=cv, op=ALU.mult),
                 reads=[f"cg{bs}", f"cv{bs}"], writes=[("actT", slot)])

        def fin_out(tt):
            osl = tt % 4
            ob_ = outb_box[0]
            rms_dve(ss3[:, tt:tt + 1], rstd3[:, tt:tt + 1], ("n3", tt))
            S.op("dve", lambda e: e.scalar_tensor_tensor(
                out=ob_[osl], in0=x1t[tt], scalar=rstd3[:, tt:tt + 1], in1=gb, op0=ALU.mult, op1=ALU.mult),
                 reads=[f"x1t{tt}", (("n3", tt), "rstd"), "gb"], writes=[f"outb{osl}"])
            S.dma("sp", out_d[tt * 128:(tt + 1) * 128, :], ob_[osl], f"out{osl}", reads=[f"outb{osl}"])

        def down_partial(g):
            nonlocal_outb = outb_box
            j0, j1 = groups[g]
            last = (g == len(groups) - 1)
            nj = j1 - j0
            sl = g % 2
            for tt in range(NT):
                for cc in range(2):
                    b = next_bank()
                    for jj in range(nj):
                        slot = (j0 + jj) % NSLOT
                        S.op("pe", lambda e, b=b, jj=jj, tt=tt, cc=cc, slot=slot: e.matmul(
                            bank(b), lhsT=actT[:, slot, tt * 128:(tt + 1) * 128],
                            rhs=wdn[sl][:, jj, cc * 512:(cc + 1) * 512],
                            start=(jj == 0), stop=(jj == nj - 1)),
                             reads=[("actT", slot), (f"wdn{sl}", jj)], writes=[psr(b)])
                    S.op("dve", lambda e, b=b, tt=tt, cc=cc: e.tensor_tensor(
                        out=x1t[tt][:, cc * 512:(cc + 1) * 512], in0=bank(b), in1=x1t[tt][:, cc * 512:(cc + 1) * 512],
                        op=ALU.add), reads=[psr(b), f"x1t{tt}"], writes=[f"x1t{tt}"])
                if last:
                    rms_act(x1t[tt], [f"x1t{tt}"], ss3[:, tt:tt + 1], ("n3", tt), 1.0 / D)
                    if tt >= 1:
                        fin_out(tt - 1)
            if last:
                fin_out(NT - 1)

        outb_box = [None]
        NG = len(groups)
        for j in range(groups[0][0], groups[0][1]):
            up_tile(j)
        for g in range(NG):
            if g + 1 < NG:
                load_wdn(g + 1)
                up_tile(groups[g + 1][0])
            else:
                for nm in ("cg0", "cg1", "cv0", "cv1"):
                    A.free(nm)
                outb_box[0] = [A.alloc(f"outb{i}", 1024) for i in range(4)]
            down_partial(g)
            if g + 1 < NG:
                for j in range(groups[g + 1][0] + 1, groups[g + 1][1]):
                    up_tile(j)
        S.emit(final_waits=[k for k in ("out", "out0", "out1", "out2", "out3") if k in S.dma_sems])
    return nc, S


_CACHE = {}


def _pack_inputs(inp):
    f = lambda a: np.ascontiguousarray(np.asarray(a, dtype=np.float32))
    cols = np.zeros((128, 220), np.float32)
    cw = f(inp["lru_conv_w"])[0]
    cb = f(inp["lru_conv_b"])[0]
    for ct in range(4):
        for k in range(4):
            cols[:, ct * 4 + k] = cw[k, ct * 128:(ct + 1) * 128]
        cols[:, 16 + ct] = cb[ct * 128:(ct + 1) * 128]
    ba = f(inp["lru_b_a"])[0]
    bx = f(inp["lru_b_x"])[0]
    lm = f(inp["lru_lambda"])[0]
    for d in range(2):
        for ct in range(4):
            cols[:, 20 + d * 4 + ct] = ba[d, ct * 128:(ct + 1) * 128]
            cols[:, 28 + d * 4 + ct] = bx[d, ct * 128:(ct + 1) * 128]
            cols[:, 36 + d * 4 + ct] = lm[d, ct * 128:(ct + 1) * 128]
    fw_ = f(inp["ffn_conv_w"])[0]
    fb_ = f(inp["ffn_conv_b"])[0]
    for jt in range(44):
        for k in range(3):
            cols[:, 44 + jt * 3 + k] = fw_[k, jt * 128:(jt + 1) * 128]
        cols[:, 176 + jt] = fb_[jt * 128:(jt + 1) * 128]
    rows = np.concatenate([f(inp["lambda_q1"])[0], f(inp["lambda_k1"])[0], f(inp["lambda_q2"])[0],
                           f(inp["lambda_k2"])[0], f(inp["subln_g"])[0]])[None, :]
    gvec = np.stack([f(inp["attn_norm_g"])[0], f(inp["ffn_norm_g"])[0], f(inp["final_norm_g"])], 0)
    wa = f(inp["lru_w_a"])[0]
    wx = f(inp["lru_w_x"])[0]
    wg = np.zeros((16, 128, 128), np.float32)
    for gate, w in enumerate((wa, wx)):
        for d in range(2):
            for ct in range(4):
                idx = (gate * 2 + d) * 4 + ct
                wg[idx, 0:64, 0:64] = w[d, 2 * ct]
                wg[idx, 64:128, 64:128] = w[d, 2 * ct + 1]
    shared = {
        "w_in": f(inp["w_in"])[0], "w_out": f(inp["w_out"])[0], "w_up": f(inp["w_up"])[0],
        "w_down": f(inp["w_down"])[0], "wg": wg, "cols": cols, "rows": np.ascontiguousarray(rows),
        "gvec": np.ascontiguousarray(gvec),
    }
    return shared


def kernel(**inputs):
    x = np.asarray(inputs["x"], dtype=np.float32)
    B = x.shape[0]
    shared = _pack_inputs(inputs)
    taps = tuple(t for t in os.environ.get("KTAPS", "").split(",") if t)
    key = ("nc", taps)
    if key not in _CACHE:
        _CACHE[key] = build(debug_taps=taps)
    nc, _ = _CACHE[key]
    in_maps = []
    for b in range(B):
        m = dict(shared)
        m["x"] = np.ascontiguousarray(x[b])
        in_maps.append(m)
    res = run_bass_kernel_spmd(nc, in_maps, core_ids=list(range(B)))
    out = np.stack([np.asarray(r["out"], dtype=np.float32) for r in res.results], 0)
    if taps:
        kernel.last_taps = [{k: v for k, v in r.items() if k.startswith("tap_")} for r in res.results]
    return out
```

```python
import os
from contextlib import ExitStack
import numpy as np
import concourse.bass as bass
import concourse.mybir as mybir
from concourse.bass_utils import run_bass_kernel_spmd

F32 = mybir.dt.float32
BF16 = mybir.dt.bfloat16
AF = mybir.ActivationFunctionType
ALU = mybir.AluOpType

ENGS = ("pe", "act", "dve", "pool", "sp")
S_TOK = 2048
D = 1024
NT = 16
DFF = 2816
NJ = 22
EPS = 1e-6
LAMBDA_INIT = 0.8 - 0.6 * 1.0


class Op:
    __slots__ = ("eng", "fn", "deps", "signal", "count", "is_dma", "dma_sem", "dma_key", "dma_target", "name")

    def __init__(self, eng, fn, name=""):
        self.eng = eng
        self.fn = fn
        self.deps = []
        self.signal = False
        self.count = None
        self.is_dma = False
        self.dma_sem = None
        self.dma_key = None
        self.dma_target = 0
        self.name = name


def _buf(r):
    return r[0] if isinstance(r, tuple) else r


class Sched:
    def __init__(self, nc, es):
        self.nc = nc
        self.es = es
        self.per_eng = {e: [] for e in ENGS}
        self.last_writer = {}
        self.readers = {}
        self.eng_sem = {e: es.enter_context(nc.semaphore("prog_" + e)) for e in ENGS if e != "sp"}
        self.dma_sems = {}
        self.dma_counts = {}
        self.buf_deps = {}
        self.nops = 0

    def _add_dep(self, o, d):
        if d is o:
            return
        if d.is_dma:
            o.deps.append((d, 16 * self.dma_counts[d.dma_key]))
            return
        if d.eng == o.eng and o.eng == "pe":
            return
        d.signal = True
        o.deps.append((d, None))

    NON_ARENA = ("ps", "psu", "lam", "zr", "ecol", "pcol", "cA", "c2A", "wupb", "wdnb")

    def _check(self, rs):
        for r in rs:
            b = _buf(r)
            if isinstance(b, tuple) or b in self.NON_ARENA or b in self.buf_deps:
                continue
            raise RuntimeError(f"resource {r!r} is not an arena buffer")

    def op(self, eng, fn, reads=(), writes=(), name=""):
        self._check(reads)
        self._check(writes)
        o = Op(eng, fn, name)
        deps = {}
        for r in reads:
            w = self.last_writer.get(r)
            if w is not None:
                deps[id(w)] = w
        for r in writes:
            w = self.last_writer.get(r)
            if w is not None:
                deps[id(w)] = w
            for rd in self.readers.get(r, {}).values():
                deps[id(rd)] = rd
        for r in list(reads) + list(writes):
            for d in self.buf_deps.get(_buf(r), ()):
                deps[id(d)] = d
        for d in deps.values():
            self._add_dep(o, d)
        for r in reads:
            rd = self.readers.setdefault(r, {})
            key = ("dma", id(o)) if False else eng
            rd[key] = o
        for r in writes:
            self.last_writer[r] = o
            self.readers[r] = {}
        self.per_eng[eng].append(o)
        self.nops += 1
        return o

    def dma(self, queue, out, in_, semkey, reads=(), writes=(), name="", **kw):
        if semkey not in self.dma_sems:
            self.dma_sems[semkey] = self.es.enter_context(self.nc.semaphore("dma_" + semkey))
            self.dma_counts[semkey] = 0

        def fn(e):
            return e.dma_start(out=out, in_=in_, **kw)

        self._check(reads)
        self._check(writes)
        o = Op(queue, fn, name)
        o.is_dma = True
        o.dma_key = semkey
        o.dma_sem = self.dma_sems[semkey]
        deps = {}
        for r in reads:
            w = self.last_writer.get(r)
            if w is not None:
                deps[id(w)] = w
        for r in writes:
            w = self.last_writer.get(r)
            rds = self.readers.get(r, {})
            if w is not None and not (w.is_dma and rds):
                deps[id(w)] = w
            for rd in rds.values():
                deps[id(rd)] = rd
        for r in list(reads) + list(writes):
            for d in self.buf_deps.get(_buf(r), ()):
                deps[id(d)] = d
        for d in deps.values():
            self._add_dep(o, d)
        self.dma_counts[semkey] += 1
        o.dma_target = 16 * self.dma_counts[semkey]
        for r in reads:
            self.readers.setdefault(r, {})[("dma", semkey)] = o
        for r in writes:
            self.last_writer[r] = o
            self.readers[r] = {}
        self.per_eng[queue].append(o)
        return o

    def ops_touching(self, bufname):
        out = {}
        for r, w in self.last_writer.items():
            if _buf(r) == bufname:
                out[id(w)] = w
        for r, rd in self.readers.items():
            if _buf(r) == bufname:
                for o in rd.values():
                    out[id(o)] = o
        return list(out.values())

    def finalize(self):
        for e in ENGS:
            c = 0
            for o in self.per_eng[e]:
                if o.is_dma:
                    continue
                if o.signal:
                    c += 1
                    o.count = c

    def emit_engine(self, ename, e):
        known = {}
        for o in self.per_eng[ename]:
            waits = {}
            for d, ov in o.deps:
                if d.is_dma:
                    key = ("dma", d.dma_key)
                    sem, val = d.dma_sem, ov
                else:
                    key = ("eng", d.eng)
                    sem, val = self.eng_sem[d.eng], d.count
                if known.get(key, 0) >= val:
                    continue
                if key not in waits or waits[key][1] < val:
                    waits[key] = (sem, val)
            for key, (sem, val) in waits.items():
                e.wait_ge(sem, val)
                known[key] = val
            ins = o.fn(e)
            if o.is_dma:
                ins.then_inc(o.dma_sem, 16)
            elif o.signal:
                ins.then_inc(self.eng_sem[ename], 1)

    def emit(self, final_waits=()):
        self.finalize()
        nc = self.nc
        with nc.Block() as block:
            @block.tensor
            def _(e):
                self.emit_engine("pe", e)

            @block.scalar
            def _(e):
                self.emit_engine("act", e)

            @block.vector
            def _(e):
                self.emit_engine("dve", e)

            @block.gpsimd
            def _(e):
                self.emit_engine("pool", e)

            @block.sync
            def _(e):
                self.emit_engine("sp", e)
                for k in final_waits:
                    e.wait_ge(self.dma_sems[k], 16 * self.dma_counts[k])


class Arena:
    def __init__(self, S, tensor, size):
        self.S = S
        self.t = tensor
        self.size = size
        self.live = {}
        self.retired = []

    def alloc(self, name, n, dt=F32, top=False):
        n = (n + 7) // 8 * 8
        spans = sorted(self.live.values())
        gaps = []
        pos = 0
        for (o, m) in spans:
            if o - pos >= n:
                gaps.append((pos, o))
            pos = max(pos, o + m)
        if self.size - pos >= n:
            gaps.append((pos, self.size))
        if not gaps:
            raise RuntimeError(f"arena full allocating {name} ({n}); live={self.live}")
        if top:
            off = gaps[-1][1] - n
        else:
            off = gaps[0][0]
        self.live[name] = (off, n)
        deps = []
        for (o, m, ops) in self.retired:
            if o < off + n and off < o + m:
                deps.extend(ops)
        self.S.buf_deps[name] = deps
        v = self.t[:, off:off + n]
        return v if dt == F32 else v.bitcast(dt)

    def free(self, name):
        off, n = self.live.pop(name)
        self.retired.append((off, n, self.S.ops_touching(name)))


def build(debug_taps=()):
    nc = bass.Bass("TRN2", target_bir_lowering=False)
    dram_in = lambda n, s: nc.dram_tensor(n, list(s), F32, kind="ExternalInput").ap()
    x_d = dram_in("x", [S_TOK, D])
    w_in_d = dram_in("w_in", [D, 2560])
    w_out_d = dram_in("w_out", [D, D])
    w_up_d = dram_in("w_up", [D, 2 * DFF])
    w_dn_d = dram_in("w_down", [DFF, D])
    wg_d = dram_in("wg", [16, 128, 128])
    cols_d = dram_in("cols", [128, 220])
    rows_d = dram_in("rows", [1, 384])
    gvec_d = dram_in("gvec", [3, D])
    out_d = nc.dram_tensor("out", [S_TOK, D], F32, kind="ExternalOutput").ap()
    taps = {}

    es = ExitStack()
    with es:
        S = Sched(nc, es)
        ARENA_N = 53200
        arena_t = es.enter_context(nc.sbuf_tensor("arena", [128, ARENA_N], F32))
        A = Arena(S, arena_t, ARENA_N)
        P = es.enter_context(nc.psum_tensor("P", [128, 4096], F32))

        def bank(b):
            return P[:, b * 512:(b + 1) * 512]

        def bankbf(b):
            return P[:, b * 512:(b + 1) * 512].bitcast(BF16)

        def psr(b):
            return ("ps", b)

        rot = {"i": 0}

        def next_bank(lo=0, hi=8):
            n = hi - lo
            b = lo + rot["i"] % n
            rot["i"] += 1
            return b

        cols = A.alloc("cols", 224)
        rowsb = A.alloc("rowsb", 384)
        gb = A.alloc("gb", 1024)
        ident = A.alloc("ident", 64, BF16)
        zeros_bf = A.alloc("zeros_bf", 64, BF16)
        identf = A.alloc("identf", 128)
        st = A.alloc("stats", 256)
        ss1 = st[:, 0:16]
        rstd1 = st[:, 16:32]
        lam_s = st[:, 32:40]
        cA = st[:, 40:48]
        c2A = st[:, 48:56]
        ecol = st[:, 56:64]
        pcol = st[:, 64:72]
        zr = st[:, 72:88]
        ss3 = st[:, 88:104]
        rstd3 = st[:, 104:120]
        gsub = A.alloc("gsub", 128)
        junk = A.alloc("junk", 512, BF16)

        S.dma("sp", cols[:, 0:220], cols_d, "c_cols", writes=["cols"])
        S.dma("sp", rowsb, rows_d[0:1, :].to_broadcast([128, 384]), "c_rows", writes=["rowsb"])
        S.dma("sp", gb, gvec_d[0:1, :].to_broadcast([128, D]), "c_gb", writes=["gb"])

        S.op("pool", lambda e: e.iota(identf, pattern=[[1, 128]], base=0, channel_multiplier=-1,
                                      allow_small_or_imprecise_dtypes=True), writes=["identf"])
        S.op("dve", lambda e: e.tensor_single_scalar(out=ident, in_=identf, scalar=0.0, op=ALU.is_equal),
             reads=["identf"], writes=["ident"])

        S.op("pool", lambda e: e.memset(zeros_bf, 0.0), writes=["zeros_bf"])
        S.op("dve", lambda e: e.scalar_tensor_tensor(out=junk[:, 0:64], in0=rowsb[:, 0:64], scalar=1.0,
                                                     in1=rowsb[:, 64:128], op0=ALU.mult, op1=ALU.mult,
                                                     accum_out=lam_s[:, 0:1]),
             reads=["rowsb"], writes=["junk", ("lam", 0)])
        S.op("dve", lambda e: e.scalar_tensor_tensor(out=junk[:, 64:128], in0=rowsb[:, 128:192], scalar=1.0,
                                                     in1=rowsb[:, 192:256], op0=ALU.mult, op1=ALU.mult,
                                                     accum_out=lam_s[:, 1:2]),
             reads=["rowsb"], writes=["junk", ("lam", 1)])
        S.op("act", lambda e: e.activation(out=lam_s[:, 2:4], in_=lam_s[:, 0:2], func=AF.Exp),
             reads=[("lam", 0), ("lam", 1)], writes=[("lam", 2)])
        S.op("dve", lambda e: e.tensor_tensor(out=lam_s[:, 4:5], in0=lam_s[:, 3:4], in1=lam_s[:, 2:3],
                                              op=ALU.subtract), reads=[("lam", 2)], writes=[("lam", 4)])
        S.op("dve", lambda e: e.tensor_scalar(out=lam_s[:, 4:5], in0=lam_s[:, 4:5], scalar1=-LAMBDA_INIT,
                                              scalar2=None, op0=ALU.add), reads=[("lam", 4)], writes=[("lam", 4)])
        neglam = lam_s[:, 4:5]
        S.op("dve", lambda e: e.tensor_scalar(out=gsub, in0=rowsb[:, 256:384], scalar1=1.0 - LAMBDA_INIT,
                                              scalar2=None, op0=ALU.mult), reads=["rowsb"], writes=["gsub"])
        S.op("act", lambda e: e.activation(out=ecol, in_=cols[:, 36:44], func=AF.Exp, scale=-1.0),
             reads=["cols"], writes=["ecol"])
        S.op("dve", lambda e: e.tensor_scalar(out=pcol, in0=ecol, scalar1=1.0 / 7.0, scalar2=None, op0=ALU.mult),
             reads=["ecol"], writes=["pcol"])
        for cst in (-1.0 / 6, 1.0 / 5, -1.0 / 4, 1.0 / 3, -1.0 / 2, 1.0):
            S.op("dve", lambda e, cst=cst: e.scalar_tensor_tensor(out=pcol, in0=pcol, scalar=float(cst), in1=ecol,
                                                                  op0=ALU.add, op1=ALU.mult),
                 reads=["pcol", "ecol"], writes=["pcol"])
        S.op("dve", lambda e: e.tensor_scalar(out=cA, in0=pcol, scalar1=-8.0, scalar2=None, op0=ALU.mult),
             reads=["pcol"], writes=["cA"])
        S.op("dve", lambda e: e.tensor_scalar(out=c2A, in0=pcol, scalar1=-16.0, scalar2=None, op0=ALU.mult),
             reads=["pcol"], writes=["c2A"])

        hT = A.alloc("hT", 8192, BF16).rearrange("p (k t) -> p k t", k=8)
        mixA = A.alloc("mixA", 4096, BF16).rearrange("p (k t) -> p k t", k=4)
        win_qkv = A.alloc("win_qkv", 6144, BF16).rearrange("p (k c) -> p k c", k=8)
        win_lru = A.alloc("win_lru", 4096, BF16).rearrange("p (k c) -> p k c", k=8)
        wg = A.alloc("wg", 1024, BF16).rearrange("p (n c) -> p n c", n=16)

        w_in_v = w_in_d.rearrange("(k p) c -> p k c", p=128)
        for kt in range(8):
            S.dma("pool", win_qkv[:, kt, :], w_in_d[kt * 128:(kt + 1) * 128, 0:1536], "w_qkv",
                  writes=[("win_qkv", kt)])

        def rms_rstd(src_ap, src_res, ss_col, rstd_col, tag, inv_n):
            n = src_ap.shape[-1]
            S.op("act", lambda e: e.activation(out=junk[:, 0:n], in_=src_ap, func=AF.Square, accum_out=ss_col),
                 reads=list(src_res), writes=["junk", (tag, "ss")])
            S.op("act", lambda e: e.activation(out=ss_col, in_=ss_col, func=AF.Sqrt, scale=inv_n, bias=EPS),
                 reads=[(tag, "ss")], writes=[(tag, "ss")])
            S.op("dve", lambda e: e.reciprocal(out=rstd_col, in_=ss_col), reads=[(tag, "ss")], writes=[(tag, "rstd")])

        def rms_act(src_ap, src_res, ss_col, tag, inv_n):
            n = src_ap.shape[-1]
            S.op("act", lambda e: e.activation(out=junk[:, 0:n], in_=src_ap, func=AF.Square, accum_out=ss_col),
                 reads=list(src_res), writes=["junk", (tag, "ss")])
            S.op("act", lambda e: e.activation(out=ss_col, in_=ss_col, func=AF.Sqrt, scale=inv_n, bias=EPS),
                 reads=[(tag, "ss")], writes=[(tag, "ss")])

        def rms_dve(ss_col, rstd_col, tag):
            S.op("dve", lambda e: e.reciprocal(out=rstd_col, in_=ss_col), reads=[(tag, "ss")], writes=[(tag, "rstd")])

        def norm_dve(src_ap, src_res, hn_slot, hn_res, ss_col, rstd_col, tag):
            rms_dve(ss_col, rstd_col, tag)
            S.op("dve", lambda e: e.scalar_tensor_tensor(out=hn_slot, in0=src_ap, scalar=rstd_col, in1=gb,
                                                         op0=ALU.mult, op1=ALU.mult),
                 reads=list(src_res) + [(tag, "rstd"), "gb"], writes=[hn_res])

        def norm_part(src_ap, src_res, hn_slot, hn_res, ss_col, rstd_col, tag):
            rms_rstd(src_ap, src_res, ss_col, rstd_col, tag, 1.0 / D)
            S.op("dve", lambda e: e.scalar_tensor_tensor(out=hn_slot, in0=src_ap, scalar=rstd_col, in1=gb,
                                                         op0=ALU.mult, op1=ALU.mult),
                 reads=list(src_res) + [(tag, "rstd"), "gb"], writes=[hn_res])

        def transpose_part(tt, dstT, dst_name, hn_slot, hn_res):
            b = next_bank()
            pv = bankbf(b).rearrange("p (k t) -> p k t", k=8)
            for kt in range(8):
                S.op("pe", lambda e, kt=kt: e.transpose(out=pv[:, kt, :], in_=hn_slot[:, kt * 128:(kt + 1) * 128],
                                                        identity=ident),
                     reads=[hn_res, "ident"], writes=[psr(b)])
            S.op("act", lambda e: e.copy(out=dstT[:, :, tt * 128:(tt + 1) * 128], in_=pv),
                 reads=[psr(b)], writes=[(dst_name, k, tt) for k in range(8)])

        qz = [A.alloc(f"qz{c}", 4096, BF16).rearrange("p (h t) -> p h t", h=4) for c in range(2)]
        kT = A.alloc("kT", 4096, BF16).rearrange("p (h t) -> p h t", h=4)
        vaug = A.alloc("vaug", 4160, BF16).rearrange("p (t h e) -> p t h e", t=16, h=4)
        absd = A.alloc("absd", 3968)
        S.op("pool", lambda e: e.memset(vaug[:, :, :, 128:130], 1.0), writes=[("vaug", "init")])
        S.op("pool", lambda e: e.memset(qz[0][64:128, :, :].rearrange("p h t -> p (h t)"), 0.0), writes=[("qz0", "z")])
        S.op("pool", lambda e: e.memset(qz[1][0:64, :, :].rearrange("p h t -> p (h t)"), 0.0), writes=[("qz1", "z")])

        ev = {"i": 0}

        def unit_qk(h, which, tq):
            col0 = h * 128 if which == "q" else 512 + h * 128
            b = next_bank()
            for kt in range(8):
                S.op("pe", lambda e, kt=kt: e.matmul(
                    bank(b), lhsT=win_qkv[:, kt, col0:col0 + 128], rhs=hT[:, kt, tq * 512:(tq + 1) * 512],
                    start=(kt == 0), stop=(kt == 7)),
                     reads=[("win_qkv", kt)] + [("hT", kt, t) for t in range(tq * 4, tq * 4 + 4)],
                     writes=[psr(b)])
            if which == "q":
                for c in range(2):
                    o_ap = qz[c][c * 64:(c + 1) * 64, h, tq * 512:(tq + 1) * 512]
                    i_ap = bank(b)[c * 64:(c + 1) * 64, :]
                    if (ev["i"] + c) % 2 == 0:
                        S.op("act", lambda e, o_ap=o_ap, i_ap=i_ap: e.activation(
                            out=o_ap, in_=i_ap, func=AF.Identity, scale=0.125),
                             reads=[psr(b), (f"qz{c}", "z")], writes=[(f"qz{c}", h, tq)])
                    else:
                        S.op("dve", lambda e, o_ap=o_ap, i_ap=i_ap: e.tensor_scalar(
                            out=o_ap, in0=i_ap, scalar1=0.125, scalar2=None, op0=ALU.mult),
                             reads=[psr(b), (f"qz{c}", "z")], writes=[(f"qz{c}", h, tq)])
            else:
                o_ap = kT[:, h, tq * 512:(tq + 1) * 512]
                if ev["i"] % 2 == 0:
                    S.op("act", lambda e: e.copy(out=o_ap, in_=bank(b)), reads=[psr(b)], writes=[("kT", h, tq)])
                else:
                    S.op("dve", lambda e: e.tensor_copy(out=o_ap, in_=bank(b)), reads=[psr(b)], writes=[("kT", h, tq)])
            ev["i"] += 1

        def unit_v(tt):
            b = next_bank()
            for kt in range(8):
                S.op("pe", lambda e, kt=kt: e.matmul(
                    bank(b), lhsT=hT[:, kt, tt * 128:(tt + 1) * 128], rhs=win_qkv[:, kt, 1024:1536],
                    start=(kt == 0), stop=(kt == 7)),
                     reads=[("win_qkv", kt), ("hT", kt, tt)], writes=[psr(b)])
            src = bank(b).rearrange("p (h e) -> p h e", h=4)
            if tt % 2 == 0:
                S.op("act", lambda e: e.copy(out=vaug[:, tt, :, 0:128], in_=src),
                     reads=[psr(b), ("vaug", "init")], writes=[("vaug", tt)])
            else:
                S.op("dve", lambda e: e.tensor_copy(out=vaug[:, tt, :, 0:128], in_=src),
                     reads=[psr(b), ("vaug", "init")], writes=[("vaug", tt)])

        NS = 3
        xs = [A.alloc(f"xs{i}", 1024) for i in range(NS)]
        hnA = [A.alloc(f"hnA{i}", 512, BF16) for i in range(NS)]
        ready = []

        def chunk_units(tq):
            u = []
            for h in range(4):
                u.append(lambda h=h: unit_qk(h, "q", tq))
                u.append(lambda h=h: unit_qk(h, "k", tq))
            for t in range(tq * 4, tq * 4 + 4):
                u.append(lambda t=t: unit_v(t))
            return u

        for tt in range(NT + 1):
            if tt < NT:
                sl = tt % NS
                S.dma("sp", xs[sl], x_d[tt * 128:(tt + 1) * 128, :], f"xs{sl}", writes=[f"xs{sl}"])
                norm_part(xs[sl], [f"xs{sl}"], hnA[sl], f"hnA{sl}", ss1[:, tt:tt + 1], rstd1[:, tt:tt + 1], ("n1", tt))
            if tt >= 1:
                pt = tt - 1
                transpose_part(pt, hT, "hT", hnA[pt % NS], f"hnA{pt % NS}")
                if pt % 4 == 3:
                    ready.extend(chunk_units(pt // 4))
            for _ in range(3):
                if ready:
                    ready.pop(0)()
        while ready:
            ready.pop(0)()
        for i in range(NS):
            A.free(f"xs{i}")
            A.free(f"hnA{i}")
        A.free("win_qkv")

        for kt in range(8):
            S.dma("pool", win_lru[:, kt, :], w_in_d[kt * 128:(kt + 1) * 128, 1536:2560], "w_lru",
                  writes=[("win_lru", kt)])
        S.dma("pool", wg, wg_d.rearrange("n p c -> p n c"), "w_g", writes=["wg"])

        S.op("pool", lambda e: e.iota(absd, pattern=[[1, 3968]], base=-1920, channel_multiplier=-1,
                                      allow_small_or_imprecise_dtypes=True), writes=["absd"])
        S.op("act", lambda e: e.activation(out=absd, in_=absd, func=AF.Abs), reads=["absd"], writes=["absd"])
        NSL = 6
        sc = [A.alloc(f"sc{i}", 512) for i in range(NSL)]
        ET = [A.alloc(f"ET{i}", 256, BF16) for i in range(NSL)]
        o1 = A.alloc("o1", 512)
        ob = A.alloc("ob", 512)
        on = A.alloc("on", 256, BF16)

        btab = A.alloc("btab", 128).rearrange("p (v m) -> p v m", v=8)
        fgt = A.alloc("fgt", 32).rearrange("p (v q) -> p v q", v=8)
        numb = [A.alloc(f"numb{c}", 520)[:, 0:516].rearrange("p (q e) -> p q e", q=4) for c in range(2)]
        sqj = A.alloc("sqj", 128)
        klf = st[:, 120:121]
        cmf = st[:, 128:144]
        ptab = st[:, 144:148]
        S.op("pool", lambda e: e.iota(klf, pattern=[[0, 1]], base=0, channel_multiplier=1,
                                      allow_small_or_imprecise_dtypes=True), writes=[("zr", "klf")])
        S.op("pool", lambda e: e.iota(cmf, pattern=[[1, 16]], base=0, channel_multiplier=0,
                                      allow_small_or_imprecise_dtypes=True), writes=[("zr", "cmf")])
        S.op("pool", lambda e: e.iota(ptab, pattern=[[128, 4]], base=0, channel_multiplier=1,
                                      allow_small_or_imprecise_dtypes=True), writes=[("zr", "ptab")])
        for h_ in range(4):
            sl_ = 2.0 ** (-2.0 * (h_ + 1))
            for sg in range(2):
                v_ = h_ * 2 + sg
                ksign = sl_ if sg == 0 else -sl_
                cadd = 0.0 if sg == 0 else sl_ * 511.0
                S.op("dve", lambda e, v_=v_, sl_=sl_, cadd=cadd: e.tensor_scalar(
                    out=btab[:, v_, :], in0=cmf, scalar1=-sl_ * 128.0, scalar2=cadd, op0=ALU.mult, op1=ALU.add),
                     reads=[("zr", "cmf")], writes=[("btab", v_)])
                S.op("dve", lambda e, v_=v_, ksign=ksign: e.scalar_tensor_tensor(
                    out=btab[:, v_, :], in0=klf.to_broadcast([128, 16]), scalar=ksign, in1=btab[:, v_, :],
                    op0=ALU.mult, op1=ALU.add), reads=[("zr", "klf"), ("btab", v_)], writes=[("btab", v_)])
            S.op("act", lambda e, h_=h_, sl_=sl_: e.activation(out=fgt[:, 2 * h_, :], in_=ptab, func=AF.Exp, scale=-sl_),
                 reads=[("zr", "ptab")], writes=[("fgt", 2 * h_)])
            S.op("act", lambda e, h_=h_, sl_=sl_: e.activation(out=fgt[:, 2 * h_ + 1, :], in_=ptab, func=AF.Exp,
                                                               scale=sl_, bias=-sl_ * 511.0),
                 reads=[("zr", "ptab")], writes=[("fgt", 2 * h_ + 1)])

        iters = []
        groups_at = []
        for h in range(4):
            slope_h = 2.0 ** (-2.0 * (h + 1))
            dmax = 40.0 / slope_h
            for qc in range(4):
                cls_kbs = {"B": [], "D": [], "A": []}
                for kb in range(16):
                    q0, q1, k0, k1 = qc * 512, qc * 512 + 511, kb * 128, kb * 128 + 127
                    mind = max(0, k0 - q1, q0 - k1)
                    if mind <= dmax:
                        cl_ = "D" if h < 2 else ("B" if k1 < q0 else ("A" if k0 > q1 else "D"))
                        cls_kbs[cl_].append(kb)
                order = [cl for cl in ("B", "D", "A") if cls_kbs[cl]]
                for c in range(2):
                    for ci_, cl in enumerate(order):
                        g = len(groups_at)
                        groups_at.append((h, qc, c, cl, ci_ == 0, ci_ == len(order) - 1))
                        kbs = cls_kbs[cl]
                        for n, kb in enumerate(kbs):
                            iters.append((h, qc, c, kb, n == 0, n == len(kbs) - 1, g))
        NI = len(iters)
        SCB = (0, 1, 2, 7)
        ACC = ((3, 4), (5, 6))
        bank_of = {}
        free_sb = list(SCB)

        def take_score_bank():
            return free_sb.pop(0)

        def release_score_bank(b):
            free_sb.append(b)

        def acc_ap(grp, qi):
            bk = ACC[grp % 2][qi // 2]
            return bank(bk)[:, (qi % 2) * 129:(qi % 2) * 129 + 129], bk

        def emit_qk(i):
            h, qc, c, kb, first, last, grp = iters[i]
            sb_ = take_score_bank()
            bank_of[i] = sb_
            S.op("pe", lambda e: e.matmul(
                bank(sb_), lhsT=kT[:, h, kb * 128:(kb + 1) * 128],
                rhs=qz[c][:, h, qc * 512:(qc + 1) * 512], start=True, stop=True),
                 reads=[("kT", h, kb // 4), (f"qz{c}", h, qc), (f"qz{c}", "z")], writes=[psr(sb_)])

        def emit_bias(i):
            h, qc, c, kb, first, last, grp = iters[i]
            if groups_at[grp][3] != "D":
                return
            sb_ = bank_of[i]
            s = i % NSL
            slope = 2.0 ** (-2.0 * (h + 1))
            Dv = qc * 512 - kb * 128 + 1920
            S.op("dve", lambda e: e.scalar_tensor_tensor(out=sc[s], in0=absd[:, Dv:Dv + 512], scalar=-slope,
                                                         in1=bank(sb_), op0=ALU.mult, op1=ALU.add),
                 reads=["absd", psr(sb_)], writes=[f"sc{s}"])
            release_score_bank(sb_)

        def emit_softmax(i):
            h, qc, c, kb, first, last, grp = iters[i]
            cl = groups_at[grp][3]
            sb_ = bank_of[i]
            s = i % NSL
            if cl != "D":
                sg = 0 if cl == "B" else 1
                m_ = (4 * qc - kb) if cl == "B" else (kb - 4 * qc)
                v_ = h * 2 + sg
                S.op("act", lambda e: e.activation(out=ET[s], in_=bank(sb_), func=AF.Exp,
                                                   bias=btab[:, v_, m_:m_ + 1]),
                     reads=[psr(sb_), ("btab", v_)], writes=[f"ET{s}"])
                release_score_bank(sb_)
                return
            S.op("act", lambda e: e.activation(out=ET[s], in_=sc[s], func=AF.Exp),
                 reads=[f"sc{s}"], writes=[f"ET{s}"])

        def emit_pv(i):
            h, qc, c, kb, first, last, grp = iters[i]
            s = i % NSL
            for qi in range(4):
                dst, bk = acc_ap(grp, qi)
                S.op("pe", lambda e, dst=dst, qi=qi: e.matmul(
                    dst, lhsT=ET[s][:, qi * 128:(qi + 1) * 128], rhs=vaug[:, kb, h, 0:129],
                    start=(first and qi % 2 == 0), stop=(last and qi % 2 == 1)),
                     reads=[f"ET{s}", ("vaug", kb)], writes=[psr(bk)])
                if qi == 0:
                    S.op("pe", lambda e, bk=bk: e.matmul(
                        bank(bk)[:, 258:512], lhsT=zeros_bf, rhs=kT[:, 0, 0:254], start=False, stop=False),
                         reads=["zeros_bf", ("kT", 0, 0)], writes=[psr(bk)])

        def finalize_stages(grp):
            h, qc, c, cl, first_cls, last_cls = groups_at[grp]
            nb = numb[c]
            nn = f"numb{c}"

            def s_combine():
                if cl == "D":
                    for j in range(2):
                        bk = ACC[grp % 2][j]
                        src = bank(bk)[:, 0:258]
                        dst = nb[:, 2 * j:2 * j + 2, :].rearrange("p q e -> p (q e)")
                        if first_cls:
                            S.op("dve", lambda e, src=src, dst=dst: e.tensor_copy(out=dst, in_=src),
                                 reads=[psr(bk)], writes=[(nn, 2 * j), (nn, 2 * j + 1)])
                        else:
                            S.op("dve", lambda e, src=src, dst=dst: e.tensor_tensor(out=dst, in0=src, in1=dst, op=ALU.add),
                                 reads=[psr(bk), (nn, 2 * j), (nn, 2 * j + 1)], writes=[(nn, 2 * j), (nn, 2 * j + 1)])
                    return
                fcol = fgt[:, 2 * h + (0 if cl == "B" else 1), :]
                for qi in range(4):
                    a, bk = acc_ap(grp, qi)
                    if first_cls:
                        S.op("dve", lambda e, a=a, qi=qi: e.tensor_scalar(
                            out=nb[:, qi, :], in0=a, scalar1=fcol[:, qi:qi + 1], scalar2=None, op0=ALU.mult),
                             reads=[psr(bk), ("fgt", 2 * h), ("fgt", 2 * h + 1)], writes=[(nn, qi)])
                    else:
                        S.op("dve", lambda e, a=a, qi=qi: e.scalar_tensor_tensor(
                            out=nb[:, qi, :], in0=a, scalar=fcol[:, qi:qi + 1], in1=nb[:, qi, :],
                            op0=ALU.mult, op1=ALU.add),
                             reads=[psr(bk), ("fgt", 2 * h), ("fgt", 2 * h + 1), (nn, qi)], writes=[(nn, qi)])

            if not last_cls:
                return [(0, s_combine)]

            single = first_cls and last_cls
            on_act = h < 2

            def srcv(qi):
                if single:
                    a, bk = acc_ap(grp, qi)
                    return a, psr(bk)
                return nb[:, qi, :], (nn, qi)

            def s_recip():
                for qi in range(4):
                    a, r_ = srcv(qi)
                    S.op("dve", lambda e, qi=qi, a=a: e.reciprocal(out=zr[:, qi:qi + 1], in_=a[:, 128:129]),
                         reads=[r_], writes=[("zr", qi)])

            if c == 0:
                def s_o1():
                    for qi in range(4):
                        a, r_ = srcv(qi)
                        if on_act:
                            S.op("act", lambda e, qi=qi, a=a: e.activation(
                                out=o1[:, qi * 128:(qi + 1) * 128], in_=a[:, 0:128], func=AF.Identity,
                                scale=zr[:, qi:qi + 1]), reads=[r_, ("zr", qi)], writes=[("o1", qi)])
                        else:
                            S.op("dve", lambda e, qi=qi, a=a: e.tensor_scalar(
                                out=o1[:, qi * 128:(qi + 1) * 128], in0=a[:, 0:128], scalar1=zr[:, qi:qi + 1],
                                scalar2=None, op0=ALU.mult),
                                 reads=[r_, ("zr", qi)], writes=[("o1", qi)])
                st_ = [] if single else [(0, s_combine)]
                return st_ + [(1, s_recip), (2 if on_act else 1, s_o1)]

            def s_ob():
                s_recip()
                S.op("dve", lambda e: e.tensor_scalar(out=zr[:, 4:8], in0=zr[:, 0:4], scalar1=neglam, scalar2=None,
                                                      op0=ALU.mult),
                     reads=[("zr", q) for q in range(4)] + [("lam", 4)], writes=[("zr", 4)])
                for qi in range(4):
                    a, r_ = srcv(qi)
                    S.op("dve", lambda e, qi=qi, a=a: e.scalar_tensor_tensor(
                        out=ob[:, qi * 128:(qi + 1) * 128], in0=a[:, 0:128], scalar=zr[:, 4 + qi:5 + qi],
                        in1=o1[:, qi * 128:(qi + 1) * 128], op0=ALU.mult, op1=ALU.add),
                         reads=[r_, ("zr", 4), ("o1", qi)], writes=[("ob", qi)])

            def s_sq():
                for qi in range(4):
                    if on_act:
                        S.op("act", lambda e, qi=qi: e.activation(
                            out=junk[:, 0:128], in_=ob[:, qi * 128:(qi + 1) * 128], func=AF.Square,
                            accum_out=zr[:, 8 + qi:9 + qi]),
                             reads=[("ob", qi)], writes=["junk", ("zr", 8 + qi)])
                        continue
                    S.op("dve", lambda e, qi=qi: e.scalar_tensor_tensor(
                        out=sqj[:, 0:128], in0=ob[:, qi * 128:(qi + 1) * 128], scalar=1.0,
                        in1=ob[:, qi * 128:(qi + 1) * 128], op0=ALU.mult, op1=ALU.mult,
                        accum_out=zr[:, 8 + qi:9 + qi]),
                         reads=[("ob", qi)], writes=["sqj", ("zr", 8 + qi)])

            def s_rstd():
                S.op("act", lambda e: e.activation(out=zr[:, 8:12], in_=zr[:, 8:12], func=AF.Ln,
                                                   scale=1.0 / 128, bias=EPS),
                     reads=[("zr", 8 + q) for q in range(4)], writes=[("zr", 8 + q) for q in range(4)])
                S.op("act", lambda e: e.activation(out=zr[:, 12:16], in_=zr[:, 8:12], func=AF.Exp, scale=-0.5),
                     reads=[("zr", 8 + q) for q in range(4)], writes=[("zr", 13)])

            def s_on():
                for qi in range(4):
                    S.op("dve", lambda e, qi=qi: e.scalar_tensor_tensor(
                        out=on[:, qi * 128:(qi + 1) * 128], in0=ob[:, qi * 128:(qi + 1) * 128],
                        scalar=zr[:, 12 + qi:13 + qi], in1=gsub, op0=ALU.mult, op1=ALU.mult),
                         reads=[("ob", qi), ("zr", 13), "gsub"], writes=[("on", qi)])

            trb = {}

            def s_tr():
                trb["b"] = take_score_bank()
                pv = bankbf(trb["b"])
                for qi in range(4):
                    S.op("pe", lambda e, qi=qi: e.transpose(out=pv[:, qi * 128:(qi + 1) * 128],
                                                            in_=on[:, qi * 128:(qi + 1) * 128], identity=ident),
                         reads=[("on", qi), "ident"], writes=[psr(trb["b"])])

            def s_copy():
                pv = bankbf(trb["b"])
                if on_act:
                    S.op("act", lambda e: e.copy(out=mixA[:, h, qc * 512:(qc + 1) * 512], in_=pv[:, 0:512]),
                         reads=[psr(trb["b"])], writes=[("mixA", h, qc)])
                else:
                    S.op("dve", lambda e: e.tensor_copy(out=mixA[:, h, qc * 512:(qc + 1) * 512], in_=pv[:, 0:512]),
                         reads=[psr(trb["b"])], writes=[("mixA", h, qc)])
                release_score_bank(trb["b"])

            st_ = [] if single else [(0, s_combine)]
            return st_ + [(1, s_ob), (4 if on_act else 2, s_sq), (6, s_rstd), (9, s_on), (11, s_tr), (13, s_copy)]

        LAG = 1
        pending = {}
        LOOK = 4
        nq = {"n": 0}

        def fill_qk(cur):
            while free_sb and nq["n"] < NI and nq["n"] <= cur + LOOK:
                emit_qk(nq["n"])
                emit_bias(nq["n"])
                nq["n"] += 1

        fill_qk(0)
        for i in range(NI):
            emit_softmax(i)
            for fn in pending.pop(i, []):
                fn()
            fill_qk(i)
            emit_pv(i)
            if iters[i][5]:
                for off, fn in finalize_stages(iters[i][6]):
                    pending.setdefault(i + 1 + LAG + off, []).append(fn)
        for k in sorted(pending):
            for fn in pending[k]:
                fn()

        for nm in ("btab", "fgt", "numb0", "numb1", "sqj", "qz0", "qz1", "kT", "vaug", "absd", "sc0", "sc1", "sc2", "sc3", "sc4", "sc5", "ET0", "ET1", "ET2", "ET3", "ET4", "ET5", "o1", "ob", "on"):
            A.free(nm)

        wout = A.alloc("wout", 4096, BF16).rearrange("p (k c) -> p k c", k=8)
        mixL = A.alloc("mixL", 4096, BF16).rearrange("p (k t) -> p k t", k=4)
        for kt in range(8):
            S.dma("pool", wout[:, kt, :], w_out_d[kt * 128:(kt + 1) * 128, :], "w_out", writes=[("wout", kt)])

        xrp = A.alloc("xrp", 2056)
        gg = A.alloc("gg", 2048)
        xcs = [A.alloc("xc0", 2048, top=True), A.alloc("xc1", 2048)]
        xcb = A.alloc("xcb", 1024, BF16)
        TA = [A.alloc(f"TA{i}", 2048) for i in range(2)]
        TB = [A.alloc(f"TB{i}", 2048) for i in range(2)]
        TC0 = A.alloc("TC0", 2048)
        TD = [A.alloc(f"TD{i}", 2048) for i in range(2)]
        S.op("pool", lambda e: e.memset(xrp[:, 0:2], 0.0), writes=[("xrp", "pad0")])
        S.op("pool", lambda e: e.memset(xrp[:, 2050:2056], 0.0), writes=[("xrp", "pad1")])

        def lru_proj(col0, evac):
            for tq in range(4):
                b = next_bank()
                for kt in range(8):
                    S.op("pe", lambda e, b=b, kt=kt, tq=tq: e.matmul(
                        bank(b), lhsT=win_lru[:, kt, col0:col0 + 128], rhs=hT[:, kt, tq * 512:(tq + 1) * 512],
                        start=(kt == 0), stop=(kt == 7)),
                         reads=[("win_lru", kt)] + [("hT", kt, t) for t in range(tq * 4, tq * 4 + 4)],
                         writes=[psr(b)])
                evac(tq, b)

        def lru_A(ct):
            xc = xcs[ct % 2]
            xn = f"xc{ct % 2}"
            lru_proj(ct * 128, lambda tq, b: S.op(
                "act", lambda e: e.copy(out=xrp[:, 2 + tq * 512:2 + (tq + 1) * 512], in_=bank(b)),
                reads=[psr(b)], writes=[("xrp", tq)]))
            xr_all = [("xrp", t) for t in range(4)] + [("xrp", "pad0"), ("xrp", "pad1")]
            S.op("dve", lambda e: e.tensor_scalar(out=xc, in0=xrp[:, 0:2048], scalar1=cols[:, ct * 4:ct * 4 + 1],
                                                  scalar2=cols[:, 16 + ct:17 + ct], op0=ALU.mult, op1=ALU.add),
                 reads=xr_all + ["cols"], writes=[xn])
            for k in range(1, 4):
                S.op("dve", lambda e, k=k: e.scalar_tensor_tensor(
                    out=xc, in0=xrp[:, k:k + 2048], scalar=cols[:, ct * 4 + k:ct * 4 + k + 1], in1=xc,
                    op0=ALU.mult, op1=ALU.add),
                     reads=xr_all + ["cols", xn], writes=[xn])
            S.op("dve", lambda e: e.tensor_copy(out=xcb, in_=xc), reads=[xn], writes=["xcb"])

        def lru_G(ct):
            for d in range(2):
                ci = d * 4 + ct
                for gate, (dstt, dn, bcol) in enumerate(((TA[d], f"TA{d}", 20 + ci), (TB[d], f"TB{d}", 28 + ci))):
                    widx = (gate * 2 + d) * 4 + ct
                    for tq in range(4):
                        b = next_bank()
                        S.op("pe", lambda e, b=b, widx=widx, tq=tq: e.matmul(
                            bank(b), lhsT=wg[:, widx, :], rhs=xcb[:, tq * 512:(tq + 1) * 512], start=True, stop=True),
                             reads=["wg", "xcb"], writes=[psr(b)])
                        S.op("act", lambda e, b=b, dstt=dstt, tq=tq, bcol=bcol: e.activation(
                            out=dstt[:, tq * 512:(tq + 1) * 512], in_=bank(b), func=AF.Sigmoid,
                            bias=cols[:, bcol:bcol + 1]),
                             reads=[psr(b), "cols"], writes=[(dn, tq)])

        def lru_E(ct):
            xc = xcs[ct % 2]
            xn = f"xc{ct % 2}"
            xr_all = [("xrp", t) for t in range(4)]
            abuf = [TC0, xrp[:, 2:2050]]
            ares = [["TC0"], xr_all]
            R = [[(f"TA{d}", t) for t in range(4)] for d in range(2)]
            I = [[(f"TB{d}", t) for t in range(4)] for d in range(2)]
            for d in range(2):
                ci = d * 4 + ct
                S.op("act", lambda e, d=d, ci=ci: e.activation(out=abuf[d], in_=TA[d], func=AF.Exp,
                                                               scale=cA[:, ci:ci + 1]),
                     reads=R[d] + ["cA"], writes=ares[d])
            for d in range(2):
                ci = d * 4 + ct
                if d == 0:
                    S.op("act", lambda e, d=d, ci=ci: e.activation(out=TA[d], in_=TA[d], func=AF.Exp,
                                                                   scale=c2A[:, ci:ci + 1]),
                         reads=R[d] + ["c2A"], writes=R[d])
                else:
                    S.op("dve", lambda e, d=d: e.tensor_tensor(out=TA[d], in0=abuf[d], in1=abuf[d], op=ALU.mult),
                         reads=ares[d], writes=R[d])
                S.op("dve", lambda e, d=d: e.tensor_tensor(out=TB[d], in0=TB[d], in1=xc, op=ALU.mult),
                     reads=I[d] + [xn], writes=I[d])
            for d in range(2):
                S.op("act", lambda e, d=d: e.activation(out=TA[d], in_=TA[d], func=AF.Sqrt, scale=-1.0, bias=1.0),
                     reads=R[d], writes=R[d])
            for d in range(2):
                S.op("dve", lambda e, d=d: e.tensor_tensor(out=TB[d], in0=TB[d], in1=TA[d], op=ALU.mult),
                     reads=I[d] + R[d], writes=I[d])
                if d == 0:
                    S.op("dve", lambda e: e.tensor_tensor_scan(
                        out=TD[0], data0=abuf[0], data1=TB[0], initial=0.0, op0=ALU.mult, op1=ALU.add),
                         reads=ares[0] + I[0], writes=["TD0"])
                else:
                    S.op("dve", lambda e: e.tensor_tensor_scan(
                        out=TD[1][:, ::-1], data0=abuf[1][:, ::-1], data1=TB[1][:, ::-1], initial=0.0,
                        op0=ALU.mult, op1=ALU.add),
                         reads=ares[1] + I[1], writes=["TD1"])

        def lru_C(ct):
            lru_proj(512 + ct * 128, lambda tq, b: S.op(
                "act", lambda e: e.activation(out=gg[:, tq * 512:(tq + 1) * 512], in_=bank(b), func=AF.Gelu_apprx_tanh),
                reads=[psr(b)], writes=[("gg", tq)]))

        def lru_D(ct):
            S.op("dve", lambda e: e.tensor_tensor(out=TD[0], in0=TD[0], in1=TD[1], op=ALU.add),
                 reads=["TD0", "TD1"], writes=["TD0"])
            S.op("dve", lambda e: e.tensor_tensor(out=mixL[:, ct, :], in0=TD[0], in1=gg, op=ALU.mult),
                 reads=["TD0"] + [("gg", t) for t in range(4)], writes=[("mixL", ct, q) for q in range(4)])

        lru_A(0)
        for ct in range(4):
            lru_G(ct)
            if ct + 1 < 4:
                lru_A(ct + 1)
            lru_E(ct)
            lru_C(ct)
            lru_D(ct)
        for nm in ("xrp", "gg", "xc0", "xc1", "xcb", "TA0", "TA1", "TB0", "TB1", "TC0", "TD0", "TD1", "win_lru", "wg"):
            A.free(nm)

        x1t = [None] * NT
        for tt_ in range(NT - 1, -1, -1):
            x1t[tt_] = A.alloc(f"x1t{tt_}", 1024, top=True)
        groups = [(0, 6), (6, 12), (12, 16), (16, 22)]
        wup = [A.alloc(f"wup{i}", 2048, BF16, top=True) for i in range(2)]
        wup3 = [w.rearrange("p (k c) -> p k c", k=8) for w in wup]
        wdn = [A.alloc(f"wdn{i}", 3072, BF16, top=True).rearrange("p (j c) -> p j c", j=6) for i in range(2)]
        w_up_v = w_up_d.rearrange("(k p) c -> p k c", p=128)
        w_dn_v = w_dn_d.rearrange("(j p) c -> p j c", p=128)

        def load_wup(jp):
            sl = jp % 2
            c0 = jp * 256
            for kt in range(8):
                S.dma("pool", wup3[sl][:, kt, 0:256], w_up_d[kt * 128:(kt + 1) * 128, c0:c0 + 256], f"wup{sl}",
                      writes=[(f"wup{sl}", kt, 0)])
                S.dma("pool", wup3[sl][:, kt, 256:512], w_up_d[kt * 128:(kt + 1) * 128, DFF + c0:DFF + c0 + 256],
                      f"wup{sl}", writes=[(f"wup{sl}", kt, 1)])

        def load_wdn(g):
            j0, j1 = groups[g]
            sl = g % 2
            for jj in range(j1 - j0):
                S.dma("pool", wdn[sl][:, jj, :], w_dn_d[(j0 + jj) * 128:(j0 + jj + 1) * 128, :], f"wdn{sl}",
                      writes=[(f"wdn{sl}", jj)])

        load_wup(0)
        load_wup(1)
        load_wdn(0)
        c_order = list(range(NT - 1, -1, -1))
        S.dma("sp", gb, gvec_d[1:2, :].to_broadcast([128, D]), "c_gb", writes=["gb"])
        for tt in c_order:
            S.dma("sp", x1t[tt], x_d[tt * 128:(tt + 1) * 128, :], f"x1_{tt}", writes=[f"x1t{tt}"])
        hnC = [A.alloc(f"hnC{i}", 512, BF16) for i in range(3)]
        def c_mm(tt, cc, b, kts):
            for kt in kts:
                mx, mname = (mixA, "mixA") if kt < 4 else (mixL, "mixL")
                S.op("pe", lambda e, kt=kt, mx=mx: e.matmul(
                    bank(b), lhsT=mx[:, kt % 4, tt * 128:(tt + 1) * 128], rhs=wout[:, kt, cc * 512:(cc + 1) * 512],
                    start=(kt == 0), stop=(kt == 7)),
                     reads=[(mname, kt % 4, tt // 4), ("wout", kt)], writes=[psr(b)])

        def c_add(tt, cc, b):
            S.op("dve", lambda e: e.tensor_tensor(
                out=x1t[tt][:, cc * 512:(cc + 1) * 512], in0=bank(b), in1=x1t[tt][:, cc * 512:(cc + 1) * 512],
                op=ALU.add), reads=[psr(b), f"x1t{tt}"], writes=[f"x1t{tt}"])

        head = c_order[:4]
        hb_ = {}
        for tt in head:
            for cc in range(2):
                hb_[(tt, cc)] = next_bank()
                c_mm(tt, cc, hb_[(tt, cc)], range(7))
        def c_norm_dve(n_):
            tt = c_order[n_]
            sl = n_ % 3
            norm_dve(x1t[tt], [f"x1t{tt}"], hnC[sl], f"hnC{sl}", ss1[:, tt:tt + 1], rstd1[:, tt:tt + 1], ("n2", tt))

        def c_tr(n_):
            tt = c_order[n_]
            sl = n_ % 3
            transpose_part(tt, hT, "hT", hnC[sl], f"hnC{sl}")

        for n_, tt in enumerate(c_order):
            for cc in range(2):
                if tt in head:
                    b = hb_[(tt, cc)]
                    c_mm(tt, cc, b, [7])
                else:
                    b = next_bank()
                    c_mm(tt, cc, b, range(8))
                c_add(tt, cc, b)
            rms_act(x1t[tt], [f"x1t{tt}"], ss1[:, tt:tt + 1], ("n2", tt), 1.0 / D)
            if n_ >= 1:
                c_norm_dve(n_ - 1)
            if n_ >= 2:
                c_tr(n_ - 2)
        c_norm_dve(NT - 1)
        c_tr(NT - 2)
        c_tr(NT - 1)
        if "mixT" in debug_taps:
            taps["mixT"] = nc.dram_tensor("tap_mixT", [128, 8 * S_TOK], F32, kind="ExternalOutput").ap()
            for kt in range(8):
                mx, mname = (mixA, "mixA") if kt < 4 else (mixL, "mixL")
                S.dma("pool", taps["mixT"][:, kt * S_TOK:(kt + 1) * S_TOK], mx[:, kt % 4, :], "out",
                      reads=[(mname, kt % 4, q) for q in range(4)])
        A.free("mixA")
        A.free("mixL")
        for i in range(3):
            A.free(f"hnC{i}")
        A.free("wout")
        if "x1" in debug_taps:
            taps["x1"] = nc.dram_tensor("tap_x1", [128, NT * D], F32, kind="ExternalOutput").ap()
            for t in range(NT):
                S.dma("sp", taps["x1"][:, t * D:(t + 1) * D], x1t[t], "out", reads=[f"x1t{t}"])

        S.dma("sp", gb, gvec_d[2:3, :].to_broadcast([128, D]), "c_gb", writes=["gb"])
        actT = A.alloc("actT", 7168, BF16).rearrange("p (j t) -> p j t", j=7)
        cgb = [A.alloc(f"cg{i}", 2048) for i in range(2)]
        cvb = [A.alloc(f"cv{i}", 2048) for i in range(2)]
        outb = None
        U_g = P[:, 0:2048]
        U_v = P[:, 2048:4096]
        NSLOT = 7

        def up_tile(j):
            jp = j // 2
            sl = jp % 2
            cj = (j % 2) * 128
            bs = j % 2
            cg, cv = cgb[bs], cvb[bs]
            slot = j % NSLOT
            for half, (U, boff, wc0, cbuf, cname, jt) in enumerate((
                    (U_g, 0, cj, cg, f"cg{bs}", j), (U_v, 4, 256 + cj, cv, f"cv{bs}", NJ + j))):
                for tq in range(4):
                    b = boff + tq
                    for kt in range(8):
                        S.op("pe", lambda e, b=b, kt=kt, tq=tq, wc0=wc0: e.matmul(
                            bank(b), lhsT=wup3[sl][:, kt, wc0:wc0 + 128], rhs=hT[:, kt, tq * 512:(tq + 1) * 512],
                            start=(kt == 0), stop=(kt == 7)),
                             reads=[(f"wup{sl}", kt, half)] + [("hT", kt, t) for t in range(tq * 4, tq * 4 + 4)],
                             writes=[psr(b)])
                rb = [psr(boff + t) for t in range(4)]
                w0 = cols[:, 44 + jt * 3:45 + jt * 3]
                w1 = cols[:, 45 + jt * 3:46 + jt * 3]
                w2 = cols[:, 46 + jt * 3:47 + jt * 3]
                bb = cols[:, 176 + jt:177 + jt]
                S.op("act", lambda e, U=U, cbuf=cbuf, w1=w1, bb=bb: e.activation(
                    out=cbuf, in_=U, func=AF.Identity, scale=w1, bias=bb),
                     reads=rb + ["cols"], writes=[cname])
                if half == 1:
                    S.op("act", lambda e: e.activation(out=cg, in_=cg, func=AF.Gelu_apprx_tanh),
                         reads=[f"cg{bs}"], writes=[f"cg{bs}"])
                S.op("dve", lambda e, U=U, cbuf=cbuf, w0=w0: e.scalar_tensor_tensor(
                    out=cbuf[:, 1:2048], in0=U[:, 0:2047], scalar=w0, in1=cbuf[:, 1:2048],
                    op0=ALU.mult, op1=ALU.add), reads=rb + ["cols", cname], writes=[cname])
                S.op("dve", lambda e, U=U, cbuf=cbuf, w2=w2: e.scalar_tensor_tensor(
                    out=cbuf[:, 0:2047], in0=U[:, 1:2048], scalar=w2, in1=cbuf[:, 0:2047],
                    op0=ALU.mult, op1=ALU.add), reads=rb + ["cols", cname], writes=[cname])
            if j % 2 == 1 and jp + 2 < 11:
                load_wup(jp + 2)
            S.op("dve", lambda e: e.tensor_tensor(out=actT[:, slot, :], in0=cg, in1=cv, op=ALU.mult),
                 reads=[f"cg{bs}", f"cv{bs}"], writes=[("actT", slot)])

        def fin_out(tt):
            osl = tt % 4
            ob_ = outb_box[0]
            rms_dve(ss3[:, tt:tt + 1], rstd3[:, tt:tt + 1], ("n3", tt))
            S.op("dve", lambda e: e.scalar_tensor_tensor(
                out=ob_[osl], in0=x1t[tt], scalar=rstd3[:, tt:tt + 1], in1=gb, op0=ALU.mult, op1=ALU.mult),
                 reads=[f"x1t{tt}", (("n3", tt), "rstd"), "gb"], writes=[f"outb{osl}"])
            S.dma("sp", out_d[tt * 128:(tt + 1) * 128, :], ob_[osl], f"out{osl}", reads=[f"outb{osl}"])

        def down_partial(g):
            nonlocal_outb = outb_box
            j0, j1 = groups[g]
            last = (g == len(groups) - 1)
            nj = j1 - j0
            sl = g % 2
            for tt in range(NT):
                for cc in range(2):
                    b = next_bank()
                    for jj in range(nj):
                        slot = (j0 + jj) % NSLOT
                        S.op("pe", lambda e, b=b, jj=jj, tt=tt, cc=cc, slot=slot: e.matmul(
                            bank(b), lhsT=actT[:, slot, tt * 128:(tt + 1) * 128],
                            rhs=wdn[sl][:, jj, cc * 512:(cc + 1) * 512],
                            start=(jj == 0), stop=(jj == nj - 1)),
                             reads=[("actT", slot), (f"wdn{sl}", jj)], writes=[psr(b)])
                    S.op("dve", lambda e, b=b, tt=tt, cc=cc: e.tensor_tensor(
                        out=x1t[tt][:, cc * 512:(cc + 1) * 512], in0=bank(b), in1=x1t[tt][:, cc * 512:(cc + 1) * 512],
                        op=ALU.add), reads=[psr(b), f"x1t{tt}"], writes=[f"x1t{tt}"])
                if last:
                    rms_act(x1t[tt], [f"x1t{tt}"], ss3[:, tt:tt + 1], ("n3", tt), 1.0 / D)
                    if tt >= 1:
                        fin_out(tt - 1)
            if last:
                fin_out(NT - 1)

        outb_box = [None]
        NG = len(groups)
        for j in range(groups[0][0], groups[0][1]):
            up_tile(j)
        for g in range(NG):
            if g + 1 < NG:
                load_wdn(g + 1)
                up_tile(groups[g + 1][0])
            else:
                for nm in ("cg0", "cg1", "cv0", "cv1"):
                    A.free(nm)
                outb_box[0] = [A.alloc(f"outb{i}", 1024) for i in range(4)]
            down_partial(g)
            if g + 1 < NG:
                for j in range(groups[g + 1][0] + 1, groups[g + 1][1]):
                    up_tile(j)
        S.emit(final_waits=[k for k in ("out", "out0", "out1", "out2", "out3") if k in S.dma_sems])
    return nc, S


_CACHE = {}


def _pack_inputs(inp):
    f = lambda a: np.ascontiguousarray(np.asarray(a, dtype=np.float32))
    cols = np.zeros((128, 220), np.float32)
    cw = f(inp["lru_conv_w"])[0]
    cb = f(inp["lru_conv_b"])[0]
    for ct in range(4):
        for k in range(4):
            cols[:, ct * 4 + k] = cw[k, ct * 128:(ct + 1) * 128]
        cols[:, 16 + ct] = cb[ct * 128:(ct + 1) * 128]
    ba = f(inp["lru_b_a"])[0]
    bx = f(inp["lru_b_x"])[0]
    lm = f(inp["lru_lambda"])[0]
    for d in range(2):
        for ct in range(4):
            cols[:, 20 + d * 4 + ct] = ba[d, ct * 128:(ct + 1) * 128]
            cols[:, 28 + d * 4 + ct] = bx[d, ct * 128:(ct + 1) * 128]
            cols[:, 36 + d * 4 + ct] = lm[d, ct * 128:(ct + 1) * 128]
    fw_ = f(inp["ffn_conv_w"])[0]
    fb_ = f(inp["ffn_conv_b"])[0]
    for jt in range(44):
        for k in range(3):
            cols[:, 44 + jt * 3 + k] = fw_[k, jt * 128:(jt + 1) * 128]
        cols[:, 176 + jt] = fb_[jt * 128:(jt + 1) * 128]
    rows = np.concatenate([f(inp["lambda_q1"])[0], f(inp["lambda_k1"])[0], f(inp["lambda_q2"])[0],
                           f(inp["lambda_k2"])[0], f(inp["subln_g"])[0]])[None, :]
    gvec = np.stack([f(inp["attn_norm_g"])[0], f(inp["ffn_norm_g"])[0], f(inp["final_norm_g"])], 0)
    wa = f(inp["lru_w_a"])[0]
    wx = f(inp["lru_w_x"])[0]
    wg = np.zeros((16, 128, 128), np.float32)
    for gate, w in enumerate((wa, wx)):
        for d in range(2):
            for ct in range(4):
                idx = (gate * 2 + d) * 4 + ct
                wg[idx, 0:64, 0:64] = w[d, 2 * ct]
                wg[idx, 64:128, 64:128] = w[d, 2 * ct + 1]
    shared = {
        "w_in": f(inp["w_in"])[0], "w_out": f(inp["w_out"])[0], "w_up": f(inp["w_up"])[0],
        "w_down": f(inp["w_down"])[0], "wg": wg, "cols": cols, "rows": np.ascontiguousarray(rows),
        "gvec": np.ascontiguousarray(gvec),
    }
    return shared


def kernel(**inputs):
    x = np.asarray(inputs["x"], dtype=np.float32)
    B = x.shape[0]
    shared = _pack_inputs(inputs)
    taps = tuple(t for t in os.environ.get("KTAPS", "").split(",") if t)
    key = ("nc", taps)
    if key not in _CACHE:
        _CACHE[key] = build(debug_taps=taps)
    nc, _ = _CACHE[key]
    in_maps = []
    for b in range(B):
        m = dict(shared)
        m["x"] = np.ascontiguousarray(x[b])
        in_maps.append(m)
    res = run_bass_kernel_spmd(nc, in_maps, core_ids=list(range(B)))
    out = np.stack([np.asarray(r["out"], dtype=np.float32) for r in res.results], 0)
    if taps:
        kernel.last_taps = [{k: v for k, v in r.items() if k.startswith("tap_")} for r in res.results]
    return out
```

```python
import os
from contextlib import ExitStack
import numpy as np
import concourse.bass as bass
import concourse.mybir as mybir
from concourse.bass_utils import run_bass_kernel_spmd

F32 = mybir.dt.float32
BF16 = mybir.dt.bfloat16
AF = mybir.ActivationFunctionType
ALU = mybir.AluOpType

ENGS = ("pe", "act", "dve", "pool", "sp")
S_TOK = 2048
D = 1024
NT = 16
DFF = 2816
NJ = 22
EPS = 1e-6
LAMBDA_INIT = 0.8 - 0.6 * 1.0


class Op:
    __slots__ = ("eng", "fn", "deps", "signal", "count", "is_dma", "dma_sem", "dma_key", "dma_target", "name")

    def __init__(self, eng, fn, name=""):
        self.eng = eng
        self.fn = fn
        self.deps = []
        self.signal = False
        self.count = None
        self.is_dma = False
        self.dma_sem = None
        self.dma_key = None
        self.dma_target = 0
        self.name = name


def _buf(r):
    return r[0] if isinstance(r, tuple) else r


class Sched:
    def __init__(self, nc, es):
        self.nc = nc
        self.es = es
        self.per_eng = {e: [] for e in ENGS}
        self.last_writer = {}
        self.readers = {}
        self.eng_sem = {e: es.enter_context(nc.semaphore("prog_" + e)) for e in ENGS if e != "sp"}
        self.dma_sems = {}
        self.dma_counts = {}
        self.buf_deps = {}
        self.nops = 0

    def _add_dep(self, o, d):
        if d is o:
            return
        if d.is_dma:
            o.deps.append((d, 16 * self.dma_counts[d.dma_key]))
            return
        if d.eng == o.eng and o.eng == "pe":
            return
        d.signal = True
        o.deps.append((d, None))

    NON_ARENA = ("ps", "psu", "lam", "zr", "ecol", "pcol", "cA", "c2A", "wupb", "wdnb")

    def _check(self, rs):
        for r in rs:
            b = _buf(r)
            if isinstance(b, tuple) or b in self.NON_ARENA or b in self.buf_deps:
                continue
            raise RuntimeError(f"resource {r!r} is not an arena buffer")

    def op(self, eng, fn, reads=(), writes=(), name=""):
        self._check(reads)
        self._check(writes)
        o = Op(eng, fn, name)
        deps = {}
        for r in reads:
            w = self.last_writer.get(r)
            if w is not None:
                deps[id(w)] = w
        for r in writes:
            w = self.last_writer.get(r)
            if w is not None:
                deps[id(w)] = w
            for rd in self.readers.get(r, {}).values():
                deps[id(rd)] = rd
        for r in list(reads) + list(writes):
            for d in self.buf_deps.get(_buf(r), ()):
                deps[id(d)] = d
        for d in deps.values():
            self._add_dep(o, d)
        for r in reads:
            rd = self.readers.setdefault(r, {})
            key = ("dma", id(o)) if False else eng
            rd[key] = o
        for r in writes:
            self.last_writer[r] = o
            self.readers[r] = {}
        self.per_eng[eng].append(o)
        self.nops += 1
        return o

    def dma(self, queue, out, in_, semkey, reads=(), writes=(), name="", **kw):
        if semkey not in self.dma_sems:
            self.dma_sems[semkey] = self.es.enter_context(self.nc.semaphore("dma_" + semkey))
            self.dma_counts[semkey] = 0

        def fn(e):
            return e.dma_start(out=out, in_=in_, **kw)

        self._check(reads)
        self._check(writes)
        o = Op(queue, fn, name)
        o.is_dma = True
        o.dma_key = semkey
        o.dma_sem = self.dma_sems[semkey]
        deps = {}
        for r in reads:
            w = self.last_writer.get(r)
            if w is not None:
                deps[id(w)] = w
        for r in writes:
            w = self.last_writer.get(r)
            rds = self.readers.get(r, {})
            if w is not None and not (w.is_dma and rds):
                deps[id(w)] = w
            for rd in rds.values():
                deps[id(rd)] = rd
        for r in list(reads) + list(writes):
            for d in self.buf_deps.get(_buf(r), ()):
                deps[id(d)] = d
        for d in deps.values():
            self._add_dep(o, d)
        self.dma_counts[semkey] += 1
        o.dma_target = 16 * self.dma_counts[semkey]
        for r in reads:
            self.readers.setdefault(r, {})[("dma", semkey)] = o
        for r in writes:
            self.last_writer[r] = o
            self.readers[r] = {}
        self.per_eng[queue].append(o)
        return o

    def ops_touching(self, bufname):
        out = {}
        for r, w in self.last_writer.items():
            if _buf(r) == bufname:
                out[id(w)] = w
        for r, rd in self.readers.items():
            if _buf(r) == bufname:
                for o in rd.values():
                    out[id(o)] = o
        return list(out.values())

    def finalize(self):
        for e in ENGS:
            c = 0
            for o in self.per_eng[e]:
                if o.is_dma:
                    continue
                if o.signal:
                    c += 1
                    o.count = c

    def emit_engine(self, ename, e):
        known = {}
        for o in self.per_eng[ename]:
            waits = {}
            for d, ov in o.deps:
                if d.is_dma:
                    key = ("dma", d.dma_key)
                    sem, val = d.dma_sem, ov
                else:
                    key = ("eng", d.eng)
                    sem, val = self.eng_sem[d.eng], d.count
                if known.get(key, 0) >= val:
                    continue
                if key not in waits or waits[key][1] < val:
                    waits[key] = (sem, val)
            for key, (sem, val) in waits.items():
                e.wait_ge(sem, val)
                known[key] = val
            ins = o.fn(e)
            if o.is_dma:
                ins.then_inc(o.dma_sem, 16)
            elif o.signal:
                ins.then_inc(self.eng_sem[ename], 1)

    def emit(self, final_waits=()):
        self.finalize()
        nc = self.nc
        with nc.Block() as block:
            @block.tensor
            def _(e):
                self.emit_engine("pe", e)

            @block.scalar
            def _(e):
                self.emit_engine("act", e)

            @block.vector
            def _(e):
                self.emit_engine("dve", e)

            @block.gpsimd
            def _(e):
                self.emit_engine("pool", e)

            @block.sync
            def _(e):
                self.emit_engine("sp", e)
                for k in final_waits:
                    e.wait_ge(self.dma_sems[k], 16 * self.dma_counts[k])


class Arena:
    def __init__(self, S, tensor, size):
        self.S = S
        self.t = tensor
        self.size = size
        self.live = {}
        self.retired = []

    def alloc(self, name, n, dt=F32, top=False):
        n = (n + 7) // 8 * 8
        spans = sorted(self.live.values())
        gaps = []
        pos = 0
        for (o, m) in spans:
            if o - pos >= n:
                gaps.append((pos, o))
            pos = max(pos, o + m)
        if self.size - pos >= n:
            gaps.append((pos, self.size))
        if not gaps:
            raise RuntimeError(f"arena full allocating {name} ({n}); live={self.live}")
        if top:
            off = gaps[-1][1] - n
        else:
            off = gaps[0][0]
        self.live[name] = (off, n)
        deps = []
        for (o, m, ops) in self.retired:
            if o < off + n and off < o + m:
                deps.extend(ops)
        self.S.buf_deps[name] = deps
        v = self.t[:, off:off + n]
        return v if dt == F32 else v.bitcast(dt)

    def free(self, name):
        off, n = self.live.pop(name)
        self.retired.append((off, n, self.S.ops_touching(name)))


def build(debug_taps=()):
    nc = bass.Bass("TRN2", target_bir_lowering=False)
    dram_in = lambda n, s: nc.dram_tensor(n, list(s), F32, kind="ExternalInput").ap()
    x_d = dram_in("x", [S_TOK, D])
    w_in_d = dram_in("w_in", [D, 2560])
    w_out_d = dram_in("w_out", [D, D])
    w_up_d = dram_in("w_up", [D, 2 * DFF])
    w_dn_d = dram_in("w_down", [DFF, D])
    wg_d = dram_in("wg", [16, 128, 128])
    cols_d = dram_in("cols", [128, 220])
    rows_d = dram_in("rows", [1, 384])
    gvec_d = dram_in("gvec", [3, D])
    out_d = nc.dram_tensor("out", [S_TOK, D], F32, kind="ExternalOutput").ap()
    taps = {}

    es = ExitStack()
    with es:
        S = Sched(nc, es)
        ARENA_N = 53200
        arena_t = es.enter_context(nc.sbuf_tensor("arena", [128, ARENA_N], F32))
        A = Arena(S, arena_t, ARENA_N)
        P = es.enter_context(nc.psum_tensor("P", [128, 4096], F32))

        def bank(b):
            return P[:, b * 512:(b + 1) * 512]

        def bankbf(b):
            return P[:, b * 512:(b + 1) * 512].bitcast(BF16)

        def psr(b):
            return ("ps", b)

        rot = {"i": 0}

        def next_bank(lo=0, hi=8):
            n = hi - lo
            b = lo + rot["i"] % n
            rot["i"] += 1
            return b

        cols = A.alloc("cols", 224)
        rowsb = A.alloc("rowsb", 384)
        gb = A.alloc("gb", 1024)
        ident = A.alloc("ident", 64, BF16)
        zeros_bf = A.alloc("zeros_bf", 64, BF16)
        identf = A.alloc("identf", 128)
        st = A.alloc("stats", 256)
        ss1 = st[:, 0:16]
        rstd1 = st[:, 16:32]
        lam_s = st[:, 32:40]
        cA = st[:, 40:48]
        c2A = st[:, 48:56]
        ecol = st[:, 56:64]
        pcol = st[:, 64:72]
        zr = st[:, 72:88]
        ss3 = st[:, 88:104]
        rstd3 = st[:, 104:120]
        gsub = A.alloc("gsub", 128)
        junk = A.alloc("junk", 512, BF16)

        S.dma("sp", cols[:, 0:220], cols_d, "c_cols", writes=["cols"])
        S.dma("sp", rowsb, rows_d[0:1, :].to_broadcast([128, 384]), "c_rows", writes=["rowsb"])
        S.dma("sp", gb, gvec_d[0:1, :].to_broadcast([128, D]), "c_gb", writes=["gb"])

        S.op("pool", lambda e: e.iota(identf, pattern=[[1, 128]], base=0, channel_multiplier=-1,
                                      allow_small_or_imprecise_dtypes=True), writes=["identf"])
        S.op("dve", lambda e: e.tensor_single_scalar(out=ident, in_=identf, scalar=0.0, op=ALU.is_equal),
             reads=["identf"], writes=["ident"])

        S.op("pool", lambda e: e.memset(zeros_bf, 0.0), writes=["zeros_bf"])
        S.op("dve", lambda e: e.scalar_tensor_tensor(out=junk[:, 0:64], in0=rowsb[:, 0:64], scalar=1.0,
                                                     in1=rowsb[:, 64:128], op0=ALU.mult, op1=ALU.mult,
                                                     accum_out=lam_s[:, 0:1]),
             reads=["rowsb"], writes=["junk", ("lam", 0)])
        S.op("dve", lambda e: e.scalar_tensor_tensor(out=junk[:, 64:128], in0=rowsb[:, 128:192], scalar=1.0,
                                                     in1=rowsb[:, 192:256], op0=ALU.mult, op1=ALU.mult,
                                                     accum_out=lam_s[:, 1:2]),
             reads=["rowsb"], writes=["junk", ("lam", 1)])
        S.op("act", lambda e: e.activation(out=lam_s[:, 2:4], in_=lam_s[:, 0:2], func=AF.Exp),
             reads=[("lam", 0), ("lam", 1)], writes=[("lam", 2)])
        S.op("dve", lambda e: e.tensor_tensor(out=lam_s[:, 4:5], in0=lam_s[:, 3:4], in1=lam_s[:, 2:3],
                                              op=ALU.subtract), reads=[("lam", 2)], writes=[("lam", 4)])
        S.op("dve", lambda e: e.tensor_scalar(out=lam_s[:, 4:5], in0=lam_s[:, 4:5], scalar1=-LAMBDA_INIT,
                                              scalar2=None, op0=ALU.add), reads=[("lam", 4)], writes=[("lam", 4)])
        neglam = lam_s[:, 4:5]
        S.op("dve", lambda e: e.tensor_scalar(out=gsub, in0=rowsb[:, 256:384], scalar1=1.0 - LAMBDA_INIT,
                                              scalar2=None, op0=ALU.mult), reads=["rowsb"], writes=["gsub"])
        S.op("act", lambda e: e.activation(out=ecol, in_=cols[:, 36:44], func=AF.Exp, scale=-1.0),
             reads=["cols"], writes=["ecol"])
        S.op("dve", lambda e: e.tensor_scalar(out=pcol, in0=ecol, scalar1=1.0 / 7.0, scalar2=None, op0=ALU.mult),
             reads=["ecol"], writes=["pcol"])
        for cst in (-1.0 / 6, 1.0 / 5, -1.0 / 4, 1.0 / 3, -1.0 / 2, 1.0):
            S.op("dve", lambda e, cst=cst: e.scalar_tensor_tensor(out=pcol, in0=pcol, scalar=float(cst), in1=ecol,
                                                                  op0=ALU.add, op1=ALU.mult),
                 reads=["pcol", "ecol"], writes=["pcol"])
        S.op("dve", lambda e: e.tensor_scalar(out=cA, in0=pcol, scalar1=-8.0, scalar2=None, op0=ALU.mult),
             reads=["pcol"], writes=["cA"])
        S.op("dve", lambda e: e.tensor_scalar(out=c2A, in0=pcol, scalar1=-16.0, scalar2=None, op0=ALU.mult),
             reads=["pcol"], writes=["c2A"])

        hT = A.alloc("hT", 8192, BF16).rearrange("p (k t) -> p k t", k=8)
        mixA = A.alloc("mixA", 4096, BF16).rearrange("p (k t) -> p k t", k=4)
        win_qkv = A.alloc("win_qkv", 6144, BF16).rearrange("p (k c) -> p k c", k=8)
        win_lru = A.alloc("win_lru", 4096, BF16).rearrange("p (k c) -> p k c", k=8)
        wg = A.alloc("wg", 1024, BF16).rearrange("p (n c) -> p n c", n=16)

        w_in_v = w_in_d.rearrange("(k p) c -> p k c", p=128)
        for kt in range(8):
            S.dma("pool", win_qkv[:, kt, :], w_in_d[kt * 128:(kt + 1) * 128, 0:1536], "w_qkv",
                  writes=[("win_qkv", kt)])

        def rms_rstd(src_ap, src_res, ss_col, rstd_col, tag, inv_n):
            n = src_ap.shape[-1]
            S.op("act", lambda e: e.activation(out=junk[:, 0:n], in_=src_ap, func=AF.Square, accum_out=ss_col),
                 reads=list(src_res), writes=["junk", (tag, "ss")])
            S.op("act", lambda e: e.activation(out=ss_col, in_=ss_col, func=AF.Sqrt, scale=inv_n, bias=EPS),
                 reads=[(tag, "ss")], writes=[(tag, "ss")])
            S.op("dve", lambda e: e.reciprocal(out=rstd_col, in_=ss_col), reads=[(tag, "ss")], writes=[(tag, "rstd")])

        def rms_act(src_ap, src_res, ss_col, tag, inv_n):
            n = src_ap.shape[-1]
            S.op("act", lambda e: e.activation(out=junk[:, 0:n], in_=src_ap, func=AF.Square, accum_out=ss_col),
                 reads=list(src_res), writes=["junk", (tag, "ss")])
            S.op("act", lambda e: e.activation(out=ss_col, in_=ss_col, func=AF.Sqrt, scale=inv_n, bias=EPS),
                 reads=[(tag, "ss")], writes=[(tag, "ss")])

        def rms_dve(ss_col, rstd_col, tag):
            S.op("dve", lambda e: e.reciprocal(out=rstd_col, in_=ss_col), reads=[(tag, "ss")], writes=[(tag, "rstd")])

        def norm_dve(src_ap, src_res, hn_slot, hn_res, ss_col, rstd_col, tag):
            rms_dve(ss_col, rstd_col, tag)
            S.op("dve", lambda e: e.scalar_tensor_tensor(out=hn_slot, in0=src_ap, scalar=rstd_col, in1=gb,
                                                         op0=ALU.mult, op1=ALU.mult),
                 reads=list(src_res) + [(tag, "rstd"), "gb"], writes=[hn_res])

        def norm_part(src_ap, src_res, hn_slot, hn_res, ss_col, rstd_col, tag):
            rms_rstd(src_ap, src_res, ss_col, rstd_col, tag, 1.0 / D)
            S.op("dve", lambda e: e.scalar_tensor_tensor(out=hn_slot, in0=src_ap, scalar=rstd_col, in1=gb,
                                                         op0=ALU.mult, op1=ALU.mult),
                 reads=list(src_res) + [(tag, "rstd"), "gb"], writes=[hn_res])

        def transpose_part(tt, dstT, dst_name, hn_slot, hn_res):
            b = next_bank()
            pv = bankbf(b).rearrange("p (k t) -> p k t", k=8)
            for kt in range(8):
                S.op("pe", lambda e, kt=kt: e.transpose(out=pv[:, kt, :], in_=hn_slot[:, kt * 128:(kt + 1) * 128],
                                                        identity=ident),
                     reads=[hn_res, "ident"], writes=[psr(b)])
            S.op("act", lambda e: e.copy(out=dstT[:, :, tt * 128:(tt + 1) * 128], in_=pv),
                 reads=[psr(b)], writes=[(dst_name, k, tt) for k in range(8)])

        qz = [A.alloc(f"qz{c}", 4096, BF16).rearrange("p (h t) -> p h t", h=4) for c in range(2)]
        kT = A.alloc("kT", 4096, BF16).rearrange("p (h t) -> p h t", h=4)
        vaug = A.alloc("vaug", 4160, BF16).rearrange("p (t h e) -> p t h e", t=16, h=4)
        absd = A.alloc("absd", 3968)
        S.op("pool", lambda e: e.memset(vaug[:, :, :, 128:130], 1.0), writes=[("vaug", "init")])
        S.op("pool", lambda e: e.memset(qz[0][64:128, :, :].rearrange("p h t -> p (h t)"), 0.0), writes=[("qz0", "z")])
        S.op("pool", lambda e: e.memset(qz[1][0:64, :, :].rearrange("p h t -> p (h t)"), 0.0), writes=[("qz1", "z")])

        ev = {"i": 0}

        def unit_qk(h, which, tq):
            col0 = h * 128 if which == "q" else 512 + h * 128
            b = next_bank()
            for kt in range(8):
                S.op("pe", lambda e, kt=kt: e.matmul(
                    bank(b), lhsT=win_qkv[:, kt, col0:col0 + 128], rhs=hT[:, kt, tq * 512:(tq + 1) * 512],
                    start=(kt == 0), stop=(kt == 7)),
                     reads=[("win_qkv", kt)] + [("hT", kt, t) for t in range(tq * 4, tq * 4 + 4)],
                     writes=[psr(b)])
            if which == "q":
                for c in range(2):
                    o_ap = qz[c][c * 64:(c + 1) * 64, h, tq * 512:(tq + 1) * 512]
                    i_ap = bank(b)[c * 64:(c + 1) * 64, :]
                    if (ev["i"] + c) % 2 == 0:
                        S.op("act", lambda e, o_ap=o_ap, i_ap=i_ap: e.activation(
                            out=o_ap, in_=i_ap, func=AF.Identity, scale=0.125),
                             reads=[psr(b), (f"qz{c}", "z")], writes=[(f"qz{c}", h, tq)])
                    else:
                        S.op("dve", lambda e, o_ap=o_ap, i_ap=i_ap: e.tensor_scalar(
                            out=o_ap, in0=i_ap, scalar1=0.125, scalar2=None, op0=ALU.mult),
                             reads=[psr(b), (f"qz{c}", "z")], writes=[(f"qz{c}", h, tq)])
            else:
                o_ap = kT[:, h, tq * 512:(tq + 1) * 512]
                if ev["i"] % 2 == 0:
                    S.op("act", lambda e: e.copy(out=o_ap, in_=bank(b)), reads=[psr(b)], writes=[("kT", h, tq)])
                else:
                    S.op("dve", lambda e: e.tensor_copy(out=o_ap, in_=bank(b)), reads=[psr(b)], writes=[("kT", h, tq)])
            ev["i"] += 1

        def unit_v(tt):
            b = next_bank()
            for kt in range(8):
                S.op("pe", lambda e, kt=kt: e.matmul(
                    bank(b), lhsT=hT[:, kt, tt * 128:(tt + 1) * 128], rhs=win_qkv[:, kt, 1024:1536],
                    start=(kt == 0), stop=(kt == 7)),
                     reads=[("win_qkv", kt), ("hT", kt, tt)], writes=[psr(b)])
            src = bank(b).rearrange("p (h e) -> p h e", h=4)
            if tt % 2 == 0:
                S.op("act", lambda e: e.copy(out=vaug[:, tt, :, 0:128], in_=src),
                     reads=[psr(b), ("vaug", "init")], writes=[("vaug", tt)])
            else:
                S.op("dve", lambda e: e.tensor_copy(out=vaug[:, tt, :, 0:128], in_=src),
                     reads=[psr(b), ("vaug", "init")], writes=[("vaug", tt)])

        NS = 3
        xs = [A.alloc(f"xs{i}", 1024) for i in range(NS)]
        hnA = [A.alloc(f"hnA{i}", 512, BF16) for i in range(NS)]
        ready = []

        def chunk_units(tq):
            u = []
            for h in range(4):
                u.append(lambda h=h: unit_qk(h, "q", tq))
                u.append(lambda h=h: unit_qk(h, "k", tq))
            for t in range(tq * 4, tq * 4 + 4):
                u.append(lambda t=t: unit_v(t))
            return u

        for tt in range(NT + 1):
            if tt < NT:
                sl = tt % NS
                S.dma("sp", xs[sl], x_d[tt * 128:(tt + 1) * 128, :], f"xs{sl}", writes=[f"xs{sl}"])
                norm_part(xs[sl], [f"xs{sl}"], hnA[sl], f"hnA{sl}", ss1[:, tt:tt + 1], rstd1[:, tt:tt + 1], ("n1", tt))
            if tt >= 1:
                pt = tt - 1
                transpose_part(pt, hT, "hT", hnA[pt % NS], f"hnA{pt % NS}")
                if pt % 4 == 3:
                    ready.extend(chunk_units(pt // 4))
            for _ in range(3):
                if ready:
                    ready.pop(0)()
        while ready:
            ready.pop(0)()
        for i in range(NS):
            A.free(f"xs{i}")
            A.free(f"hnA{i}")
        A.free("win_qkv")

        for kt in range(8):
            S.dma("pool", win_lru[:, kt, :], w_in_d[kt * 128:(kt + 1) * 128, 1536:2560], "w_lru",
                  writes=[("win_lru", kt)])
        S.dma("pool", wg, wg_d.rearrange("n p c -> p n c"), "w_g", writes=["wg"])

        S.op("pool", lambda e: e.iota(absd, pattern=[[1, 3968]], base=-1920, channel_multiplier=-1,
                                      allow_small_or_imprecise_dtypes=True), writes=["absd"])
        S.op("act", lambda e: e.activation(out=absd, in_=absd, func=AF.Abs), reads=["absd"], writes=["absd"])
        NSL = 6
        sc = [A.alloc(f"sc{i}", 512) for i in range(NSL)]
        ET = [A.alloc(f"ET{i}", 256, BF16) for i in range(NSL)]
        o1 = A.alloc("o1", 512)
        ob = A.alloc("ob", 512)
        on = A.alloc("on", 256, BF16)

        btab = A.alloc("btab", 128).rearrange("p (v m) -> p v m", v=8)
        fgt = A.alloc("fgt", 32).rearrange("p (v q) -> p v q", v=8)
        numb = [A.alloc(f"numb{c}", 520)[:, 0:516].rearrange("p (q e) -> p q e", q=4) for c in range(2)]
        sqj = A.alloc("sqj", 128)
        klf = st[:, 120:121]
        cmf = st[:, 128:144]
        ptab = st[:, 144:148]
        S.op("pool", lambda e: e.iota(klf, pattern=[[0, 1]], base=0, channel_multiplier=1,
                                      allow_small_or_imprecise_dtypes=True), writes=[("zr", "klf")])
        S.op("pool", lambda e: e.iota(cmf, pattern=[[1, 16]], base=0, channel_multiplier=0,
                                      allow_small_or_imprecise_dtypes=True), writes=[("zr", "cmf")])
        S.op("pool", lambda e: e.iota(ptab, pattern=[[128, 4]], base=0, channel_multiplier=1,
                                      allow_small_or_imprecise_dtypes=True), writes=[("zr", "ptab")])
        for h_ in range(4):
            sl_ = 2.0 ** (-2.0 * (h_ + 1))
            for sg in range(2):
                v_ = h_ * 2 + sg
                ksign = sl_ if sg == 0 else -sl_
                cadd = 0.0 if sg == 0 else sl_ * 511.0
                S.op("dve", lambda e, v_=v_, sl_=sl_, cadd=cadd: e.tensor_scalar(
                    out=btab[:, v_, :], in0=cmf, scalar1=-sl_ * 128.0, scalar2=cadd, op0=ALU.mult, op1=ALU.add),
                     reads=[("zr", "cmf")], writes=[("btab", v_)])
                S.op("dve", lambda e, v_=v_, ksign=ksign: e.scalar_tensor_tensor(
                    out=btab[:, v_, :], in0=klf.to_broadcast([128, 16]), scalar=ksign, in1=btab[:, v_, :],
                    op0=ALU.mult, op1=ALU.add), reads=[("zr", "klf"), ("btab", v_)], writes=[("btab", v_)])
            S.op("act", lambda e, h_=h_, sl_=sl_: e.activation(out=fgt[:, 2 * h_, :], in_=ptab, func=AF.Exp, scale=-sl_),
                 reads=[("zr", "ptab")], writes=[("fgt", 2 * h_)])
            S.op("act", lambda e, h_=h_, sl_=sl_: e.activation(out=fgt[:, 2 * h_ + 1, :], in_=ptab, func=AF.Exp,
                                                               scale=sl_, bias=-sl_ * 511.0),
                 reads=[("zr", "ptab")], writes=[("fgt", 2 * h_ + 1)])

        iters = []
        groups_at = []
        for h in range(4):
            slope_h = 2.0 ** (-2.0 * (h + 1))
            dmax = 40.0 / slope_h
            for qc in range(4):
                cls_kbs = {"B": [], "D": [], "A": []}
                for kb in range(16):
                    q0, q1, k0, k1 = qc * 512, qc * 512 + 511, kb * 128, kb * 128 + 127
                    mind = max(0, k0 - q1, q0 - k1)
                    if mind <= dmax:
                        cl_ = "D" if h < 2 else ("B" if k1 < q0 else ("A" if k0 > q1 else "D"))
                        cls_kbs[cl_].append(kb)
                order = [cl for cl in ("B", "D", "A") if cls_kbs[cl]]
                for c in range(2):
                    for ci_, cl in enumerate(order):
                        g = len(groups_at)
                        groups_at.append((h, qc, c, cl, ci_ == 0, ci_ == len(order) - 1))
                        kbs = cls_kbs[cl]
                        for n, kb in enumerate(kbs):
                            iters.append((h, qc, c, kb, n == 0, n == len(kbs) - 1, g))
        NI = len(iters)
        SCB = (0, 1, 2, 7)
        ACC = ((3, 4), (5, 6))
        bank_of = {}
        free_sb = list(SCB)

        def take_score_bank():
            return free_sb.pop(0)

        def release_score_bank(b):
            free_sb.append(b)

        def acc_ap(grp, qi):
            bk = ACC[grp % 2][qi // 2]
            return bank(bk)[:, (qi % 2) * 129:(qi % 2) * 129 + 129], bk

        def emit_qk(i):
            h, qc, c, kb, first, last, grp = iters[i]
            sb_ = take_score_bank()
            bank_of[i] = sb_
            S.op("pe", lambda e: e.matmul(
                bank(sb_), lhsT=kT[:, h, kb * 128:(kb + 1) * 128],
                rhs=qz[c][:, h, qc * 512:(qc + 1) * 512], start=True, stop=True),
                 reads=[("kT", h, kb // 4), (f"qz{c}", h, qc), (f"qz{c}", "z")], writes=[psr(sb_)])

        def emit_bias(i):
            h, qc, c, kb, first, last, grp = iters[i]
            if groups_at[grp][3] != "D":
                return
            sb_ = bank_of[i]
            s = i % NSL
            slope = 2.0 ** (-2.0 * (h + 1))
            Dv = qc * 512 - kb * 128 + 1920
            S.op("dve", lambda e: e.scalar_tensor_tensor(out=sc[s], in0=absd[:, Dv:Dv + 512], scalar=-slope,
                                                         in1=bank(sb_), op0=ALU.mult, op1=ALU.add),
                 reads=["absd", psr(sb_)], writes=[f"sc{s}"])
            release_score_bank(sb_)

        def emit_softmax(i):
            h, qc, c, kb, first, last, grp = iters[i]
            cl = groups_at[grp][3]
            sb_ = bank_of[i]
            s = i % NSL
            if cl != "D":
                sg = 0 if cl == "B" else 1
                m_ = (4 * qc - kb) if cl == "B" else (kb - 4 * qc)
                v_ = h * 2 + sg
                S.op("act", lambda e: e.activation(out=ET[s], in_=bank(sb_), func=AF.Exp,
                                                   bias=btab[:, v_, m_:m_ + 1]),
                     reads=[psr(sb_), ("btab", v_)], writes=[f"ET{s}"])
                release_score_bank(sb_)
                return
            S.op("act", lambda e: e.activation(out=ET[s], in_=sc[s], func=AF.Exp),
                 reads=[f"sc{s}"], writes=[f"ET{s}"])

        def emit_pv(i):
            h, qc, c, kb, first, last, grp = iters[i]
            s = i % NSL
            for qi in range(4):
                dst, bk = acc_ap(grp, qi)
                S.op("pe", lambda e, dst=dst, qi=qi: e.matmul(
                    dst, lhsT=ET[s][:, qi * 128:(qi + 1) * 128], rhs=vaug[:, kb, h, 0:129],
                    start=(first and qi % 2 == 0), stop=(last and qi % 2 == 1)),
                     reads=[f"ET{s}", ("vaug", kb)], writes=[psr(bk)])
                if qi == 0:
                    S.op("pe", lambda e, bk=bk: e.matmul(
                        bank(bk)[:, 258:512], lhsT=zeros_bf, rhs=kT[:, 0, 0:254], start=False, stop=False),
                         reads=["zeros_bf", ("kT", 0, 0)], writes=[psr(bk)])

        def finalize_stages(grp):
            h, qc, c, cl, first_cls, last_cls = groups_at[grp]
            nb = numb[c]
            nn = f"numb{c}"

            def s_combine():
                if cl == "D":
                    for j in range(2):
                        bk = ACC[grp % 2][j]
                        src = bank(bk)[:, 0:258]
                        dst = nb[:, 2 * j:2 * j + 2, :].rearrange("p q e -> p (q e)")
                        if first_cls:
                            S.op("dve", lambda e, src=src, dst=dst: e.tensor_copy(out=dst, in_=src),
                                 reads=[psr(bk)], writes=[(nn, 2 * j), (nn, 2 * j + 1)])
                        else:
                            S.op("dve", lambda e, src=src, dst=dst: e.tensor_tensor(out=dst, in0=src, in1=dst, op=ALU.add),
                                 reads=[psr(bk), (nn, 2 * j), (nn, 2 * j + 1)], writes=[(nn, 2 * j), (nn, 2 * j + 1)])
                    return
                fcol = fgt[:, 2 * h + (0 if cl == "B" else 1), :]
                for qi in range(4):
                    a, bk = acc_ap(grp, qi)
                    if first_cls:
                        S.op("dve", lambda e, a=a, qi=qi: e.tensor_scalar(
                            out=nb[:, qi, :], in0=a, scalar1=fcol[:, qi:qi + 1], scalar2=None, op0=ALU.mult),
                             reads=[psr(bk), ("fgt", 2 * h), ("fgt", 2 * h + 1)], writes=[(nn, qi)])
                    else:
                        S.op("dve", lambda e, a=a, qi=qi: e.scalar_tensor_tensor(
                            out=nb[:, qi, :], in0=a, scalar=fcol[:, qi:qi + 1], in1=nb[:, qi, :],
                            op0=ALU.mult, op1=ALU.add),
                             reads=[psr(bk), ("fgt", 2 * h), ("fgt", 2 * h + 1), (nn, qi)], writes=[(nn, qi)])

            if not last_cls:
                return [(0, s_combine)]

            single = first_cls and last_cls
            on_act = h < 2

            def srcv(qi):
                if single:
                    a, bk = acc_ap(grp, qi)
                    return a, psr(bk)
                return nb[:, qi, :], (nn, qi)

            def s_recip():
                for qi in range(4):
                    a, r_ = srcv(qi)
                    S.op("dve", lambda e, qi=qi, a=a: e.reciprocal(out=zr[:, qi:qi + 1], in_=a[:, 128:129]),
                         reads=[r_], writes=[("zr", qi)])

            if c == 0:
                def s_o1():
                    for qi in range(4):
                        a, r_ = srcv(qi)
                        if on_act:
                            S.op("act", lambda e, qi=qi, a=a: e.activation(
                                out=o1[:, qi * 128:(qi + 1) * 128], in_=a[:, 0:128], func=AF.Identity,
                                scale=zr[:, qi:qi + 1]), reads=[r_, ("zr", qi)], writes=[("o1", qi)])
                        else:
                            S.op("dve", lambda e, qi=qi, a=a: e.tensor_scalar(
                                out=o1[:, qi * 128:(qi + 1) * 128], in0=a[:, 0:128], scalar1=zr[:, qi:qi + 1],
                                scalar2=None, op0=ALU.mult),
                                 reads=[r_, ("zr", qi)], writes=[("o1", qi)])
                st_ = [] if single else [(0, s_combine)]
                return st_ + [(1, s_recip), (3 if on_act else 1, s_o1)]

            def s_ob():
                s_recip()
                S.op("dve", lambda e: e.tensor_scalar(out=zr[:, 4:8], in0=zr[:, 0:4], scalar1=neglam, scalar2=None,
                                                      op0=ALU.mult),
                     reads=[("zr", q) for q in range(4)] + [("lam", 4)], writes=[("zr", 4)])
                for qi in range(4):
                    a, r_ = srcv(qi)
                    S.op("dve", lambda e, qi=qi, a=a: e.scalar_tensor_tensor(
                        out=ob[:, qi * 128:(qi + 1) * 128], in0=a[:, 0:128], scalar=zr[:, 4 + qi:5 + qi],
                        in1=o1[:, qi * 128:(qi + 1) * 128], op0=ALU.mult, op1=ALU.add),
                         reads=[r_, ("zr", 4), ("o1", qi)], writes=[("ob", qi)])

            def s_sq():
                for qi in range(4):
                    if on_act:
                        S.op("act", lambda e, qi=qi: e.activation(
                            out=junk[:, 0:128], in_=ob[:, qi * 128:(qi + 1) * 128], func=AF.Square,
                            accum_out=zr[:, 8 + qi:9 + qi]),
                             reads=[("ob", qi)], writes=["junk", ("zr", 8 + qi)])
                        continue
                    S.op("dve", lambda e, qi=qi: e.scalar_tensor_tensor(
                        out=sqj[:, 0:128], in0=ob[:, qi * 128:(qi + 1) * 128], scalar=1.0,
                        in1=ob[:, qi * 128:(qi + 1) * 128], op0=ALU.mult, op1=ALU.mult,
                        accum_out=zr[:, 8 + qi:9 + qi]),
                         reads=[("ob", qi)], writes=["sqj", ("zr", 8 + qi)])

            def s_rstd():
                S.op("act", lambda e: e.activation(out=zr[:, 8:12], in_=zr[:, 8:12], func=AF.Ln,
                                                   scale=1.0 / 128, bias=EPS),
                     reads=[("zr", 8 + q) for q in range(4)], writes=[("zr", 8 + q) for q in range(4)])
                S.op("act", lambda e: e.activation(out=zr[:, 12:16], in_=zr[:, 8:12], func=AF.Exp, scale=-0.5),
                     reads=[("zr", 8 + q) for q in range(4)], writes=[("zr", 13)])

            def s_on():
                for qi in range(4):
                    S.op("dve", lambda e, qi=qi: e.scalar_tensor_tensor(
                        out=on[:, qi * 128:(qi + 1) * 128], in0=ob[:, qi * 128:(qi + 1) * 128],
                        scalar=zr[:, 12 + qi:13 + qi], in1=gsub, op0=ALU.mult, op1=ALU.mult),
                         reads=[("ob", qi), ("zr", 13), "gsub"], writes=[("on", qi)])

            trb = {}

            def s_tr():
                trb["b"] = take_score_bank()
                pv = bankbf(trb["b"])
                for qi in range(4):
                    S.op("pe", lambda e, qi=qi: e.transpose(out=pv[:, qi * 128:(qi + 1) * 128],
                                                            in_=on[:, qi * 128:(qi + 1) * 128], identity=ident),
                         reads=[("on", qi), "ident"], writes=[psr(trb["b"])])

            def s_copy():
                pv = bankbf(trb["b"])
                if on_act:
                    S.op("act", lambda e: e.copy(out=mixA[:, h, qc * 512:(qc + 1) * 512], in_=pv[:, 0:512]),
                         reads=[psr(trb["b"])], writes=[("mixA", h, qc)])
                else:
                    S.op("dve", lambda e: e.tensor_copy(out=mixA[:, h, qc * 512:(qc + 1) * 512], in_=pv[:, 0:512]),
                         reads=[psr(trb["b"])], writes=[("mixA", h, qc)])
                release_score_bank(trb["b"])

            st_ = [] if single else [(0, s_combine)]
            return st_ + [(1, s_ob), (4 if on_act else 2, s_sq), (6, s_rstd), (9, s_on), (11, s_tr), (13, s_copy)]

        LAG = 2
        pending = {}
        LOOK = 4
        nq = {"n": 0}

        def fill_qk(cur):
            while free_sb and nq["n"] < NI and nq["n"] <= cur + LOOK:
                emit_qk(nq["n"])
                emit_bias(nq["n"])
                nq["n"] += 1

        fill_qk(0)
        for i in range(NI):
            emit_softmax(i)
            for fn in pending.pop(i, []):
                fn()
            fill_qk(i)
            emit_pv(i)
            if iters[i][5]:
                for off, fn in finalize_stages(iters[i][6]):
                    pending.setdefault(i + 1 + LAG + off, []).append(fn)
        for k in sorted(pending):
            for fn in pending[k]:
                fn()

        for nm in ("btab", "fgt", "numb0", "numb1", "sqj", "qz0", "qz1", "kT", "vaug", "absd", "sc0", "sc1", "sc2", "sc3", "sc4", "sc5", "ET0", "ET1", "ET2", "ET3", "ET4", "ET5", "o1", "ob", "on"):
            A.free(nm)

        wout = A.alloc("wout", 4096, BF16).rearrange("p (k c) -> p k c", k=8)
        mixL = A.alloc("mixL", 4096, BF16).rearrange("p (k t) -> p k t", k=4)
        for kt in range(8):
            S.dma("pool", wout[:, kt, :], w_out_d[kt * 128:(kt + 1) * 128, :], "w_out", writes=[("wout", kt)])

        xrp = A.alloc("xrp", 2056)
        gg = A.alloc("gg", 2048)
        xcs = [A.alloc("xc0", 2048, top=True), A.alloc("xc1", 2048)]
        xcb = A.alloc("xcb", 1024, BF16)
        TA = [A.alloc(f"TA{i}", 2048) for i in range(2)]
        TB = [A.alloc(f"TB{i}", 2048) for i in range(2)]
        TC0 = A.alloc("TC0", 2048)
        TD = [A.alloc(f"TD{i}", 2048) for i in range(2)]
        S.op("pool", lambda e: e.memset(xrp[:, 0:2], 0.0), writes=[("xrp", "pad0")])
        S.op("pool", lambda e: e.memset(xrp[:, 2050:2056], 0.0), writes=[("xrp", "pad1")])

        def lru_proj(col0, evac):
            for tq in range(4):
                b = next_bank()
                for kt in range(8):
                    S.op("pe", lambda e, b=b, kt=kt, tq=tq: e.matmul(
                        bank(b), lhsT=win_lru[:, kt, col0:col0 + 128], rhs=hT[:, kt, tq * 512:(tq + 1) * 512],
                        start=(kt == 0), stop=(kt == 7)),
                         reads=[("win_lru", kt)] + [("hT", kt, t) for t in range(tq * 4, tq * 4 + 4)],
                         writes=[psr(b)])
                evac(tq, b)

        def lru_A(ct):
            xc = xcs[ct % 2]
            xn = f"xc{ct % 2}"
            lru_proj(ct * 128, lambda tq, b: S.op(
                "act", lambda e: e.copy(out=xrp[:, 2 + tq * 512:2 + (tq + 1) * 512], in_=bank(b)),
                reads=[psr(b)], writes=[("xrp", tq)]))
            xr_all = [("xrp", t) for t in range(4)] + [("xrp", "pad0"), ("xrp", "pad1")]
            S.op("dve", lambda e: e.tensor_scalar(out=xc, in0=xrp[:, 0:2048], scalar1=cols[:, ct * 4:ct * 4 + 1],
                                                  scalar2=cols[:, 16 + ct:17 + ct], op0=ALU.mult, op1=ALU.add),
                 reads=xr_all + ["cols"], writes=[xn])
            for k in range(1, 4):
                S.op("dve", lambda e, k=k: e.scalar_tensor_tensor(
                    out=xc, in0=xrp[:, k:k + 2048], scalar=cols[:, ct * 4 + k:ct * 4 + k + 1], in1=xc,
                    op0=ALU.mult, op1=ALU.add),
                     reads=xr_all + ["cols", xn], writes=[xn])
            S.op("dve", lambda e: e.tensor_copy(out=xcb, in_=xc), reads=[xn], writes=["xcb"])

        def lru_G(ct):
            for d in range(2):
                ci = d * 4 + ct
                for gate, (dstt, dn, bcol) in enumerate(((TA[d], f"TA{d}", 20 + ci), (TB[d], f"TB{d}", 28 + ci))):
                    widx = (gate * 2 + d) * 4 + ct
                    for tq in range(4):
                        b = next_bank()
                        S.op("pe", lambda e, b=b, widx=widx, tq=tq: e.matmul(
                            bank(b), lhsT=wg[:, widx, :], rhs=xcb[:, tq * 512:(tq + 1) * 512], start=True, stop=True),
                             reads=["wg", "xcb"], writes=[psr(b)])
                        S.op("act", lambda e, b=b, dstt=dstt, tq=tq, bcol=bcol: e.activation(
                            out=dstt[:, tq * 512:(tq + 1) * 512], in_=bank(b), func=AF.Sigmoid,
                            bias=cols[:, bcol:bcol + 1]),
                             reads=[psr(b), "cols"], writes=[(dn, tq)])

        def lru_E(ct):
            xc = xcs[ct % 2]
            xn = f"xc{ct % 2}"
            xr_all = [("xrp", t) for t in range(4)]
            abuf = [TC0, xrp[:, 2:2050]]
            ares = [["TC0"], xr_all]
            R = [[(f"TA{d}", t) for t in range(4)] for d in range(2)]
            I = [[(f"TB{d}", t) for t in range(4)] for d in range(2)]
            for d in range(2):
                ci = d * 4 + ct
                S.op("act", lambda e, d=d, ci=ci: e.activation(out=abuf[d], in_=TA[d], func=AF.Exp,
                                                               scale=cA[:, ci:ci + 1]),
                     reads=R[d] + ["cA"], writes=ares[d])
            for d in range(2):
                ci = d * 4 + ct
                if d == 0:
                    S.op("act", lambda e, d=d, ci=ci: e.activation(out=TA[d], in_=TA[d], func=AF.Exp,
                                                                   scale=c2A[:, ci:ci + 1]),
                         reads=R[d] + ["c2A"], writes=R[d])
                else:
                    S.op("dve", lambda e, d=d: e.tensor_tensor(out=TA[d], in0=abuf[d], in1=abuf[d], op=ALU.mult),
                         reads=ares[d], writes=R[d])
                S.op("dve", lambda e, d=d: e.tensor_tensor(out=TB[d], in0=TB[d], in1=xc, op=ALU.mult),
                     reads=I[d] + [xn], writes=I[d])
            for d in range(2):
                S.op("act", lambda e, d=d: e.activation(out=TA[d], in_=TA[d], func=AF.Sqrt, scale=-1.0, bias=1.0),
                     reads=R[d], writes=R[d])
            for d in range(2):
                S.op("dve", lambda e, d=d: e.tensor_tensor(out=TB[d], in0=TB[d], in1=TA[d], op=ALU.mult),
                     reads=I[d] + R[d], writes=I[d])
                if d == 0:
                    S.op("dve", lambda e: e.tensor_tensor_scan(
                        out=TD[0], data0=abuf[0], data1=TB[0], initial=0.0, op0=ALU.mult, op1=ALU.add),
                         reads=ares[0] + I[0], writes=["TD0"])
                else:
                    S.op("dve", lambda e: e.tensor_tensor_scan(
                        out=TD[1][:, ::-1], data0=abuf[1][:, ::-1], data1=TB[1][:, ::-1], initial=0.0,
                        op0=ALU.mult, op1=ALU.add),
                         reads=ares[1] + I[1], writes=["TD1"])

        def lru_C(ct):
            lru_proj(512 + ct * 128, lambda tq, b: S.op(
                "act", lambda e: e.activation(out=gg[:, tq * 512:(tq + 1) * 512], in_=bank(b), func=AF.Gelu_apprx_tanh),
                reads=[psr(b)], writes=[("gg", tq)]))

        def lru_D(ct):
            S.op("dve", lambda e: e.tensor_tensor(out=TD[0], in0=TD[0], in1=TD[1], op=ALU.add),
                 reads=["TD0", "TD1"], writes=["TD0"])
            S.op("dve", lambda e: e.tensor_tensor(out=mixL[:, ct, :], in0=TD[0], in1=gg, op=ALU.mult),
                 reads=["TD0"] + [("gg", t) for t in range(4)], writes=[("mixL", ct, q) for q in range(4)])

        lru_A(0)
        for ct in range(4):
            lru_G(ct)
            if ct + 1 < 4:
                lru_A(ct + 1)
            lru_E(ct)
            lru_C(ct)
            lru_D(ct)
        for nm in ("xrp", "gg", "xc0", "xc1", "xcb", "TA0", "TA1", "TB0", "TB1", "TC0", "TD0", "TD1", "win_lru", "wg"):
            A.free(nm)

        x1t = [None] * NT
        for tt_ in range(NT - 1, -1, -1):
            x1t[tt_] = A.alloc(f"x1t{tt_}", 1024, top=True)
        groups = [(0, 6), (6, 12), (12, 16), (16, 22)]
        wup = [A.alloc(f"wup{i}", 2048, BF16, top=True) for i in range(2)]
        wup3 = [w.rearrange("p (k c) -> p k c", k=8) for w in wup]
        wdn = [A.alloc(f"wdn{i}", 3072, BF16, top=True).rearrange("p (j c) -> p j c", j=6) for i in range(2)]
        w_up_v = w_up_d.rearrange("(k p) c -> p k c", p=128)
        w_dn_v = w_dn_d.rearrange("(j p) c -> p j c", p=128)

        def load_wup(jp):
            sl = jp % 2
            c0 = jp * 256
            for kt in range(8):
                S.dma("pool", wup3[sl][:, kt, 0:256], w_up_d[kt * 128:(kt + 1) * 128, c0:c0 + 256], f"wup{sl}g",
                      writes=[(f"wup{sl}", kt, 0)])
                S.dma("pool", wup3[sl][:, kt, 256:512], w_up_d[kt * 128:(kt + 1) * 128, DFF + c0:DFF + c0 + 256],
                      f"wup{sl}v", writes=[(f"wup{sl}", kt, 1)])

        def load_wdn(g):
            j0, j1 = groups[g]
            sl = g % 2
            for jj in range(j1 - j0):
                S.dma("pool", wdn[sl][:, jj, :], w_dn_d[(j0 + jj) * 128:(j0 + jj + 1) * 128, :], f"wdn{sl}",
                      writes=[(f"wdn{sl}", jj)])

        load_wup(0)
        load_wup(1)
        load_wdn(0)
        c_order = list(range(NT - 1, -1, -1))
        S.dma("sp", gb, gvec_d[1:2, :].to_broadcast([128, D]), "c_gb", writes=["gb"])
        for tt in c_order:
            S.dma("sp", x1t[tt], x_d[tt * 128:(tt + 1) * 128, :], f"x1_{tt}", writes=[f"x1t{tt}"])
        hnC = [A.alloc(f"hnC{i}", 512, BF16) for i in range(3)]
        def c_mm(tt, cc, b, kts):
            for kt in kts:
                mx, mname = (mixA, "mixA") if kt < 4 else (mixL, "mixL")
                S.op("pe", lambda e, kt=kt, mx=mx: e.matmul(
                    bank(b), lhsT=mx[:, kt % 4, tt * 128:(tt + 1) * 128], rhs=wout[:, kt, cc * 512:(cc + 1) * 512],
                    start=(kt == 0), stop=(kt == 7)),
                     reads=[(mname, kt % 4, tt // 4), ("wout", kt)], writes=[psr(b)])

        def c_add(tt, cc, b):
            S.op("dve", lambda e: e.tensor_tensor(
                out=x1t[tt][:, cc * 512:(cc + 1) * 512], in0=bank(b), in1=x1t[tt][:, cc * 512:(cc + 1) * 512],
                op=ALU.add), reads=[psr(b), f"x1t{tt}"], writes=[f"x1t{tt}"])

        head = c_order[:4]
        hb_ = {}
        for tt in head:
            for cc in range(2):
                hb_[(tt, cc)] = next_bank()
                c_mm(tt, cc, hb_[(tt, cc)], range(7))
        def c_norm_dve(n_):
            tt = c_order[n_]
            sl = n_ % 3
            norm_dve(x1t[tt], [f"x1t{tt}"], hnC[sl], f"hnC{sl}", ss1[:, tt:tt + 1], rstd1[:, tt:tt + 1], ("n2", tt))

        def c_tr(n_):
            tt = c_order[n_]
            sl = n_ % 3
            transpose_part(tt, hT, "hT", hnC[sl], f"hnC{sl}")

        for n_, tt in enumerate(c_order):
            for cc in range(2):
                if tt in head:
                    b = hb_[(tt, cc)]
                    c_mm(tt, cc, b, [7])
                else:
                    b = next_bank()
                    c_mm(tt, cc, b, range(8))
                c_add(tt, cc, b)
            rms_act(x1t[tt], [f"x1t{tt}"], ss1[:, tt:tt + 1], ("n2", tt), 1.0 / D)
            if n_ >= 1:
                c_norm_dve(n_ - 1)
            if n_ >= 2:
                c_tr(n_ - 2)
        c_norm_dve(NT - 1)
        c_tr(NT - 2)
        c_tr(NT - 1)
        if "mixT" in debug_taps:
            taps["mixT"] = nc.dram_tensor("tap_mixT", [128, 8 * S_TOK], F32, kind="ExternalOutput").ap()
            for kt in range(8):
                mx, mname = (mixA, "mixA") if kt < 4 else (mixL, "mixL")
                S.dma("pool", taps["mixT"][:, kt * S_TOK:(kt + 1) * S_TOK], mx[:, kt % 4, :], "out",
                      reads=[(mname, kt % 4, q) for q in range(4)])
        A.free("mixA")
        A.free("mixL")
        for i in range(3):
            A.free(f"hnC{i}")
        A.free("wout")
        if "x1" in debug_taps:
            taps["x1"] = nc.dram_tensor("tap_x1", [128, NT * D], F32, kind="ExternalOutput").ap()
            for t in range(NT):
                S.dma("sp", taps["x1"][:, t * D:(t + 1) * D], x1t[t], "out", reads=[f"x1t{t}"])

        S.dma("sp", gb, gvec_d[2:3, :].to_broadcast([128, D]), "c_gb", writes=["gb"])
        actT = A.alloc("actT", 7168, BF16).rearrange("p (j t) -> p j t", j=7)
        cgb = [A.alloc(f"cg{i}", 2048) for i in range(2)]
        cvb = [A.alloc(f"cv{i}", 2048) for i in range(2)]
        outb = None
        U_g = P[:, 0:2048]
        U_v = P[:, 2048:4096]
        NSLOT = 7

        def up_tile(j):
            jp = j // 2
            sl = jp % 2
            cj = (j % 2) * 128
            bs = j % 2
            cg, cv = cgb[bs], cvb[bs]
            slot = j % NSLOT
            for half, (U, boff, wc0, cbuf, cname, jt) in enumerate((
                    (U_g, 0, cj, cg, f"cg{bs}", j), (U_v, 4, 256 + cj, cv, f"cv{bs}", NJ + j))):
                for tq in range(4):
                    b = boff + tq
                    for kt in range(8):
                        S.op("pe", lambda e, b=b, kt=kt, tq=tq, wc0=wc0: e.matmul(
                            bank(b), lhsT=wup3[sl][:, kt, wc0:wc0 + 128], rhs=hT[:, kt, tq * 512:(tq + 1) * 512],
                            start=(kt == 0), stop=(kt == 7)),
                             reads=[(f"wup{sl}", kt, half)] + [("hT", kt, t) for t in range(tq * 4, tq * 4 + 4)],
                             writes=[psr(b)])
                rb = [psr(boff + t) for t in range(4)]
                w0 = cols[:, 44 + jt * 3:45 + jt * 3]
                w1 = cols[:, 45 + jt * 3:46 + jt * 3]
                w2 = cols[:, 46 + jt * 3:47 + jt * 3]
                bb = cols[:, 176 + jt:177 + jt]
                S.op("act", lambda e, U=U, cbuf=cbuf, w1=w1, bb=bb: e.activation(
                    out=cbuf, in_=U, func=AF.Identity, scale=w1, bias=bb),
                     reads=rb + ["cols"], writes=[cname])
                if half == 1:
                    S.op("act", lambda e: e.activation(out=cg, in_=cg, func=AF.Gelu_apprx_tanh),
                         reads=[f"cg{bs}"], writes=[f"cg{bs}"])
                S.op("dve", lambda e, U=U, cbuf=cbuf, w0=w0: e.scalar_tensor_tensor(
                    out=cbuf[:, 1:2048], in0=U[:, 0:2047], scalar=w0, in1=cbuf[:, 1:2048],
                    op0=ALU.mult, op1=ALU.add), reads=rb + ["cols", cname], writes=[cname])
                S.op("dve", lambda e, U=U, cbuf=cbuf, w2=w2: e.scalar_tensor_tensor(
                    out=cbuf[:, 0:2047], in0=U[:, 1:2048], scalar=w2, in1=cbuf[:, 0:2047],
                    op0=ALU.mult, op1=ALU.add), reads=rb + ["cols", cname], writes=[cname])
            if j % 2 == 1 and jp + 2 < 11:
                load_wup(jp + 2)
            S.op("dve", lambda e: e.tensor_tensor(out=actT[:, slot, :], in0=cg, in1=cv, op=ALU.mult),
                 reads=[f"cg{bs}", f"cv{bs}"], writes=[("actT", slot)])

        def fin_out(tt):
            osl = tt % 4
            ob_ = outb_box[0]
            rms_dve(ss3[:, tt:tt + 1], rstd3[:, tt:tt + 1], ("n3", tt))
            S.op("dve", lambda e: e.scalar_tensor_tensor(
                out=ob_[osl], in0=x1t[tt], scalar=rstd3[:, tt:tt + 1], in1=gb, op0=ALU.mult, op1=ALU.mult),
                 reads=[f"x1t{tt}", (("n3", tt), "rstd"), "gb"], writes=[f"outb{osl}"])
            S.dma("sp", out_d[tt * 128:(tt + 1) * 128, :], ob_[osl], f"out{osl}", reads=[f"outb{osl}"])

        def down_partial(g):
            nonlocal_outb = outb_box
            j0, j1 = groups[g]
            last = (g == len(groups) - 1)
            nj = j1 - j0
            sl = g % 2
            for tt in range(NT):
                for cc in range(2):
                    b = next_bank()
                    for jj in range(nj):
                        slot = (j0 + jj) % NSLOT
                        S.op("pe", lambda e, b=b, jj=jj, tt=tt, cc=cc, slot=slot: e.matmul(
                            bank(b), lhsT=actT[:, slot, tt * 128:(tt + 1) * 128],
                            rhs=wdn[sl][:, jj, cc * 512:(cc + 1) * 512],
                            start=(jj == 0), stop=(jj == nj - 1)),
                             reads=[("actT", slot), (f"wdn{sl}", jj)], writes=[psr(b)])
                    S.op("dve", lambda e, b=b, tt=tt, cc=cc: e.tensor_tensor(
                        out=x1t[tt][:, cc * 512:(cc + 1) * 512], in0=bank(b), in1=x1t[tt][:, cc * 512:(cc + 1) * 512],
                        op=ALU.add), reads=[psr(b), f"x1t{tt}"], writes=[f"x1t{tt}"])
                if last:
                    rms_act(x1t[tt], [f"x1t{tt}"], ss3[:, tt:tt + 1], ("n3", tt), 1.0 / D)
                    if tt >= 1:
                        fin_out(tt - 1)
            if last:
                fin_out(NT - 1)

        outb_box = [None]
        NG = len(groups)
        for j in range(groups[0][0], groups[0][1]):
            up_tile(j)
        for g in range(NG):
            if g + 1 < NG:
                load_wdn(g + 1)
                up_tile(groups[g + 1][0])
            else:
                for nm in ("cg0", "cg1", "cv0", "cv1"):
                    A.free(nm)
                outb_box[0] = [A.alloc(f"outb{i}", 1024) for i in range(4)]
            down_partial(g)
            if g + 1 < NG:
                for j in range(groups[g + 1][0] + 1, groups[g + 1][1]):
                    up_tile(j)
        S.emit(final_waits=[k for k in ("out", "out0", "out1", "out2", "out3") if k in S.dma_sems])
    return nc, S


_CACHE = {}


def _pack_inputs(inp):
    f = lambda a: np.ascontiguousarray(np.asarray(a, dtype=np.float32))
    cols = np.zeros((128, 220), np.float32)
    cw = f(inp["lru_conv_w"])[0]
    cb = f(inp["lru_conv_b"])[0]
    for ct in range(4):
        for k in range(4):
            cols[:, ct * 4 + k] = cw[k, ct * 128:(ct + 1) * 128]
        cols[:, 16 + ct] = cb[ct * 128:(ct + 1) * 128]
    ba = f(inp["lru_b_a"])[0]
    bx = f(inp["lru_b_x"])[0]
    lm = f(inp["lru_lambda"])[0]
    for d in range(2):
        for ct in range(4):
            cols[:, 20 + d * 4 + ct] = ba[d, ct * 128:(ct + 1) * 128]
            cols[:, 28 + d * 4 + ct] = bx[d, ct * 128:(ct + 1) * 128]
            cols[:, 36 + d * 4 + ct] = lm[d, ct * 128:(ct + 1) * 128]
    fw_ = f(inp["ffn_conv_w"])[0]
    fb_ = f(inp["ffn_conv_b"])[0]
    for jt in range(44):
        for k in range(3):
            cols[:, 44 + jt * 3 + k] = fw_[k, jt * 128:(jt + 1) * 128]
        cols[:, 176 + jt] = fb_[jt * 128:(jt + 1) * 128]
    rows = np.concatenate([f(inp["lambda_q1"])[0], f(inp["lambda_k1"])[0], f(inp["lambda_q2"])[0],
                           f(inp["lambda_k2"])[0], f(inp["subln_g"])[0]])[None, :]
    gvec = np.stack([f(inp["attn_norm_g"])[0], f(inp["ffn_norm_g"])[0], f(inp["final_norm_g"])], 0)
    wa = f(inp["lru_w_a"])[0]
    wx = f(inp["lru_w_x"])[0]
    wg = np.zeros((16, 128, 128), np.float32)
    for gate, w in enumerate((wa, wx)):
        for d in range(2):
            for ct in range(4):
                idx = (gate * 2 + d) * 4 + ct
                wg[idx, 0:64, 0:64] = w[d, 2 * ct]
                wg[idx, 64:128, 64:128] = w[d, 2 * ct + 1]
    shared = {
        "w_in": f(inp["w_in"])[0], "w_out": f(inp["w_out"])[0], "w_up": f(inp["w_up"])[0],
        "w_down": f(inp["w_down"])[0], "wg": wg, "cols": cols, "rows": np.ascontiguousarray(rows),
        "gvec": np.ascontiguousarray(gvec),
    }
    return shared


def kernel(**inputs):
    x = np.asarray(inputs["x"], dtype=np.float32)
    B = x.shape[0]
    shared = _pack_inputs(inputs)
    taps = tuple(t for t in os.environ.get("KTAPS", "").split(",") if t)
    key = ("nc", taps)
    if key not in _CACHE:
        _CACHE[key] = build(debug_taps=taps)
    nc, _ = _CACHE[key]
    in_maps = []
    for b in range(B):
        m = dict(shared)
        m["x"] = np.ascontiguousarray(x[b])
        in_maps.append(m)
    res = run_bass_kernel_spmd(nc, in_maps, core_ids=list(range(B)))
    out = np.stack([np.asarray(r["out"], dtype=np.float32) for r in res.results], 0)
    if taps:
        kernel.last_taps = [{k: v for k, v in r.items() if k.startswith("tap_")} for r in res.results]
    return out
```

```python
import os
from contextlib import ExitStack
import numpy as np
import concourse.bass as bass
import concourse.mybir as mybir
from concourse.bass_utils import run_bass_kernel_spmd

F32 = mybir.dt.float32
BF16 = mybir.dt.bfloat16
AF = mybir.ActivationFunctionType
ALU = mybir.AluOpType

ENGS = ("pe", "act", "dve", "pool", "sp")
S_TOK = 2048
D = 1024
NT = 16
DFF = 2816
NJ = 22
EPS = 1e-6
LAMBDA_INIT = 0.8 - 0.6 * 1.0


class Op:
    __slots__ = ("eng", "fn", "deps", "signal", "count", "is_dma", "dma_sem", "dma_key", "dma_target", "name")

    def __init__(self, eng, fn, name=""):
        self.eng = eng
        self.fn = fn
        self.deps = []
        self.signal = False
        self.count = None
        self.is_dma = False
        self.dma_sem = None
        self.dma_key = None
        self.dma_target = 0
        self.name = name


def _buf(r):
    return r[0] if isinstance(r, tuple) else r


class Sched:
    def __init__(self, nc, es):
        self.nc = nc
        self.es = es
        self.per_eng = {e: [] for e in ENGS}
        self.last_writer = {}
        self.readers = {}
        self.eng_sem = {e: es.enter_context(nc.semaphore("prog_" + e)) for e in ENGS if e != "sp"}
        self.dma_sems = {}
        self.dma_counts = {}
        self.buf_deps = {}
        self.nops = 0

    def _add_dep(self, o, d):
        if d is o:
            return
        if d.is_dma:
            o.deps.append((d, 16 * self.dma_counts[d.dma_key]))
            return
        if d.eng == o.eng and o.eng == "pe":
            return
        d.signal = True
        o.deps.append((d, None))

    NON_ARENA = ("ps", "psu", "lam", "zr", "ecol", "pcol", "cA", "c2A", "wupb", "wdnb")

    def _check(self, rs):
        for r in rs:
            b = _buf(r)
            if isinstance(b, tuple) or b in self.NON_ARENA or b in self.buf_deps:
                continue
            raise RuntimeError(f"resource {r!r} is not an arena buffer")

    def op(self, eng, fn, reads=(), writes=(), name=""):
        self._check(reads)
        self._check(writes)
        o = Op(eng, fn, name)
        deps = {}
        for r in reads:
            w = self.last_writer.get(r)
            if w is not None:
                deps[id(w)] = w
        for r in writes:
            w = self.last_writer.get(r)
            if w is not None:
                deps[id(w)] = w
            for rd in self.readers.get(r, {}).values():
                deps[id(rd)] = rd
        for r in list(reads) + list(writes):
            for d in self.buf_deps.get(_buf(r), ()):
                deps[id(d)] = d
        for d in deps.values():
            self._add_dep(o, d)
        for r in reads:
            rd = self.readers.setdefault(r, {})
            key = ("dma", id(o)) if False else eng
            rd[key] = o
        for r in writes:
            self.last_writer[r] = o
            self.readers[r] = {}
        self.per_eng[eng].append(o)
        self.nops += 1
        return o

    def dma(self, queue, out, in_, semkey, reads=(), writes=(), name="", **kw):
        if semkey not in self.dma_sems:
            self.dma_sems[semkey] = self.es.enter_context(self.nc.semaphore("dma_" + semkey))
            self.dma_counts[semkey] = 0

        def fn(e):
            return e.dma_start(out=out, in_=in_, **kw)

        self._check(reads)
        self._check(writes)
        o = Op(queue, fn, name)
        o.is_dma = True
        o.dma_key = semkey
        o.dma_sem = self.dma_sems[semkey]
        deps = {}
        for r in reads:
            w = self.last_writer.get(r)
            if w is not None:
                deps[id(w)] = w
        for r in writes:
            w = self.last_writer.get(r)
            rds = self.readers.get(r, {})
            if w is not None and not (w.is_dma and rds):
                deps[id(w)] = w
            for rd in rds.values():
                deps[id(rd)] = rd
        for r in list(reads) + list(writes):
            for d in self.buf_deps.get(_buf(r), ()):
                deps[id(d)] = d
        for d in deps.values():
            self._add_dep(o, d)
        self.dma_counts[semkey] += 1
        o.dma_target = 16 * self.dma_counts[semkey]
        for r in reads:
            self.readers.setdefault(r, {})[("dma", semkey)] = o
        for r in writes:
            self.last_writer[r] = o
            self.readers[r] = {}
        self.per_eng[queue].append(o)
        return o

    def ops_touching(self, bufname):
        out = {}
        for r, w in self.last_writer.items():
            if _buf(r) == bufname:
                out[id(w)] = w
        for r, rd in self.readers.items():
            if _buf(r) == bufname:
                for o in rd.values():
                    out[id(o)] = o
        return list(out.values())

    def finalize(self):
        for e in ENGS:
            c = 0
            for o in self.per_eng[e]:
                if o.is_dma:
                    continue
                if o.signal:
                    c += 1
                    o.count = c

    def emit_engine(self, ename, e):
        known = {}
        for o in self.per_eng[ename]:
            waits = {}
            for d, ov in o.deps:
                if d.is_dma:
                    key = ("dma", d.dma_key)
                    sem, val = d.dma_sem, ov
                else:
                    key = ("eng", d.eng)
                    sem, val = self.eng_sem[d.eng], d.count
                if known.get(key, 0) >= val:
                    continue
                if key not in waits or waits[key][1] < val:
                    waits[key] = (sem, val)
            for key, (sem, val) in waits.items():
                e.wait_ge(sem, val)
                known[key] = val
            ins = o.fn(e)
            if o.is_dma:
                ins.then_inc(o.dma_sem, 16)
            elif o.signal:
                ins.then_inc(self.eng_sem[ename], 1)

    def emit(self, final_waits=()):
        self.finalize()
        nc = self.nc
        with nc.Block() as block:
            @block.tensor
            def _(e):
                self.emit_engine("pe", e)

            @block.scalar
            def _(e):
                self.emit_engine("act", e)

            @block.vector
            def _(e):
                self.emit_engine("dve", e)

            @block.gpsimd
            def _(e):
                self.emit_engine("pool", e)

            @block.sync
            def _(e):
                self.emit_engine("sp", e)
                for k in final_waits:
                    e.wait_ge(self.dma_sems[k], 16 * self.dma_counts[k])


class Arena:
    def __init__(self, S, tensor, size):
        self.S = S
        self.t = tensor
        self.size = size
        self.live = {}
        self.retired = []

    def alloc(self, name, n, dt=F32, top=False):
        n = (n + 7) // 8 * 8
        spans = sorted(self.live.values())
        gaps = []
        pos = 0
        for (o, m) in spans:
            if o - pos >= n:
                gaps.append((pos, o))
            pos = max(pos, o + m)
        if self.size - pos >= n:
            gaps.append((pos, self.size))
        if not gaps:
            raise RuntimeError(f"arena full allocating {name} ({n}); live={self.live}")
        if top:
            off = gaps[-1][1] - n
        else:
            off = gaps[0][0]
        self.live[name] = (off, n)
        deps = []
        for (o, m, ops) in self.retired:
            if o < off + n and off < o + m:
                deps.extend(ops)
        self.S.buf_deps[name] = deps
        v = self.t[:, off:off + n]
        return v if dt == F32 else v.bitcast(dt)

    def free(self, name):
        off, n = self.live.pop(name)
        self.retired.append((off, n, self.S.ops_touching(name)))


def build(debug_taps=()):
    nc = bass.Bass("TRN2", target_bir_lowering=False)
    dram_in = lambda n, s: nc.dram_tensor(n, list(s), F32, kind="ExternalInput").ap()
    x_d = dram_in("x", [S_TOK, D])
    w_in_d = dram_in("w_in", [D, 2560])
    w_out_d = dram_in("w_out", [D, D])
    w_up_d = dram_in("w_up", [D, 2 * DFF])
    w_dn_d = dram_in("w_down", [DFF, D])
    wg_d = dram_in("wg", [16, 128, 128])
    cols_d = dram_in("cols", [128, 220])
    rows_d = dram_in("rows", [1, 384])
    gvec_d = dram_in("gvec", [3, D])
    out_d = nc.dram_tensor("out", [S_TOK, D], F32, kind="ExternalOutput").ap()
    taps = {}

    es = ExitStack()
    with es:
        S = Sched(nc, es)
        ARENA_N = 53200
        arena_t = es.enter_context(nc.sbuf_tensor("arena", [128, ARENA_N], F32))
        A = Arena(S, arena_t, ARENA_N)
        P = es.enter_context(nc.psum_tensor("P", [128, 4096], F32))

        def bank(b):
            return P[:, b * 512:(b + 1) * 512]

        def bankbf(b):
            return P[:, b * 512:(b + 1) * 512].bitcast(BF16)

        def psr(b):
            return ("ps", b)

        rot = {"i": 0}

        def next_bank(lo=0, hi=8):
            n = hi - lo
            b = lo + rot["i"] % n
            rot["i"] += 1
            return b

        cols = A.alloc("cols", 224)
        rowsb = A.alloc("rowsb", 384)
        gb = A.alloc("gb", 1024)
        ident = A.alloc("ident", 64, BF16)
        zeros_bf = A.alloc("zeros_bf", 64, BF16)
        identf = A.alloc("identf", 128)
        st = A.alloc("stats", 256)
        ss1 = st[:, 0:16]
        rstd1 = st[:, 16:32]
        lam_s = st[:, 32:40]
        cA = st[:, 40:48]
        c2A = st[:, 48:56]
        ecol = st[:, 56:64]
        pcol = st[:, 64:72]
        zr = st[:, 72:88]
        ss3 = st[:, 88:104]
        rstd3 = st[:, 104:120]
        gsub = A.alloc("gsub", 128)
        junk = A.alloc("junk", 512, BF16)

        S.dma("sp", cols[:, 0:220], cols_d, "c_cols", writes=["cols"])
        S.dma("sp", rowsb, rows_d[0:1, :].to_broadcast([128, 384]), "c_rows", writes=["rowsb"])
        S.dma("sp", gb, gvec_d[0:1, :].to_broadcast([128, D]), "c_gb", writes=["gb"])

        S.op("pool", lambda e: e.iota(identf, pattern=[[1, 128]], base=0, channel_multiplier=-1,
                                      allow_small_or_imprecise_dtypes=True), writes=["identf"])
        S.op("dve", lambda e: e.tensor_single_scalar(out=ident, in_=identf, scalar=0.0, op=ALU.is_equal),
             reads=["identf"], writes=["ident"])

        S.op("pool", lambda e: e.memset(zeros_bf, 0.0), writes=["zeros_bf"])
        S.op("dve", lambda e: e.scalar_tensor_tensor(out=junk[:, 0:64], in0=rowsb[:, 0:64], scalar=1.0,
                                                     in1=rowsb[:, 64:128], op0=ALU.mult, op1=ALU.mult,
                                                     accum_out=lam_s[:, 0:1]),
             reads=["rowsb"], writes=["junk", ("lam", 0)])
        S.op("dve", lambda e: e.scalar_tensor_tensor(out=junk[:, 64:128], in0=rowsb[:, 128:192], scalar=1.0,
                                                     in1=rowsb[:, 192:256], op0=ALU.mult, op1=ALU.mult,
                                                     accum_out=lam_s[:, 1:2]),
             reads=["rowsb"], writes=["junk", ("lam", 1)])
        S.op("act", lambda e: e.activation(out=lam_s[:, 2:4], in_=lam_s[:, 0:2], func=AF.Exp),
             reads=[("lam", 0), ("lam", 1)], writes=[("lam", 2)])
        S.op("dve", lambda e: e.tensor_tensor(out=lam_s[:, 4:5], in0=lam_s[:, 3:4], in1=lam_s[:, 2:3],
                                              op=ALU.subtract), reads=[("lam", 2)], writes=[("lam", 4)])
        S.op("dve", lambda e: e.tensor_scalar(out=lam_s[:, 4:5], in0=lam_s[:, 4:5], scalar1=-LAMBDA_INIT,
                                              scalar2=None, op0=ALU.add), reads=[("lam", 4)], writes=[("lam", 4)])
        neglam = lam_s[:, 4:5]
        S.op("dve", lambda e: e.tensor_scalar(out=gsub, in0=rowsb[:, 256:384], scalar1=1.0 - LAMBDA_INIT,
                                              scalar2=None, op0=ALU.mult), reads=["rowsb"], writes=["gsub"])
        S.op("act", lambda e: e.activation(out=ecol, in_=cols[:, 36:44], func=AF.Exp, scale=-1.0),
             reads=["cols"], writes=["ecol"])
        S.op("dve", lambda e: e.tensor_scalar(out=pcol, in0=ecol, scalar1=1.0 / 7.0, scalar2=None, op0=ALU.mult),
             reads=["ecol"], writes=["pcol"])
        for cst in (-1.0 / 6, 1.0 / 5, -1.0 / 4, 1.0 / 3, -1.0 / 2, 1.0):
            S.op("dve", lambda e, cst=cst: e.scalar_tensor_tensor(out=pcol, in0=pcol, scalar=float(cst), in1=ecol,
                                                                  op0=ALU.add, op1=ALU.mult),
                 reads=["pcol", "ecol"], writes=["pcol"])
        S.op("dve", lambda e: e.tensor_scalar(out=cA, in0=pcol, scalar1=-8.0, scalar2=None, op0=ALU.mult),
             reads=["pcol"], writes=["cA"])
        S.op("dve", lambda e: e.tensor_scalar(out=c2A, in0=pcol, scalar1=-16.0, scalar2=None, op0=ALU.mult),
             reads=["pcol"], writes=["c2A"])

        hT = A.alloc("hT", 8192, BF16).rearrange("p (k t) -> p k t", k=8)
        mixA = A.alloc("mixA", 4096, BF16).rearrange("p (k t) -> p k t", k=4)
        win_qkv = A.alloc("win_qkv", 6144, BF16).rearrange("p (k c) -> p k c", k=8)
        win_lru = A.alloc("win_lru", 4096, BF16).rearrange("p (k c) -> p k c", k=8)
        wg = A.alloc("wg", 1024, BF16).rearrange("p (n c) -> p n c", n=16)

        w_in_v = w_in_d.rearrange("(k p) c -> p k c", p=128)
        for kt in range(8):
            S.dma("pool", win_qkv[:, kt, :], w_in_d[kt * 128:(kt + 1) * 128, 0:1536], "w_qkv",
                  writes=[("win_qkv", kt)])

        def rms_rstd(src_ap, src_res, ss_col, rstd_col, tag, inv_n):
            n = src_ap.shape[-1]
            S.op("act", lambda e: e.activation(out=junk[:, 0:n], in_=src_ap, func=AF.Square, accum_out=ss_col),
                 reads=list(src_res), writes=["junk", (tag, "ss")])
            S.op("act", lambda e: e.activation(out=ss_col, in_=ss_col, func=AF.Sqrt, scale=inv_n, bias=EPS),
                 reads=[(tag, "ss")], writes=[(tag, "ss")])
            S.op("dve", lambda e: e.reciprocal(out=rstd_col, in_=ss_col), reads=[(tag, "ss")], writes=[(tag, "rstd")])

        def rms_act(src_ap, src_res, ss_col, tag, inv_n):
            n = src_ap.shape[-1]
            S.op("act", lambda e: e.activation(out=junk[:, 0:n], in_=src_ap, func=AF.Square, accum_out=ss_col),
                 reads=list(src_res), writes=["junk", (tag, "ss")])
            S.op("act", lambda e: e.activation(out=ss_col, in_=ss_col, func=AF.Sqrt, scale=inv_n, bias=EPS),
                 reads=[(tag, "ss")], writes=[(tag, "ss")])

        def rms_dve(ss_col, rstd_col, tag):
            S.op("dve", lambda e: e.reciprocal(out=rstd_col, in_=ss_col), reads=[(tag, "ss")], writes=[(tag, "rstd")])

        def norm_dve(src_ap, src_res, hn_slot, hn_res, ss_col, rstd_col, tag):
            rms_dve(ss_col, rstd_col, tag)
            S.op("dve", lambda e: e.scalar_tensor_tensor(out=hn_slot, in0=src_ap, scalar=rstd_col, in1=gb,
                                                         op0=ALU.mult, op1=ALU.mult),
                 reads=list(src_res) + [(tag, "rstd"), "gb"], writes=[hn_res])

        def norm_part(src_ap, src_res, hn_slot, hn_res, ss_col, rstd_col, tag):
            rms_rstd(src_ap, src_res, ss_col, rstd_col, tag, 1.0 / D)
            S.op("dve", lambda e: e.scalar_tensor_tensor(out=hn_slot, in0=src_ap, scalar=rstd_col, in1=gb,
                                                         op0=ALU.mult, op1=ALU.mult),
                 reads=list(src_res) + [(tag, "rstd"), "gb"], writes=[hn_res])

        def transpose_part(tt, dstT, dst_name, hn_slot, hn_res):
            b = next_bank()
            pv = bankbf(b).rearrange("p (k t) -> p k t", k=8)
            for kt in range(8):
                S.op("pe", lambda e, kt=kt: e.transpose(out=pv[:, kt, :], in_=hn_slot[:, kt * 128:(kt + 1) * 128],
                                                        identity=ident),
                     reads=[hn_res, "ident"], writes=[psr(b)])
            S.op("act", lambda e: e.copy(out=dstT[:, :, tt * 128:(tt + 1) * 128], in_=pv),
                 reads=[psr(b)], writes=[(dst_name, k, tt) for k in range(8)])

        qz = [A.alloc(f"qz{c}", 4096, BF16).rearrange("p (h t) -> p h t", h=4) for c in range(2)]
        kT = A.alloc("kT", 4096, BF16).rearrange("p (h t) -> p h t", h=4)
        vaug = A.alloc("vaug", 4160, BF16).rearrange("p (t h e) -> p t h e", t=16, h=4)
        absd = A.alloc("absd", 3968)
        S.op("pool", lambda e: e.memset(vaug[:, :, :, 128:130], 1.0), writes=[("vaug", "init")])
        S.op("pool", lambda e: e.memset(qz[0][64:128, :, :].rearrange("p h t -> p (h t)"), 0.0), writes=[("qz0", "z")])
        S.op("pool", lambda e: e.memset(qz[1][0:64, :, :].rearrange("p h t -> p (h t)"), 0.0), writes=[("qz1", "z")])

        ev = {"i": 0}

        def unit_qk(h, which, tq):
            col0 = h * 128 if which == "q" else 512 + h * 128
            b = next_bank()
            for kt in range(8):
                S.op("pe", lambda e, kt=kt: e.matmul(
                    bank(b), lhsT=win_qkv[:, kt, col0:col0 + 128], rhs=hT[:, kt, tq * 512:(tq + 1) * 512],
                    start=(kt == 0), stop=(kt == 7)),
                     reads=[("win_qkv", kt)] + [("hT", kt, t) for t in range(tq * 4, tq * 4 + 4)],
                     writes=[psr(b)])
            if which == "q":
                for c in range(2):
                    o_ap = qz[c][c * 64:(c + 1) * 64, h, tq * 512:(tq + 1) * 512]
                    i_ap = bank(b)[c * 64:(c + 1) * 64, :]
                    if (ev["i"] + c) % 2 == 0:
                        S.op("act", lambda e, o_ap=o_ap, i_ap=i_ap: e.activation(
                            out=o_ap, in_=i_ap, func=AF.Identity, scale=0.125),
                             reads=[psr(b), (f"qz{c}", "z")], writes=[(f"qz{c}", h, tq)])
                    else:
                        S.op("dve", lambda e, o_ap=o_ap, i_ap=i_ap: e.tensor_scalar(
                            out=o_ap, in0=i_ap, scalar1=0.125, scalar2=None, op0=ALU.mult),
                             reads=[psr(b), (f"qz{c}", "z")], writes=[(f"qz{c}", h, tq)])
            else:
                o_ap = kT[:, h, tq * 512:(tq + 1) * 512]
                if ev["i"] % 2 == 0:
                    S.op("act", lambda e: e.copy(out=o_ap, in_=bank(b)), reads=[psr(b)], writes=[("kT", h, tq)])
                else:
                    S.op("dve", lambda e: e.tensor_copy(out=o_ap, in_=bank(b)), reads=[psr(b)], writes=[("kT", h, tq)])
            ev["i"] += 1

        def unit_v(tt):
            b = next_bank()
            for kt in range(8):
                S.op("pe", lambda e, kt=kt: e.matmul(
                    bank(b), lhsT=hT[:, kt, tt * 128:(tt + 1) * 128], rhs=win_qkv[:, kt, 1024:1536],
                    start=(kt == 0), stop=(kt == 7)),
                     reads=[("win_qkv", kt), ("hT", kt, tt)], writes=[psr(b)])
            src = bank(b).rearrange("p (h e) -> p h e", h=4)
            if tt % 2 == 0:
                S.op("act", lambda e: e.copy(out=vaug[:, tt, :, 0:128], in_=src),
                     reads=[psr(b), ("vaug", "init")], writes=[("vaug", tt)])
            else:
                S.op("dve", lambda e: e.tensor_copy(out=vaug[:, tt, :, 0:128], in_=src),
                     reads=[psr(b), ("vaug", "init")], writes=[("vaug", tt)])

        NS = 3
        xs = [A.alloc(f"xs{i}", 1024) for i in range(NS)]
        hnA = [A.alloc(f"hnA{i}", 512, BF16) for i in range(NS)]
        ready = []

        def chunk_units(tq):
            u = []
            for h in range(4):
                u.append(lambda h=h: unit_qk(h, "q", tq))
                u.append(lambda h=h: unit_qk(h, "k", tq))
            for t in range(tq * 4, tq * 4 + 4):
                u.append(lambda t=t: unit_v(t))
            return u

        for tt in range(NT + 1):
            if tt < NT:
                sl = tt % NS
                S.dma("sp", xs[sl], x_d[tt * 128:(tt + 1) * 128, :], f"xs{sl}", writes=[f"xs{sl}"])
                norm_part(xs[sl], [f"xs{sl}"], hnA[sl], f"hnA{sl}", ss1[:, tt:tt + 1], rstd1[:, tt:tt + 1], ("n1", tt))
            if tt >= 1:
                pt = tt - 1
                transpose_part(pt, hT, "hT", hnA[pt % NS], f"hnA{pt % NS}")
                if pt % 4 == 3:
                    ready.extend(chunk_units(pt // 4))
            for _ in range(3):
                if ready:
                    ready.pop(0)()
        while ready:
            ready.pop(0)()
        for i in range(NS):
            A.free(f"xs{i}")
            A.free(f"hnA{i}")
        A.free("win_qkv")

        for kt in range(8):
            S.dma("pool", win_lru[:, kt, :], w_in_d[kt * 128:(kt + 1) * 128, 1536:2560], "w_lru",
                  writes=[("win_lru", kt)])
        S.dma("pool", wg, wg_d.rearrange("n p c -> p n c"), "w_g", writes=["wg"])

        S.op("pool", lambda e: e.iota(absd, pattern=[[1, 3968]], base=-1920, channel_multiplier=-1,
                                      allow_small_or_imprecise_dtypes=True), writes=["absd"])
        S.op("act", lambda e: e.activation(out=absd, in_=absd, func=AF.Abs), reads=["absd"], writes=["absd"])
        NSL = 6
        sc = [A.alloc(f"sc{i}", 512) for i in range(NSL)]
        ET = [A.alloc(f"ET{i}", 256, BF16) for i in range(NSL)]
        o1 = A.alloc("o1", 512)
        ob = A.alloc("ob", 512)
        on = A.alloc("on", 256, BF16)

        btab = A.alloc("btab", 128).rearrange("p (v m) -> p v m", v=8)
        fgt = A.alloc("fgt", 32).rearrange("p (v q) -> p v q", v=8)
        numb = [A.alloc(f"numb{c}", 520)[:, 0:516].rearrange("p (q e) -> p q e", q=4) for c in range(2)]
        sqj = A.alloc("sqj", 128)
        klf = st[:, 120:121]
        cmf = st[:, 128:144]
        ptab = st[:, 144:148]
        S.op("pool", lambda e: e.iota(klf, pattern=[[0, 1]], base=0, channel_multiplier=1,
                                      allow_small_or_imprecise_dtypes=True), writes=[("zr", "klf")])
        S.op("pool", lambda e: e.iota(cmf, pattern=[[1, 16]], base=0, channel_multiplier=0,
                                      allow_small_or_imprecise_dtypes=True), writes=[("zr", "cmf")])
        S.op("pool", lambda e: e.iota(ptab, pattern=[[128, 4]], base=0, channel_multiplier=1,
                                      allow_small_or_imprecise_dtypes=True), writes=[("zr", "ptab")])
        for h_ in range(4):
            sl_ = 2.0 ** (-2.0 * (h_ + 1))
            for sg in range(2):
                v_ = h_ * 2 + sg
                ksign = sl_ if sg == 0 else -sl_
                cadd = 0.0 if sg == 0 else sl_ * 511.0
                S.op("dve", lambda e, v_=v_, sl_=sl_, cadd=cadd: e.tensor_scalar(
                    out=btab[:, v_, :], in0=cmf, scalar1=-sl_ * 128.0, scalar2=cadd, op0=ALU.mult, op1=ALU.add),
                     reads=[("zr", "cmf")], writes=[("btab", v_)])
                S.op("dve", lambda e, v_=v_, ksign=ksign: e.scalar_tensor_tensor(
                    out=btab[:, v_, :], in0=klf.to_broadcast([128, 16]), scalar=ksign, in1=btab[:, v_, :],
                    op0=ALU.mult, op1=ALU.add), reads=[("zr", "klf"), ("btab", v_)], writes=[("btab", v_)])
            S.op("act", lambda e, h_=h_, sl_=sl_: e.activation(out=fgt[:, 2 * h_, :], in_=ptab, func=AF.Exp, scale=-sl_),
                 reads=[("zr", "ptab")], writes=[("fgt", 2 * h_)])
            S.op("act", lambda e, h_=h_, sl_=sl_: e.activation(out=fgt[:, 2 * h_ + 1, :], in_=ptab, func=AF.Exp,
                                                               scale=sl_, bias=-sl_ * 511.0),
                 reads=[("zr", "ptab")], writes=[("fgt", 2 * h_ + 1)])

        iters = []
        groups_at = []
        for h in range(4):
            slope_h = 2.0 ** (-2.0 * (h + 1))
            dmax = 40.0 / slope_h
            for qc in range(4):
                cls_kbs = {"B": [], "D": [], "A": []}
                for kb in range(16):
                    q0, q1, k0, k1 = qc * 512, qc * 512 + 511, kb * 128, kb * 128 + 127
                    mind = max(0, k0 - q1, q0 - k1)
                    if mind <= dmax:
                        cl_ = "D" if h < 2 else ("B" if k1 < q0 else ("A" if k0 > q1 else "D"))
                        cls_kbs[cl_].append(kb)
                order = [cl for cl in ("B", "D", "A") if cls_kbs[cl]]
                for c in range(2):
                    for ci_, cl in enumerate(order):
                        g = len(groups_at)
                        groups_at.append((h, qc, c, cl, ci_ == 0, ci_ == len(order) - 1))
                        kbs = cls_kbs[cl]
                        for n, kb in enumerate(kbs):
                            iters.append((h, qc, c, kb, n == 0, n == len(kbs) - 1, g))
        NI = len(iters)
        SCB = (0, 1, 2, 7)
        ACC = ((3, 4), (5, 6))
        bank_of = {}
        free_sb = list(SCB)

        def take_score_bank():
            return free_sb.pop(0)

        def release_score_bank(b):
            free_sb.append(b)

        def acc_ap(grp, qi):
            bk = ACC[grp % 2][qi // 2]
            return bank(bk)[:, (qi % 2) * 129:(qi % 2) * 129 + 129], bk

        def emit_qk(i):
            h, qc, c, kb, first, last, grp = iters[i]
            sb_ = take_score_bank()
            bank_of[i] = sb_
            S.op("pe", lambda e: e.matmul(
                bank(sb_), lhsT=kT[:, h, kb * 128:(kb + 1) * 128],
                rhs=qz[c][:, h, qc * 512:(qc + 1) * 512], start=True, stop=True),
                 reads=[("kT", h, kb // 4), (f"qz{c}", h, qc), (f"qz{c}", "z")], writes=[psr(sb_)])

        def emit_bias(i):
            h, qc, c, kb, first, last, grp = iters[i]
            if groups_at[grp][3] != "D":
                return
            sb_ = bank_of[i]
            s = i % NSL
            slope = 2.0 ** (-2.0 * (h + 1))
            Dv = qc * 512 - kb * 128 + 1920
            S.op("dve", lambda e: e.scalar_tensor_tensor(out=sc[s], in0=absd[:, Dv:Dv + 512], scalar=-slope,
                                                         in1=bank(sb_), op0=ALU.mult, op1=ALU.add),
                 reads=["absd", psr(sb_)], writes=[f"sc{s}"])
            release_score_bank(sb_)

        def emit_softmax(i):
            h, qc, c, kb, first, last, grp = iters[i]
            cl = groups_at[grp][3]
            sb_ = bank_of[i]
            s = i % NSL
            if cl != "D":
                sg = 0 if cl == "B" else 1
                m_ = (4 * qc - kb) if cl == "B" else (kb - 4 * qc)
                v_ = h * 2 + sg
                S.op("act", lambda e: e.activation(out=ET[s], in_=bank(sb_), func=AF.Exp,
                                                   bias=btab[:, v_, m_:m_ + 1]),
                     reads=[psr(sb_), ("btab", v_)], writes=[f"ET{s}"])
                release_score_bank(sb_)
                return
            S.op("act", lambda e: e.activation(out=ET[s], in_=sc[s], func=AF.Exp),
                 reads=[f"sc{s}"], writes=[f"ET{s}"])

        def emit_pv(i):
            h, qc, c, kb, first, last, grp = iters[i]
            s = i % NSL
            for qi in range(4):
                dst, bk = acc_ap(grp, qi)
                S.op("pe", lambda e, dst=dst, qi=qi: e.matmul(
                    dst, lhsT=ET[s][:, qi * 128:(qi + 1) * 128], rhs=vaug[:, kb, h, 0:129],
                    start=(first and qi % 2 == 0), stop=(last and qi % 2 == 1)),
                     reads=[f"ET{s}", ("vaug", kb)], writes=[psr(bk)])
                if qi == 0:
                    S.op("pe", lambda e, bk=bk: e.matmul(
                        bank(bk)[:, 258:512], lhsT=zeros_bf, rhs=kT[:, 0, 0:254], start=False, stop=False),
                         reads=["zeros_bf", ("kT", 0, 0)], writes=[psr(bk)])

        def finalize_stages(grp):
            h, qc, c, cl, first_cls, last_cls = groups_at[grp]
            nb = numb[c]
            nn = f"numb{c}"

            def s_combine():
                if cl == "D":
                    for j in range(2):
                        bk = ACC[grp % 2][j]
                        src = bank(bk)[:, 0:258]
                        dst = nb[:, 2 * j:2 * j + 2, :].rearrange("p q e -> p (q e)")
                        if first_cls:
                            S.op("dve", lambda e, src=src, dst=dst: e.tensor_copy(out=dst, in_=src),
                                 reads=[psr(bk)], writes=[(nn, 2 * j), (nn, 2 * j + 1)])
                        else:
                            S.op("dve", lambda e, src=src, dst=dst: e.tensor_tensor(out=dst, in0=src, in1=dst, op=ALU.add),
                                 reads=[psr(bk), (nn, 2 * j), (nn, 2 * j + 1)], writes=[(nn, 2 * j), (nn, 2 * j + 1)])
                    return
                fcol = fgt[:, 2 * h + (0 if cl == "B" else 1), :]
                for qi in range(4):
                    a, bk = acc_ap(grp, qi)
                    if first_cls:
                        S.op("dve", lambda e, a=a, qi=qi: e.tensor_scalar(
                            out=nb[:, qi, :], in0=a, scalar1=fcol[:, qi:qi + 1], scalar2=None, op0=ALU.mult),
                             reads=[psr(bk), ("fgt", 2 * h), ("fgt", 2 * h + 1)], writes=[(nn, qi)])
                    else:
                        S.op("dve", lambda e, a=a, qi=qi: e.scalar_tensor_tensor(
                            out=nb[:, qi, :], in0=a, scalar=fcol[:, qi:qi + 1], in1=nb[:, qi, :],
                            op0=ALU.mult, op1=ALU.add),
                             reads=[psr(bk), ("fgt", 2 * h), ("fgt", 2 * h + 1), (nn, qi)], writes=[(nn, qi)])

            if not last_cls:
                return [(0, s_combine)]

            single = first_cls and last_cls
            on_act = h < 2

            def srcv(qi):
                if single:
                    a, bk = acc_ap(grp, qi)
                    return a, psr(bk)
                return nb[:, qi, :], (nn, qi)

            def s_recip():
                if single:
                    b0_ = ACC[grp % 2][0]
                    src_ = P[:, b0_ * 512:(b0_ + 2) * 512].rearrange("p (b c) -> p b c", b=2)[:, :, 0:258]
                    src_ = src_.rearrange("p b (q e) -> p b q e", q=2)[:, :, :, 128]
                    dst_ = zr[:, 0:4].rearrange("p (b q) -> p b q", b=2)
                    rr_ = [psr(b0_), psr(b0_ + 1)]
                else:
                    src_ = nb[:, :, 128]
                    dst_ = zr[:, 0:4]
                    rr_ = [(nn, q) for q in range(4)]
                S.op("dve", lambda e: e.reciprocal(out=dst_, in_=src_), reads=rr_,
                     writes=[("zr", q) for q in range(4)])

            if c == 0:
                def s_o1():
                    for qi in range(4):
                        a, r_ = srcv(qi)
                        if on_act:
                            S.op("act", lambda e, qi=qi, a=a: e.activation(
                                out=o1[:, qi * 128:(qi + 1) * 128], in_=a[:, 0:128], func=AF.Identity,
                                scale=zr[:, qi:qi + 1]), reads=[r_, ("zr", qi)], writes=[("o1", qi)])
                        else:
                            S.op("dve", lambda e, qi=qi, a=a: e.tensor_scalar(
                                out=o1[:, qi * 128:(qi + 1) * 128], in0=a[:, 0:128], scalar1=zr[:, qi:qi + 1],
                                scalar2=None, op0=ALU.mult),
                                 reads=[r_, ("zr", qi)], writes=[("o1", qi)])
                st_ = [] if single else [(0, s_combine)]
                return st_ + [(1, s_recip), (3 if on_act else 1, s_o1)]

            def s_ob():
                s_recip()
                S.op("dve", lambda e: e.tensor_scalar(out=zr[:, 4:8], in0=zr[:, 0:4], scalar1=neglam, scalar2=None,
                                                      op0=ALU.mult),
                     reads=[("zr", q) for q in range(4)] + [("lam", 4)], writes=[("zr", 4)])
                for qi in range(4):
                    a, r_ = srcv(qi)
                    S.op("dve", lambda e, qi=qi, a=a: e.scalar_tensor_tensor(
                        out=ob[:, qi * 128:(qi + 1) * 128], in0=a[:, 0:128], scalar=zr[:, 4 + qi:5 + qi],
                        in1=o1[:, qi * 128:(qi + 1) * 128], op0=ALU.mult, op1=ALU.add),
                         reads=[r_, ("zr", 4), ("o1", qi)], writes=[("ob", qi)])

            def s_sq():
                for qi in range(4):
                    if on_act:
                        S.op("act", lambda e, qi=qi: e.activation(
                            out=junk[:, 0:128], in_=ob[:, qi * 128:(qi + 1) * 128], func=AF.Square,
                            accum_out=zr[:, 8 + qi:9 + qi]),
                             reads=[("ob", qi)], writes=["junk", ("zr", 8 + qi)])
                        continue
                    S.op("dve", lambda e, qi=qi: e.scalar_tensor_tensor(
                        out=sqj[:, 0:128], in0=ob[:, qi * 128:(qi + 1) * 128], scalar=1.0,
                        in1=ob[:, qi * 128:(qi + 1) * 128], op0=ALU.mult, op1=ALU.mult,
                        accum_out=zr[:, 8 + qi:9 + qi]),
                         reads=[("ob", qi)], writes=["sqj", ("zr", 8 + qi)])

            def s_rstd():
                S.op("act", lambda e: e.activation(out=zr[:, 8:12], in_=zr[:, 8:12], func=AF.Ln,
                                                   scale=1.0 / 128, bias=EPS),
                     reads=[("zr", 8 + q) for q in range(4)], writes=[("zr", 8 + q) for q in range(4)])
                S.op("act", lambda e: e.activation(out=zr[:, 12:16], in_=zr[:, 8:12], func=AF.Exp, scale=-0.5),
                     reads=[("zr", 8 + q) for q in range(4)], writes=[("zr", 13)])

            def s_on():
                for qi in range(4):
                    S.op("dve", lambda e, qi=qi: e.scalar_tensor_tensor(
                        out=on[:, qi * 128:(qi + 1) * 128], in0=ob[:, qi * 128:(qi + 1) * 128],
                        scalar=zr[:, 12 + qi:13 + qi], in1=gsub, op0=ALU.mult, op1=ALU.mult),
                         reads=[("ob", qi), ("zr", 13), "gsub"], writes=[("on", qi)])

            trb = {}

            def s_tr():
                trb["b"] = take_score_bank()
                pv = bankbf(trb["b"])
                for qi in range(4):
                    S.op("pe", lambda e, qi=qi: e.transpose(out=pv[:, qi * 128:(qi + 1) * 128],
                                                            in_=on[:, qi * 128:(qi + 1) * 128], identity=ident),
                         reads=[("on", qi), "ident"], writes=[psr(trb["b"])])

            def s_copy():
                pv = bankbf(trb["b"])
                if on_act:
                    S.op("act", lambda e: e.copy(out=mixA[:, h, qc * 512:(qc + 1) * 512], in_=pv[:, 0:512]),
                         reads=[psr(trb["b"])], writes=[("mixA", h, qc)])
                else:
                    S.op("dve", lambda e: e.tensor_copy(out=mixA[:, h, qc * 512:(qc + 1) * 512], in_=pv[:, 0:512]),
                         reads=[psr(trb["b"])], writes=[("mixA", h, qc)])
                release_score_bank(trb["b"])

            st_ = [] if single else [(0, s_combine)]
            return st_ + [(1, s_ob), (4 if on_act else 2, s_sq), (6, s_rstd), (9, s_on), (11, s_tr), (13, s_copy)]

        LAG = 2
        pending = {}
        LOOK = 4
        nq = {"n": 0}

        def fill_qk(cur):
            while free_sb and nq["n"] < NI and nq["n"] <= cur + LOOK:
                emit_qk(nq["n"])
                emit_bias(nq["n"])
                nq["n"] += 1

        fill_qk(0)
        for i in range(NI):
            emit_softmax(i)
            for fn in pending.pop(i, []):
                fn()
            fill_qk(i)
            emit_pv(i)
            if iters[i][5]:
                for off, fn in finalize_stages(iters[i][6]):
                    pending.setdefault(i + 1 + LAG + off, []).append(fn)
        for k in sorted(pending):
            for fn in pending[k]:
                fn()

        for nm in ("btab", "fgt", "numb0", "numb1", "sqj", "qz0", "qz1", "kT", "vaug", "absd", "sc0", "sc1", "sc2", "sc3", "sc4", "sc5", "ET0", "ET1", "ET2", "ET3", "ET4", "ET5", "o1", "ob", "on"):
            A.free(nm)

        wout = A.alloc("wout", 4096, BF16).rearrange("p (k c) -> p k c", k=8)
        mixL = A.alloc("mixL", 4096, BF16).rearrange("p (k t) -> p k t", k=4)
        for kt in range(8):
            S.dma("pool", wout[:, kt, :], w_out_d[kt * 128:(kt + 1) * 128, :], "w_out", writes=[("wout", kt)])

        xrp = A.alloc("xrp", 2056)
        gg = A.alloc("gg", 2048)
        xcs = [A.alloc("xc0", 2048, top=True), A.alloc("xc1", 2048)]
        xcb = A.alloc("xcb", 1024, BF16)
        TA = [A.alloc(f"TA{i}", 2048) for i in range(2)]
        TB = [A.alloc(f"TB{i}", 2048) for i in range(2)]
        TC0 = A.alloc("TC0", 2048)
        TD = [A.alloc(f"TD{i}", 2048) for i in range(2)]
        S.op("pool", lambda e: e.memset(xrp[:, 0:2], 0.0), writes=[("xrp", "pad0")])
        S.op("pool", lambda e: e.memset(xrp[:, 2050:2056], 0.0), writes=[("xrp", "pad1")])

        def lru_proj(col0, evac):
            for tq in range(4):
                b = next_bank()
                for kt in range(8):
                    S.op("pe", lambda e, b=b, kt=kt, tq=tq: e.matmul(
                        bank(b), lhsT=win_lru[:, kt, col0:col0 + 128], rhs=hT[:, kt, tq * 512:(tq + 1) * 512],
                        start=(kt == 0), stop=(kt == 7)),
                         reads=[("win_lru", kt)] + [("hT", kt, t) for t in range(tq * 4, tq * 4 + 4)],
                         writes=[psr(b)])
                evac(tq, b)

        def lru_A(ct):
            xc = xcs[ct % 2]
            xn = f"xc{ct % 2}"
            lru_proj(ct * 128, lambda tq, b: S.op(
                "act", lambda e: e.copy(out=xrp[:, 2 + tq * 512:2 + (tq + 1) * 512], in_=bank(b)),
                reads=[psr(b)], writes=[("xrp", tq)]))
            xr_all = [("xrp", t) for t in range(4)] + [("xrp", "pad0"), ("xrp", "pad1")]
            S.op("dve", lambda e: e.tensor_scalar(out=xc, in0=xrp[:, 0:2048], scalar1=cols[:, ct * 4:ct * 4 + 1],
                                                  scalar2=cols[:, 16 + ct:17 + ct], op0=ALU.mult, op1=ALU.add),
                 reads=xr_all + ["cols"], writes=[xn])
            for k in range(1, 4):
                S.op("dve", lambda e, k=k: e.scalar_tensor_tensor(
                    out=xc, in0=xrp[:, k:k + 2048], scalar=cols[:, ct * 4 + k:ct * 4 + k + 1], in1=xc,
                    op0=ALU.mult, op1=ALU.add),
                     reads=xr_all + ["cols", xn], writes=[xn])
            S.op("dve", lambda e: e.tensor_copy(out=xcb, in_=xc), reads=[xn], writes=["xcb"])

        def lru_G(ct):
            for d in range(2):
                ci = d * 4 + ct
                for gate, (dstt, dn, bcol) in enumerate(((TA[d], f"TA{d}", 20 + ci), (TB[d], f"TB{d}", 28 + ci))):
                    widx = (gate * 2 + d) * 4 + ct
                    for tq in range(4):
                        b = next_bank()
                        S.op("pe", lambda e, b=b, widx=widx, tq=tq: e.matmul(
                            bank(b), lhsT=wg[:, widx, :], rhs=xcb[:, tq * 512:(tq + 1) * 512], start=True, stop=True),
                             reads=["wg", "xcb"], writes=[psr(b)])
                        S.op("act", lambda e, b=b, dstt=dstt, tq=tq, bcol=bcol: e.activation(
                            out=dstt[:, tq * 512:(tq + 1) * 512], in_=bank(b), func=AF.Sigmoid,
                            bias=cols[:, bcol:bcol + 1]),
                             reads=[psr(b), "cols"], writes=[(dn, tq)])

        def lru_E(ct):
            xc = xcs[ct % 2]
            xn = f"xc{ct % 2}"
            xr_all = [("xrp", t) for t in range(4)]
            abuf = [TC0, xrp[:, 2:2050]]
            ares = [["TC0"], xr_all]
            R = [[(f"TA{d}", t) for t in range(4)] for d in range(2)]
            I = [[(f"TB{d}", t) for t in range(4)] for d in range(2)]
            for d in range(2):
                ci = d * 4 + ct
                S.op("act", lambda e, d=d, ci=ci: e.activation(out=abuf[d], in_=TA[d], func=AF.Exp,
                                                               scale=cA[:, ci:ci + 1]),
                     reads=R[d] + ["cA"], writes=ares[d])
            for d in range(2):
                ci = d * 4 + ct
                if d == 0:
                    S.op("act", lambda e, d=d, ci=ci: e.activation(out=TA[d], in_=TA[d], func=AF.Exp,
                                                                   scale=c2A[:, ci:ci + 1]),
                         reads=R[d] + ["c2A"], writes=R[d])
                else:
                    S.op("dve", lambda e, d=d: e.tensor_tensor(out=TA[d], in0=abuf[d], in1=abuf[d], op=ALU.mult),
                         reads=ares[d], writes=R[d])
                S.op("dve", lambda e, d=d: e.tensor_tensor(out=TB[d], in0=TB[d], in1=xc, op=ALU.mult),
                     reads=I[d] + [xn], writes=I[d])
            for d in range(2):
                S.op("act", lambda e, d=d: e.activation(out=TA[d], in_=TA[d], func=AF.Sqrt, scale=-1.0, bias=1.0),
                     reads=R[d], writes=R[d])
            for d in range(2):
                S.op("dve", lambda e, d=d: e.tensor_tensor(out=TB[d], in0=TB[d], in1=TA[d], op=ALU.mult),
                     reads=I[d] + R[d], writes=I[d])
                if d == 0:
                    S.op("dve", lambda e: e.tensor_tensor_scan(
                        out=TD[0], data0=abuf[0], data1=TB[0], initial=0.0, op0=ALU.mult, op1=ALU.add),
                         reads=ares[0] + I[0], writes=["TD0"])
                else:
                    S.op("dve", lambda e: e.tensor_tensor_scan(
                        out=TD[1][:, ::-1], data0=abuf[1][:, ::-1], data1=TB[1][:, ::-1], initial=0.0,
                        op0=ALU.mult, op1=ALU.add),
                         reads=ares[1] + I[1], writes=["TD1"])

        def lru_C(ct):
            lru_proj(512 + ct * 128, lambda tq, b: S.op(
                "act", lambda e: e.activation(out=gg[:, tq * 512:(tq + 1) * 512], in_=bank(b), func=AF.Gelu_apprx_tanh),
                reads=[psr(b)], writes=[("gg", tq)]))

        def lru_D(ct):
            S.op("dve", lambda e: e.tensor_tensor(out=TD[0], in0=TD[0], in1=TD[1], op=ALU.add),
                 reads=["TD0", "TD1"], writes=["TD0"])
            S.op("dve", lambda e: e.tensor_tensor(out=mixL[:, ct, :], in0=TD[0], in1=gg, op=ALU.mult),
                 reads=["TD0"] + [("gg", t) for t in range(4)], writes=[("mixL", ct, q) for q in range(4)])

        lru_A(0)
        for ct in range(4):
            lru_G(ct)
            if ct + 1 < 4:
                lru_A(ct + 1)
            lru_E(ct)
            lru_C(ct)
            lru_D(ct)
        for nm in ("xrp", "gg", "xc0", "xc1", "xcb", "TA0", "TA1", "TB0", "TB1", "TC0", "TD0", "TD1", "win_lru", "wg"):
            A.free(nm)

        x1t = [None] * NT
        for tt_ in range(NT - 1, -1, -1):
            x1t[tt_] = A.alloc(f"x1t{tt_}", 1024, top=True)
        groups = [(0, 6), (6, 12), (12, 16), (16, 22)]
        wup = [A.alloc(f"wup{i}", 2048, BF16, top=True) for i in range(2)]
        wup3 = [w.rearrange("p (k c) -> p k c", k=8) for w in wup]
        wdn = [A.alloc(f"wdn{i}", 3072, BF16, top=True).rearrange("p (j c) -> p j c", j=6) for i in range(2)]
        w_up_v = w_up_d.rearrange("(k p) c -> p k c", p=128)
        w_dn_v = w_dn_d.rearrange("(j p) c -> p j c", p=128)

        def load_wup(jp):
            sl = jp % 2
            c0 = jp * 256
            for kt in range(8):
                S.dma("pool", wup3[sl][:, kt, 0:256], w_up_d[kt * 128:(kt + 1) * 128, c0:c0 + 256], f"wup{sl}",
                      writes=[(f"wup{sl}", kt, 0)])
                S.dma("pool", wup3[sl][:, kt, 256:512], w_up_d[kt * 128:(kt + 1) * 128, DFF + c0:DFF + c0 + 256],
                      f"wup{sl}", writes=[(f"wup{sl}", kt, 1)])

        def load_wdn(g):
            j0, j1 = groups[g]
            sl = g % 2
            for jj in range(j1 - j0):
                S.dma("pool", wdn[sl][:, jj, :], w_dn_d[(j0 + jj) * 128:(j0 + jj + 1) * 128, :], f"wdn{sl}",
                      writes=[(f"wdn{sl}", jj)])

        load_wup(0)
        load_wup(1)
        load_wdn(0)
        c_order = list(range(NT - 1, -1, -1))
        S.dma("sp", gb, gvec_d[1:2, :].to_broadcast([128, D]), "c_gb", writes=["gb"])
        for tt in c_order:
            S.dma("sp", x1t[tt], x_d[tt * 128:(tt + 1) * 128, :], f"x1_{tt}", writes=[f"x1t{tt}"])
        hnC = [A.alloc(f"hnC{i}", 512, BF16) for i in range(3)]
        def c_mm(tt, cc, b, kts):
            for kt in kts:
                mx, mname = (mixA, "mixA") if kt < 4 else (mixL, "mixL")
                S.op("pe", lambda e, kt=kt, mx=mx: e.matmul(
                    bank(b), lhsT=mx[:, kt % 4, tt * 128:(tt + 1) * 128], rhs=wout[:, kt, cc * 512:(cc + 1) * 512],
                    start=(kt == 0), stop=(kt == 7)),
                     reads=[(mname, kt % 4, tt // 4), ("wout", kt)], writes=[psr(b)])

        def c_add(tt, cc, b):
            S.op("dve", lambda e: e.tensor_tensor(
                out=x1t[tt][:, cc * 512:(cc + 1) * 512], in0=bank(b), in1=x1t[tt][:, cc * 512:(cc + 1) * 512],
                op=ALU.add), reads=[psr(b), f"x1t{tt}"], writes=[f"x1t{tt}"])

        head = c_order[:4]
        hb_ = {}
        for tt in head:
            for cc in range(2):
                hb_[(tt, cc)] = next_bank()
                c_mm(tt, cc, hb_[(tt, cc)], range(7))
        def c_norm_dve(n_):
            tt = c_order[n_]
            sl = n_ % 3
            norm_dve(x1t[tt], [f"x1t{tt}"], hnC[sl], f"hnC{sl}", ss1[:, tt:tt + 1], rstd1[:, tt:tt + 1], ("n2", tt))

        def c_tr(n_):
            tt = c_order[n_]
            sl = n_ % 3
            transpose_part(tt, hT, "hT", hnC[sl], f"hnC{sl}")

        for n_, tt in enumerate(c_order):
            for cc in range(2):
                if tt in head:
                    b = hb_[(tt, cc)]
                    c_mm(tt, cc, b, [7])
                else:
                    b = next_bank()
                    c_mm(tt, cc, b, range(8))
                c_add(tt, cc, b)
            rms_act(x1t[tt], [f"x1t{tt}"], ss1[:, tt:tt + 1], ("n2", tt), 1.0 / D)
            if n_ >= 1:
                c_norm_dve(n_ - 1)
            if n_ >= 2:
                c_tr(n_ - 2)
        c_norm_dve(NT - 1)
        c_tr(NT - 2)
        c_tr(NT - 1)
        if "mixT" in debug_taps:
            taps["mixT"] = nc.dram_tensor("tap_mixT", [128, 8 * S_TOK], F32, kind="ExternalOutput").ap()
            for kt in range(8):
                mx, mname = (mixA, "mixA") if kt < 4 else (mixL, "mixL")
                S.dma("pool", taps["mixT"][:, kt * S_TOK:(kt + 1) * S_TOK], mx[:, kt % 4, :], "out",
                      reads=[(mname, kt % 4, q) for q in range(4)])
        A.free("mixA")
        A.free("mixL")
        for i in range(3):
            A.free(f"hnC{i}")
        A.free("wout")
        if "x1" in debug_taps:
            taps["x1"] = nc.dram_tensor("tap_x1", [128, NT * D], F32, kind="ExternalOutput").ap()
            for t in range(NT):
                S.dma("sp", taps["x1"][:, t * D:(t + 1) * D], x1t[t], "out", reads=[f"x1t{t}"])

        S.dma("sp", gb, gvec_d[2:3, :].to_broadcast([128, D]), "c_gb", writes=["gb"])
        actT = A.alloc("actT", 7168, BF16).rearrange("p (j t) -> p j t", j=7)
        cgb = [A.alloc(f"cg{i}", 2048) for i in range(2)]
        cvb = [A.alloc(f"cv{i}", 2048) for i in range(2)]
        outb = None
        U_g = P[:, 0:2048]
        U_v = P[:, 2048:4096]
        NSLOT = 7

        def up_tile(j):
            jp = j // 2
            sl = jp % 2
            cj = (j % 2) * 128
            bs = j % 2
            cg, cv = cgb[bs], cvb[bs]
            slot = j % NSLOT
            for half, (U, boff, wc0, cbuf, cname, jt) in enumerate((
                    (U_g, 0, cj, cg, f"cg{bs}", j), (U_v, 4, 256 + cj, cv, f"cv{bs}", NJ + j))):
                for tq in range(4):
                    b = boff + tq
                    for kt in range(8):
                        S.op("pe", lambda e, b=b, kt=kt, tq=tq, wc0=wc0: e.matmul(
                            bank(b), lhsT=wup3[sl][:, kt, wc0:wc0 + 128], rhs=hT[:, kt, tq * 512:(tq + 1) * 512],
                            start=(kt == 0), stop=(kt == 7)),
                             reads=[(f"wup{sl}", kt, half)] + [("hT", kt, t) for t in range(tq * 4, tq * 4 + 4)],
                             writes=[psr(b)])
                rb = [psr(boff + t) for t in range(4)]
                w0 = cols[:, 44 + jt * 3:45 + jt * 3]
                w1 = cols[:, 45 + jt * 3:46 + jt * 3]
                w2 = cols[:, 46 + jt * 3:47 + jt * 3]
                bb = cols[:, 176 + jt:177 + jt]
                S.op("act", lambda e, U=U, cbuf=cbuf, w1=w1, bb=bb: e.activation(
                    out=cbuf, in_=U, func=AF.Identity, scale=w1, bias=bb),
                     reads=rb + ["cols"], writes=[cname])
                if half == 1:
                    S.op("act", lambda e: e.activation(out=cg, in_=cg, func=AF.Gelu_apprx_tanh),
                         reads=[f"cg{bs}"], writes=[f"cg{bs}"])
                S.op("dve", lambda e, U=U, cbuf=cbuf, w0=w0: e.scalar_tensor_tensor(
                    out=cbuf[:, 1:2048], in0=U[:, 0:2047], scalar=w0, in1=cbuf[:, 1:2048],
                    op0=ALU.mult, op1=ALU.add), reads=rb + ["cols", cname], writes=[cname])
                S.op("dve", lambda e, U=U, cbuf=cbuf, w2=w2: e.scalar_tensor_tensor(
                    out=cbuf[:, 0:2047], in0=U[:, 1:2048], scalar=w2, in1=cbuf[:, 0:2047],
                    op0=ALU.mult, op1=ALU.add), reads=rb + ["cols", cname], writes=[cname])
            if j % 2 == 1 and jp + 2 < 11:
                load_wup(jp + 2)
            S.op("dve", lambda e: e.tensor_tensor(out=actT[:, slot, :], in0=cg, in1=cv, op=ALU.mult),
                 reads=[f"cg{bs}", f"cv{bs}"], writes=[("actT", slot)])

        def fin_out(tt):
            osl = tt % 4
            ob_ = outb_box[0]
            rms_dve(ss3[:, tt:tt + 1], rstd3[:, tt:tt + 1], ("n3", tt))
            S.op("dve", lambda e: e.scalar_tensor_tensor(
                out=ob_[osl], in0=x1t[tt], scalar=rstd3[:, tt:tt + 1], in1=gb, op0=ALU.mult, op1=ALU.mult),
                 reads=[f"x1t{tt}", (("n3", tt), "rstd"), "gb"], writes=[f"outb{osl}"])
            S.dma("sp", out_d[tt * 128:(tt + 1) * 128, :], ob_[osl], f"out{osl}", reads=[f"outb{osl}"])

        def down_partial(g):
            nonlocal_outb = outb_box
            j0, j1 = groups[g]
            last = (g == len(groups) - 1)
            nj = j1 - j0
            sl = g % 2
            for tt in range(NT):
                for cc in range(2):
                    b = next_bank()
                    for jj in range(nj):
                        slot = (j0 + jj) % NSLOT
                        S.op("pe", lambda e, b=b, jj=jj, tt=tt, cc=cc, slot=slot: e.matmul(
                            bank(b), lhsT=actT[:, slot, tt * 128:(tt + 1) * 128],
                            rhs=wdn[sl][:, jj, cc * 512:(cc + 1) * 512],
                            start=(jj == 0), stop=(jj == nj - 1)),
                             reads=[("actT", slot), (f"wdn{sl}", jj)], writes=[psr(b)])
                    S.op("dve", lambda e, b=b, tt=tt, cc=cc: e.tensor_tensor(
                        out=x1t[tt][:, cc * 512:(cc + 1) * 512], in0=bank(b), in1=x1t[tt][:, cc * 512:(cc + 1) * 512],
                        op=ALU.add), reads=[psr(b), f"x1t{tt}"], writes=[f"x1t{tt}"])
                if last:
                    rms_act(x1t[tt], [f"x1t{tt}"], ss3[:, tt:tt + 1], ("n3", tt), 1.0 / D)
                    if tt >= 1:
                        fin_out(tt - 1)
            if last:
                fin_out(NT - 1)

        outb_box = [None]
        NG = len(groups)
        for j in range(groups[0][0], groups[0][1]):
            up_tile(j)
        for g in range(NG):
            if g + 1 < NG:
                load_wdn(g + 1)
                up_tile(groups[g + 1][0])
            else:
                for nm in ("cg0", "cg1", "cv0", "cv1"):
                    A.free(nm)
                outb_box[0] = [A.alloc(f"outb{i}", 1024) for i in range(4)]
            down_partial(g)
            if g + 1 < NG:
                for j in range(groups[g + 1][0] + 1, groups[g + 1][1]):
                    up_tile(j)
        S.emit(final_waits=[k for k in ("out", "out0", "out1", "out2", "out3") if k in S.dma_sems])
    return nc, S


_CACHE = {}


def _pack_inputs(inp):
    f = lambda a: np.ascontiguousarray(np.asarray(a, dtype=np.float32))
    cols = np.zeros((128, 220), np.float32)
    cw = f(inp["lru_conv_w"])[0]
    cb = f(inp["lru_conv_b"])[0]
    for ct in range(4):
        for k in range(4):
            cols[:, ct * 4 + k] = cw[k, ct * 128:(ct + 1) * 128]
        cols[:, 16 + ct] = cb[ct * 128:(ct + 1) * 128]
    ba = f(inp["lru_b_a"])[0]
    bx = f(inp["lru_b_x"])[0]
    lm = f(inp["lru_lambda"])[0]
    for d in range(2):
        for ct in range(4):
            cols[:, 20 + d * 4 + ct] = ba[d, ct * 128:(ct + 1) * 128]
            cols[:, 28 + d * 4 + ct] = bx[d, ct * 128:(ct + 1) * 128]
            cols[:, 36 + d * 4 + ct] = lm[d, ct * 128:(ct + 1) * 128]
    fw_ = f(inp["ffn_conv_w"])[0]
    fb_ = f(inp["ffn_conv_b"])[0]
    for jt in range(44):
        for k in range(3):
            cols[:, 44 + jt * 3 + k] = fw_[k, jt * 128:(jt + 1) * 128]
        cols[:, 176 + jt] = fb_[jt * 128:(jt + 1) * 128]
    rows = np.concatenate([f(inp["lambda_q1"])[0], f(inp["lambda_k1"])[0], f(inp["lambda_q2"])[0],
                           f(inp["lambda_k2"])[0], f(inp["subln_g"])[0]])[None, :]
    gvec = np.stack([f(inp["attn_norm_g"])[0], f(inp["ffn_norm_g"])[0], f(inp["final_norm_g"])], 0)
    wa = f(inp["lru_w_a"])[0]
    wx = f(inp["lru_w_x"])[0]
    wg = np.zeros((16, 128, 128), np.float32)
    for gate, w in enumerate((wa, wx)):
        for d in range(2):
            for ct in range(4):
                idx = (gate * 2 + d) * 4 + ct
                wg[idx, 0:64, 0:64] = w[d, 2 * ct]
                wg[idx, 64:128, 64:128] = w[d, 2 * ct + 1]
    shared = {
        "w_in": f(inp["w_in"])[0], "w_out": f(inp["w_out"])[0], "w_up": f(inp["w_up"])[0],
        "w_down": f(inp["w_down"])[0], "wg": wg, "cols": cols, "rows": np.ascontiguousarray(rows),
        "gvec": np.ascontiguousarray(gvec),
    }
    return shared


def kernel(**inputs):
    x = np.asarray(inputs["x"], dtype=np.float32)
    B = x.shape[0]
    shared = _pack_inputs(inputs)
    taps = tuple(t for t in os.environ.get("KTAPS", "").split(",") if t)
    key = ("nc", taps)
    if key not in _CACHE:
        _CACHE[key] = build(debug_taps=taps)
    nc, _ = _CACHE[key]
    in_maps = []
    for b in range(B):
        m = dict(shared)
        m["x"] = np.ascontiguousarray(x[b])
        in_maps.append(m)
    res = run_bass_kernel_spmd(nc, in_maps, core_ids=list(range(B)))
    out = np.stack([np.asarray(r["out"], dtype=np.float32) for r in res.results], 0)
    if taps:
        kernel.last_taps = [{k: v for k, v in r.items() if k.startswith("tap_")} for r in res.results]
    return out
```
